# Optimizing a Trainium2 kernel written in Bass

```python
import jax, jax.numpy as jnp
from jax import lax
import numpy as np

D_MODEL = 2048
BATCH = 4
SEQ = 4096
DEPTH = 1

GRID_W = 64
MEM_LEN = 256

ATT_HEAD_DIM = 128
ATT_WIDTH = D_MODEL // 2
ATT_HEADS = ATT_WIDTH // ATT_HEAD_DIM
ATT_KV_HEADS = 2
Q_BLOCK = 128
ROPE_THETA = 10000.0

ML_WIDTH = D_MODEL - ATT_WIDTH
ML_HEADS = 4
ML_V_DIM = ML_WIDTH // ML_HEADS
ML_QK_DIM = ML_V_DIM // 2
ML_CHUNK = 64

MIX_WIDTH = ATT_WIDTH + ML_WIDTH
IN_SPLITS = (ATT_WIDTH, ATT_KV_HEADS * ATT_HEAD_DIM, ATT_KV_HEADS * ATT_HEAD_DIM,
             ML_HEADS * ML_QK_DIM, ML_HEADS * ML_QK_DIM, ML_WIDTH, ML_WIDTH,
             2 * ML_HEADS, 2 * ML_HEADS)
IN_TOTAL = sum(IN_SPLITS)

XA_HEADS = 4
XA_HEAD_DIM = D_MODEL // XA_HEADS

PEER_HEADS = 8
PEER_NKEYS = 128
PEER_EXPERTS = PEER_NKEYS * PEER_NKEYS
PEER_TOPK = 16
PEER_QDIM = 256
PEER_BLOCK = 128

ALPHA = (2.0 * DEPTH) ** 0.25
BETA = (8.0 * DEPTH) ** -0.25

LN_EPS = 1e-5
RMS_EPS = 1e-6

kernel_name = "hybrid_attn_mlstm_peer_encoder"


def _layer_norm(x, g, b):
    xf = x.astype(jnp.float32)
    xc = xf - jnp.mean(xf, -1, keepdims=True)
    var = jnp.mean(xc * xc, -1, keepdims=True)
    y = xc * lax.rsqrt(var + LN_EPS) * g.astype(jnp.float32) + b.astype(jnp.float32)
    return y.astype(x.dtype)


def _rms_norm(x, g):
    xf = x.astype(jnp.float32)
    return xf * lax.rsqrt(jnp.mean(xf * xf, -1, keepdims=True) + RMS_EPS) * g.astype(jnp.float32)


def _axial_rope_tables(seq_len):
    rows = seq_len // GRID_W
    row = jnp.repeat(jnp.arange(rows, dtype=jnp.float32), GRID_W)
    col = jnp.tile(jnp.arange(GRID_W, dtype=jnp.float32), rows)
    n_freq = ATT_HEAD_DIM // 4
    inv_freq = ROPE_THETA ** (-jnp.arange(n_freq, dtype=jnp.float32) / n_freq)
    ang_r = row[:, None] * inv_freq
    ang_c = col[:, None] * inv_freq
    return (jnp.cos(ang_r), jnp.sin(ang_r), jnp.cos(ang_c), jnp.sin(ang_c))


def _rotate(z, cos, sin):
    z1, z2 = jnp.split(z, 2, axis=-1)
    c = cos[:, None, :]
    s = sin[:, None, :]
    return jnp.concatenate([z1 * c - z2 * s, z1 * s + z2 * c], axis=-1)


def _apply_axial_rope(z, tables):
    cos_r, sin_r, cos_c, sin_c = tables
    z_row, z_col = jnp.split(z, 2, axis=-1)
    return jnp.concatenate([_rotate(z_row, cos_r, sin_r), _rotate(z_col, cos_c, sin_c)], axis=-1)


def _blocked_gqa(q, k, v):
    B, S, H, Dh = q.shape
    G = H // ATT_KV_HEADS
    nb = S // Q_BLOCK
    qb = jnp.transpose(q.reshape(B, nb, Q_BLOCK, ATT_KV_HEADS, G, Dh), (1, 0, 2, 3, 4, 5))
    scale = Dh ** -0.5

    def block(q_blk):
        s = jnp.einsum('bqhgd,bkhd->bhgqk', q_blk, k, preferred_element_type=jnp.float32) * scale
        p = jax.nn.softmax(s, axis=-1)
        return jnp.einsum('bhgqk,bkhd->bqhgd', p.astype(v.dtype), v)

    o = lax.map(block, qb)
    return jnp.transpose(o, (1, 0, 2, 3, 4, 5)).reshape(B, S, H * Dh)


def _mlstm_chunkwise(q, k, v, log_f, log_i):
    B, H, S, dk = q.shape
    dv = v.shape[-1]
    L = ML_CHUNK
    nc = S // L

    def to_chunks(a):
        return jnp.moveaxis(a.reshape((B, H, nc, L) + a.shape[3:]), 2, 0)

    within = jnp.tril(jnp.ones((L, L), dtype=bool))

    def step(carry, inp):
        C, n, m = carry
        qc, kc, vc, fc, ic = inp
        b = jnp.cumsum(fc, axis=-1)
        log_d = b[..., :, None] - b[..., None, :] + ic[..., None, :]
        log_d = jnp.where(within, log_d, -jnp.inf)
        log_inter = b + m[..., None]
        m_row = jnp.maximum(log_inter, jnp.max(log_d, axis=-1))
        d = jnp.exp(log_d - m_row[..., None])
        w_inter = jnp.exp(log_inter - m_row)
        qk = jnp.einsum('bhld,bhsd->bhls', qc, kc) * d
        num = jnp.einsum('bhls,bhse->bhle', qk, vc) + w_inter[..., None] * jnp.einsum('bhld,bhde->bhle', qc, C)
        den = jnp.sum(qk, axis=-1) + w_inter * jnp.einsum('bhld,bhd->bhl', qc, n)
        h = num / jnp.maximum(jnp.abs(den), jnp.exp(-m_row))[..., None]
        b_last = b[..., -1]
        log_w = b_last[..., None] - b + ic
        m_new = jnp.maximum(b_last + m, jnp.max(log_w, axis=-1))
        w = jnp.exp(log_w - m_new[..., None])
        decay = jnp.exp(b_last + m - m_new)
        C = decay[..., None, None] * C + jnp.einsum('bhs,bhsd,bhse->bhde', w, kc, vc)
        n = decay[..., None] * n + jnp.einsum('bhs,bhsd->bhd', w, kc)
        return (C, n, m_new), h

    init = (jnp.zeros((B, H, dk, dv), jnp.float32), jnp.zeros((B, H, dk), jnp.float32),
            jnp.zeros((B, H), jnp.float32))
    _, h = lax.scan(step, init, (to_chunks(q), to_chunks(k), to_chunks(v), to_chunks(log_f), to_chunks(log_i)))
    return jnp.moveaxis(h, 0, 2).reshape(B, H, S, dv)


def _flip_seq(z):
    return jnp.flip(z, axis=2)


def _ml_heads(z, dim):
    B, S, _ = z.shape
    return jnp.transpose(z.reshape(B, S, ML_HEADS, dim).astype(jnp.float32), (0, 2, 1, 3))


def _hybrid_mixer(x, w_in, b_igate, b_fgate, att_q_norm, att_k_norm, ml_norm, w_out):
    B, S, _ = x.shape
    h = x @ w_in
    cuts = np.cumsum(IN_SPLITS)[:-1].tolist()
    aq, ak, av, mq, mk, mv, mo, gi, gf = jnp.split(h, cuts, axis=-1)

    tables = _axial_rope_tables(S)
    aq = _apply_axial_rope(_rms_norm(aq.reshape(B, S, ATT_HEADS, ATT_HEAD_DIM), att_q_norm), tables).astype(x.dtype)
    ak = _apply_axial_rope(_rms_norm(ak.reshape(B, S, ATT_KV_HEADS, ATT_HEAD_DIM), att_k_norm), tables).astype(x.dtype)
    av = av.reshape(B, S, ATT_KV_HEADS, ATT_HEAD_DIM)
    att = _blocked_gqa(aq, ak, av)

    mq = _ml_heads(mq, ML_QK_DIM)
    mk = _ml_heads(mk, ML_QK_DIM) * (ML_QK_DIM ** -0.5)
    mv = _ml_heads(mv, ML_V_DIM)
    log_i = jnp.transpose((gi.reshape(B, S, 2, ML_HEADS) + b_igate).astype(jnp.float32), (2, 0, 3, 1))
    log_f = jnp.transpose(jax.nn.log_sigmoid((gf.reshape(B, S, 2, ML_HEADS) + b_fgate).astype(jnp.float32)), (2, 0, 3, 1))
    h_fwd = _mlstm_chunkwise(mq, mk, mv, log_f[0], log_i[0])
    h_bwd = _flip_seq(_mlstm_chunkwise(_flip_seq(mq), _flip_seq(mk), _flip_seq(mv),
                                       _flip_seq(log_f[1]), _flip_seq(log_i[1])))
    hm = jnp.transpose(h_fwd + h_bwd, (0, 2, 1, 3))
    hm = _rms_norm(hm, ml_norm.reshape(ML_HEADS, ML_V_DIM)).reshape(B, S, ML_WIDTH)
    ml = (hm * jax.nn.sigmoid(mo.astype(jnp.float32))).astype(x.dtype)

    return jnp.concatenate([att, ml], axis=-1) @ w_out


def _memory_cross_attention(x, mem, wq, wk, wv, wo):
    B, S, D = x.shape
    M = mem.shape[1]
    q = (x @ wq).reshape(B, S, XA_HEADS, XA_HEAD_DIM)
    k = (mem @ wk).reshape(B, M, XA_HEADS, XA_HEAD_DIM)
    v = (mem @ wv).reshape(B, M, XA_HEADS, XA_HEAD_DIM)
    s = jnp.einsum('bqhd,bkhd->bhqk', q, k, preferred_element_type=jnp.float32) * (XA_HEAD_DIM ** -0.5)
    p = jax.nn.softmax(s, axis=-1)
    o = jnp.einsum('bhqk,bkhd->bqhd', p.astype(v.dtype), v).reshape(B, S, D)
    return o @ wo


def _peer(x, wq, subkeys, u, v):
    B, S, D = x.shape
    T = B * S
    xt = x.reshape(T, D)
    q = (xt @ wq).reshape(T, PEER_HEADS, 2, PEER_QDIM // 2)
    s = jnp.einsum('thcd,hckd->thck', q, subkeys).astype(jnp.float32)
    top_s, top_i = lax.top_k(s, PEER_TOPK)
    cand_s = top_s[:, :, 0, :, None] + top_s[:, :, 1, None, :]
    cand_e = top_i[:, :, 0, :, None] * PEER_NKEYS + top_i[:, :, 1, None, :]
    best_s, best_pos = lax.top_k(cand_s.reshape(T, PEER_HEADS, PEER_TOPK * PEER_TOPK), PEER_TOPK)
    experts = jnp.take_along_axis(cand_e.reshape(T, PEER_HEADS, PEER_TOPK * PEER_TOPK), best_pos, axis=-1)
    gates = jax.nn.softmax(best_s, axis=-1)
    E = PEER_HEADS * PEER_TOPK
    nb = T // PEER_BLOCK
    experts = experts.reshape(nb, PEER_BLOCK, E)
    gates = gates.reshape(nb, PEER_BLOCK, E).astype(x.dtype)

    def block(args):
        xb, eb, gb = args
        a = jax.nn.gelu(jnp.einsum('ted,td->te', u[eb], xb), approximate=False) * gb
        return jnp.einsum('te,ted->td', a, v[eb])

    out = lax.map(block, (xt.reshape(nb, PEER_BLOCK, D), experts, gates))
    return out.reshape(B, S, D)


def setup_inputs(seed: int = 0) -> dict:
    key = jax.random.key(seed)
    ks = jax.random.split(key, 23)
    f32 = jnp.float32
    D = D_MODEL

    def nrm(k, shape, scale):
        return jax.random.normal(k, shape, f32) * scale

    def gain(k, shape):
        return 1.0 + 0.02 * jax.random.normal(k, shape, f32)

    def bias(k, shape):
        return 0.02 * jax.random.normal(k, shape, f32)

    col_scale = jnp.asarray(np.concatenate(
        [np.full((n,), BETA if i in (2, 5) else 1.0, dtype=np.float32) for i, n in enumerate(IN_SPLITS)]))
    return {
        'x': nrm(ks[0], (BATCH, SEQ, D), 1.0),
        'mem': nrm(ks[1], (BATCH, MEM_LEN, D), 1.0),
        'w_in': nrm(ks[2], (DEPTH, D, IN_TOTAL), D ** -0.5) * col_scale,
        'b_igate': -1.0 + 0.1 * jax.random.normal(ks[3], (DEPTH, 2, ML_HEADS), f32),
        'b_fgate': jnp.linspace(3.0, 6.0, ML_HEADS, dtype=f32) + 0.1 * jax.random.normal(ks[4], (DEPTH, 2, ML_HEADS), f32),
        'att_q_norm': gain(ks[5], (DEPTH, ATT_HEAD_DIM)),
        'att_k_norm': gain(ks[6], (DEPTH, ATT_HEAD_DIM)),
        'ml_norm': gain(ks[7], (DEPTH, ML_WIDTH)),
        'w_out': nrm(ks[8], (DEPTH, MIX_WIDTH, D), MIX_WIDTH ** -0.5) * BETA,
        'ln1_g': gain(ks[9], (DEPTH, D)),
        'ln1_b': bias(ks[10], (DEPTH, D)),
        'xa_wq': nrm(ks[11], (DEPTH, D, D), D ** -0.5),
        'xa_wk': nrm(ks[12], (DEPTH, D, D), D ** -0.5),
        'xa_wv': nrm(ks[13], (DEPTH, D, D), D ** -0.5) * BETA,
        'xa_wo': nrm(ks[14], (DEPTH, D, D), D ** -0.5) * BETA,
        'ln2_g': gain(ks[15], (DEPTH, D)),
        'ln2_b': bias(ks[16], (DEPTH, D)),
        'peer_wq': nrm(ks[17], (DEPTH, D, PEER_HEADS * PEER_QDIM), D ** -0.5),
        'peer_subkeys': nrm(ks[18], (DEPTH, PEER_HEADS, 2, PEER_NKEYS, PEER_QDIM // 2), (PEER_QDIM // 2) ** -0.5),
        'peer_u': nrm(ks[19], (DEPTH, PEER_EXPERTS, D), D ** -0.5),
        'peer_v': nrm(ks[20], (DEPTH, PEER_EXPERTS, D), (PEER_HEADS * PEER_TOPK) ** -0.5) * BETA,
        'ln3_g': gain(ks[21], (DEPTH, D)),
        'ln3_b': bias(ks[22], (DEPTH, D)),
    }


def reference(x, mem, w_in, b_igate, b_fgate, att_q_norm, att_k_norm, ml_norm, w_out,
              ln1_g, ln1_b, xa_wq, xa_wk, xa_wv, xa_wo, ln2_g, ln2_b,
              peer_wq, peer_subkeys, peer_u, peer_v, ln3_g, ln3_b):
    for l in range(DEPTH):
        y = _hybrid_mixer(x, w_in[l], b_igate[l], b_fgate[l], att_q_norm[l], att_k_norm[l], ml_norm[l], w_out[l])
        x = _layer_norm(ALPHA * x + y, ln1_g[l], ln1_b[l])
        y = _memory_cross_attention(x, mem, xa_wq[l], xa_wk[l], xa_wv[l], xa_wo[l])
        x = _layer_norm(ALPHA * x + y, ln2_g[l], ln2_b[l])
        y = _peer(x, peer_wq[l], peer_subkeys[l], peer_u[l], peer_v[l])
        x = _layer_norm(ALPHA * x + y, ln3_g[l], ln3_b[l])
    return x
```

```python
import contextlib
import math
import numpy as np
import concourse.bass as bass
import concourse.mybir as mybir
from concourse.bass_utils import run_bass_kernel_spmd

F32 = mybir.dt.float32
BF16 = mybir.dt.bfloat16
U32 = mybir.dt.uint32
AF = mybir.ActivationFunctionType
ALU = mybir.AluOpType
AX = mybir.AxisListType

ALPHA = 2.0 ** 0.25
LN_EPS = 1e-5
RMS_EPS = 1e-6
NEG = -1.0e4


class Prog:
    ENGS = ('pe', 'act', 'dve', 'pool', 'sp')

    def __init__(self, nc):
        self.nc = nc
        self.ops = {e: [] for e in self.ENGS}
        self.res = {}
        self.dma_sem_count = {}
        self.epoch = 0

    def _deps(self, eng, reads, writes, is_dma):
        deps = []
        for r in reads:
            st = self.res.get(r)
            if st:
                deps.extend(st['w'])
        for w in writes:
            st = self.res.get(w)
            if st:
                for t in st['w']:
                    if not (is_dma and t[0] == 'dma'):
                        deps.append(t)
                deps.extend(st['r'])
        out = []
        for d in deps:
            if d[0] == 'eng' and d[1] == eng and eng == 'pe':
                continue
            if d not in out:
                out.append(d)
        return out

    def _commit(self, tok, reads, writes):
        for r in reads:
            st = self.res.setdefault(r, {'w': [], 'r': []})
            st['r'].append(tok)
        for w in writes:
            st = self.res.setdefault(w, {'w': [], 'r': []})
            if tok[0] == 'dma' and st['w'] and all(t[0] == 'dma' for t in st['w']) and not st['r']:
                st['w'] = [t for t in st['w'] if t[1] != tok[1]] + [tok]
            else:
                st['w'] = [tok]
            st['r'] = []

    def op(self, eng, fn, reads=(), writes=()):
        deps = self._deps(eng, reads, writes, False)
        tok = ('eng', eng, len(self.ops[eng]))
        self.ops[eng].append({'fn': fn, 'deps': deps, 'needed': False, 'dma': None, 'ep': self.epoch})
        self._commit(tok, reads, writes)
        return tok

    def dma(self, eng, out, in_, reads=(), writes=(), semkey=None):
        deps = self._deps(eng, reads, writes, True)
        sk = semkey if semkey is not None else writes[0]
        self.dma_sem_count[sk] = self.dma_sem_count.get(sk, 0) + 16
        tok = ('dma', sk, self.dma_sem_count[sk])
        self.ops[eng].append({'fn': (lambda e: e.dma_start(out=out, in_=in_)), 'deps': deps,
                              'needed': True, 'dma': sk, 'ep': self.epoch})
        self._commit(tok, reads, writes)
        return tok

    def all_tokens(self):
        toks = []
        for e in self.ENGS:
            for i in range(len(self.ops[e]) - 1, -1, -1):
                o = self.ops[e][i]
                if o['fn'] is not None and o['dma'] is None:
                    toks.append(('eng', e, i))
                    break
        for sk, v in self.dma_sem_count.items():
            toks.append(('dma', sk, v))
        return toks

    def barrier(self, new_epoch=True):
        toks = self.all_tokens()
        for e in self.ENGS:
            self.ops[e].append({'fn': None, 'deps': [t for t in toks if not (t[0] == 'eng' and t[1] == e)],
                                'needed': False, 'dma': None, 'ep': self.epoch})
        self.res = {}
        if new_epoch:
            self.epoch += 1

    def emit(self, final_tokens=()):
        nc = self.nc
        for e in self.ENGS:
            for o in self.ops[e]:
                for d in o['deps']:
                    if d[0] == 'eng':
                        self.ops[d[1]][d[2]]['needed'] = True
        self.maxval = {}
        for e in self.ENGS:
            c = {}
            for o in self.ops[e]:
                if o['dma'] is None and o['needed']:
                    c[o['ep']] = c.get(o['ep'], 0) + 1
                    o['val'] = c[o['ep']]
            self.maxval[e] = dict(c)
        with contextlib.ExitStack() as st:
            esem = {(e, ep): st.enter_context(nc.semaphore('es_%s_%d' % (e, ep)))
                    for e in self.ENGS if e != 'sp' for ep in range(self.epoch + 1)}
            dsem = {}
            for i, k in enumerate(self.dma_sem_count):
                dsem[k] = st.enter_context(nc.semaphore('ds_%d' % i))
            block = st.enter_context(nc.Block())

            def tokval(d):
                if d[0] == 'eng':
                    o = self.ops[d[1]][d[2]]
                    return ('e', d[1], o['ep']), esem[(d[1], o['ep'])], o['val']
                return ('d', d[1]), dsem[d[1]], d[2]

            def body(ename, extra_final=()):
                def f(eng):
                    seen = {}

                    def waits(deps):
                        best = {}
                        for d in deps:
                            k, s, v = tokval(d)
                            if v > best.get(k, (None, 0))[1]:
                                best[k] = (s, v)
                        for k, (s, v) in best.items():
                            if seen.get(k, 0) >= v:
                                continue
                            seen[k] = v
                            eng.wait_ge(s, v)
                    for o in self.ops[ename]:
                        waits(o['deps'])
                        if o['fn'] is None:
                            continue
                        ins = o['fn'](eng)
                        if o['dma'] is not None:
                            ins.then_inc(dsem[o['dma']], 16)
                        elif o['needed']:
                            ins.then_inc(esem[(ename, o['ep'])], 1)
                    waits(extra_final)
                return f
            block.tensor(body('pe'))
            block.scalar(body('act'))
            block.vector(body('dve'))
            block.gpsimd(body('pool'))
            block.sync(body('sp', tuple(final_tokens)))


def build_program(stage='all'):
    nc = bass.Bass("TRN2", target_bir_lowering=False)

    def din(n, s, d=F32):
        return nc.dram_tensor(n, s, d, kind="ExternalInput").ap()

    def dscr(n, s, d):
        return nc.dram_tensor(n, s, d, kind="Internal").ap()

    xl = din("xl", [4096, 2048])
    rope = din("rope", [4096, 256])
    w_in = din("w_in", [2048, 4624])
    bg = din("bg", [1, 16])
    gqk = din("gqk", [1, 256])
    mln_d = din("mln", [1, 1024])
    memb = din("memb", [256, 2048])
    cst_d = din("cst", [128, 1024])
    w_out = din("w_out", [2048, 2048])
    xa_wq = din("xa_wq", [2048, 2048])
    xa_wk = din("xa_wk", [2048, 2048])
    xa_wv = din("xa_wv", [2048, 2048])
    xa_wo = din("xa_wo", [2048, 2048])
    lnp = din("lnp", [6, 2048])
    pwq = din("pwq", [2048, 2048])
    skT = din("skT", [128, 16 * 128])
    uh = din("uh", [128, 128, 2048])
    vh = din("vh", [128, 128, 2048])

    hb_d = dscr("hb_s", [2048, 1024], F32)
    cat_d = dscr("cat_s", [2048, 2048], BF16)
    x1_d = dscr("x1_s", [2048, 2048], F32)
    x2_d = dscr("x2_s", [2048, 2048], F32)
    ubf = dscr("ubf_s", [128, 128, 2048], BF16)
    vbf = dscr("vbf_s", [128, 128, 2048], BF16)

    if stage == 'mix':
        out_d = nc.dram_tensor("out", [2048, 2048], BF16, kind="ExternalOutput").ap()
    else:
        out_d = nc.dram_tensor("out", [2048, 2048], F32, kind="ExternalOutput").ap()

    P = Prog(nc)
    final = []

    with contextlib.ExitStack() as G:
        def sbuf(st, n, s, d):
            return st.enter_context(nc.sbuf_tensor('sb_' + n, s, d))

        ps = [G.enter_context(nc.psum_tensor("ps%d" % i, [128, 512], F32)) for i in range(8)]

        def PS(i):
            return ('ps', i)

        cst = sbuf(G, "cst", [128, 1024], F32)
        ident_b = sbuf(G, "ident_b", [128, 128], BF16)
        ones_b = sbuf(G, "ones_b", [128, 128], BF16)
        ident_f = cst[:, 0:128]
        Umask = cst[:, 128:256]
        Lmask = cst[:, 256:384]
        NEGU = cst[:, 384:512]
        NEGL = cst[:, 512:640]
        ones_f = cst[:, 640:768]
        iota_f = cst[:, 768:896]

        P.dma('sp', cst[:], cst_d, writes=['cst'])
        P.op('dve', lambda e: e.tensor_copy(out=ident_b[:], in_=ident_f), reads=['cst'], writes=['ident_b'])
        P.op('dve', lambda e: e.tensor_copy(out=ones_b[:], in_=ones_f), reads=['cst'], writes=['ones_b'])

        def mm(out, lhsT, rhs, start, stop, reads, writes):
            return P.op('pe', lambda e: e.matmul(out, lhsT, rhs, start=start, stop=stop), reads=reads, writes=writes)

        def tr(out, in_, reads, writes, fp32=False):
            idn = ident_f if fp32 else ident_b[:]
            return P.op('pe', lambda e: e.transpose(out=out, in_=in_, identity=idn),
                        reads=list(reads) + (['cst'] if fp32 else ['ident_b']), writes=writes)

        def act(out, in_, func, reads, writes, bias=None, scale=None):
            kw = {}
            if bias is not None:
                kw['bias'] = bias
            if scale is not None:
                kw['scale'] = scale
            return P.op('act', lambda e: e.activation(out=out, in_=in_, func=func, **kw), reads=reads, writes=writes)

        def tt(eng, out, in0, in1, op, reads, writes):
            return P.op(eng, lambda e: e.tensor_tensor(out=out, in0=in0, in1=in1, op=op), reads=reads, writes=writes)

        def ts(eng, out, in0, s1, s2, op0, op1, reads, writes):
            return P.op(eng, lambda e: e.tensor_scalar(out=out, in0=in0, scalar1=s1, scalar2=s2, op0=op0, op1=op1),
                        reads=reads, writes=writes)

        def stt(out, in0, scalar, in1, op0, op1, reads, writes):
            return P.op('dve', lambda e: e.scalar_tensor_tensor(out=out, in0=in0, scalar=scalar, in1=in1, op0=op0, op1=op1),
                        reads=reads, writes=writes)

        def cp(eng, out, in_, reads, writes):
            if eng == 'act':
                return P.op('act', lambda e: e.activation(out=out, in_=in_, func=AF.Copy), reads=reads, writes=writes)
            return P.op(eng, lambda e: e.tensor_copy(out=out, in_=in_), reads=reads, writes=writes)

        def red(out, in_, reads, writes, op=ALU.add):
            return P.op('dve', lambda e: e.tensor_reduce(out=out, in_=in_, axis=AX.X, op=op), reads=reads, writes=writes)

        def recip(out, in_, reads, writes):
            return P.op('dve', lambda e: e.reciprocal(out=out, in_=in_), reads=reads, writes=writes)

        def memset(eng, ap, val, writes):
            return P.op(eng, lambda e: e.memset(ap, val), writes=writes)

        def op_max(out, in_, reads, writes):
            return P.op('dve', lambda e: e.max(out=out, in_=in_), reads=reads, writes=writes)

        def op_maxidx(out, mx, vals_, reads, writes):
            return P.op('dve', lambda e: e.max_index(out=out, in_max=mx, in_values=vals_), reads=reads, writes=writes)

        def op_mrep(out, rep, vals_, reads, writes):
            return P.op('dve', lambda e: e.match_replace(out=out, in_to_replace=rep, in_values=vals_, imm_value=-1.0e30),
                        reads=reads, writes=writes)

        def op_tss(out, in_, scalar, op, reads, writes):
            return P.op('dve', lambda e: e.tensor_single_scalar(out=out, in_=in_, scalar=scalar, op=op), reads=reads, writes=writes)

        def op_bnstats(out, in_, reads, writes):
            return P.op('dve', lambda e: e.bn_stats(out=out, in_=in_), reads=reads, writes=writes)

        def op_bnaggr(out, in_, reads, writes):
            return P.op('dve', lambda e: e.bn_aggr(out=out, in_=in_), reads=reads, writes=writes)

        rr_state = {'tb': 0, 'ev': 0, 'uv': 0}

        def cast_uv(n):
            for _ in range(n):
                k = rr_state['uv']
                if k >= 256:
                    return
                rr_state['uv'] += 1
                j = k // 2
                if k % 2 == 0:
                    P.dma('pool', ubf[j], uh[j], writes=[('ubf', j)], semkey='ucast')
                else:
                    P.dma('pool', vbf[j], vh[j], writes=[('vbf', j)], semkey='vcast')

        def transpose16(src, src_res, dstT, dst_res, col0):
            for half in range(2):
                bank = 6 + (rr_state['tb'] % 2)
                rr_state['tb'] += 1
                psb = ps[bank][:].bitcast(BF16)
                for k in range(8):
                    c = half * 8 + k
                    tr(psb[:, k * 128:(k + 1) * 128], src[:, c * 128:(c + 1) * 128], [src_res], [PS(bank)])
                eng = 'act' if rr_state['ev'] % 2 == 0 else 'dve'
                rr_state['ev'] += 1
                cp(eng, dstT[:, half * 8:(half + 1) * 8, col0:col0 + 128],
                   psb.rearrange("p (k t) -> p k t", k=8), [], [PS(bank), dst_res])

        with contextlib.ExitStack() as M:
            KT = sbuf(M, "KT", [128, 2, 4096], BF16)
            VA = sbuf(M, "VA", [128, 32, 2, 130], BF16)
            xb = [sbuf(M, "xb%d" % i, [128, 2048], BF16) for i in range(2)]
            xT = sbuf(M, "xT", [128, 16, 512], BF16)
            wb = [sbuf(M, "wb%d" % i, [128, 16, 512], BF16) for i in range(2)]
            wg = sbuf(M, "wg", [128, 16, 16], BF16)
            ropeT = [sbuf(M, "ropeT%d" % i, [128, 256], F32) for i in range(2)]
            gqk_t = sbuf(M, "gqk_t", [128, 256], F32)
            bgt = sbuf(M, "bgt", [128, 16], F32)
            mln_t = sbuf(M, "mln_t", [128, 1024], F32)
            ig = sbuf(M, "ig", [128, 4, 8], F32)
            lf = sbuf(M, "lf", [128, 4, 8], F32)
            t0 = sbuf(M, "t0", [128, 1024], F32)
            t1 = sbuf(M, "t1", [128, 1024], F32)
            qkb = sbuf(M, "qkb", [128, 512], BF16)
            sm = sbuf(M, "sm", [128, 64], F32)
            QT = sbuf(M, "QT", [128, 8, 512], BF16)
            mqT = sbuf(M, "mqT", [128, 4, 512], BF16)
            mkT = sbuf(M, "mkT", [128, 4, 512], BF16)
            mk_tok = sbuf(M, "mk_tok", [128, 4, 512], BF16)
            mv_aug = sbuf(M, "mv_aug", [128, 4, 4, 258], BF16)
            mo_sig = sbuf(M, "mo_sig", [128, 4, 1024], BF16)
            att_g = sbuf(M, "att_g", [128, 4, 1024], BF16)
            mlb = [sbuf(M, "mlb%d" % i, [128, 1024], BF16) for i in range(2)]
            PTb = [sbuf(M, "PTb%d" % i, [128, 512], BF16) for i in range(3)]
            Fm = [sbuf(M, "Fm%d" % i, [128, 128], F32) for i in range(2)]
            DT = [sbuf(M, "DT%d" % i, [128, 128], F32) for i in range(2)]
            EB = [sbuf(M, "EB%d" % i, [128, 128], F32) for i in range(2)]
            PTm = [sbuf(M, "PTm%d" % i, [128, 128], BF16) for i in range(2)]
            qsT = [sbuf(M, "qsT%d" % i, [128, 128], BF16) for i in range(2)]
            kw = [sbuf(M, "kw%d" % i, [128, 128], BF16) for i in range(2)]
            mc = [sbuf(M, "mc%d" % i, [128, 8], F32) for i in range(2)]
            Cf = [sbuf(M, "Cf%d" % i, [128, 4, 257], F32) for i in range(2)]
            Cb = [sbuf(M, "Cb%d" % i, [128, 4, 258], BF16) for i in range(2)]
            hout = sbuf(M, "hout", [128, 1024], F32)
            hbl = sbuf(M, "hbl", [128, 1024], F32)

            P.dma('sp', gqk_t[:], gqk.partition_broadcast(128), writes=['gqk_t'])
            P.dma('sp', bgt[:], bg.partition_broadcast(128), writes=['bgt'])
            P.dma('sp', mln_t[:], mln_d.partition_broadcast(128), writes=['mln_t'])
            w_in_v = w_in.rearrange("(c p) n -> p c n", p=128)
            P.dma('pool', wg[:], w_in_v[:, :, 4608:4624], writes=['wg'])
            memset('pool', VA[:], 1.0, ['VA'])
            memset('pool', mv_aug[:], 1.0, ['mv_aug'])
            for d in range(2):
                memset('pool', Cf[d][:], 0.0, [('Cf', d)])
                memset('pool', Cb[d][:], 0.0, [('Cb', d)])

            st = {'xslot': 0, 'wslot': 0, 'rslot': 0, 'bank': 0, 'mi': 0}

            def load_w(col0, ncols):
                slot = st['wslot'] % 2
                st['wslot'] += 1
                for q in range(4):
                    P.dma('pool', wb[slot][:, q * 4:(q + 1) * 4, 0:ncols], w_in_v[:, q * 4:(q + 1) * 4, col0:col0 + ncols],
                          writes=[('wb', slot)])
                return slot

            def nbank():
                b = st['bank'] % 6
                st['bank'] += 1
                return b

            def load_group(tiles):
                for i, tile in enumerate(tiles):
                    slot = st['xslot'] % 2
                    st['xslot'] += 1
                    P.dma('pool', xb[slot][:], xl[tile * 128:(tile + 1) * 128, :], writes=[('xb', slot)])
                    transpose16(xb[slot], ('xb', slot), xT, 'xT', i * 128)

            def proj_tok(i, wslot, c0, ncols, bank):
                for c in range(16):
                    mm(ps[bank][:, 0:ncols], xT[:, c, i * 128:(i + 1) * 128], wb[wslot][:, c, c0:c0 + ncols],
                       c == 0, c == 15, ['xT', ('wb', wslot)], [PS(bank)])

            def proj_feat(j, wslot, bank):
                for c in range(16):
                    mm(ps[bank][:, 0:512], wb[wslot][:, c, j * 128:(j + 1) * 128], xT[:, c, :],
                       c == 0, c == 15, ['xT', ('wb', wslot)], [PS(bank)])

            def norm_rope(psap, bank, H, gain, rslot, outb):
                W = H * 128
                cp('act', t0[:, 0:W], psap, [], [PS(bank), 't0'])
                tt('dve', t1[:, 0:W], t0[:, 0:W], t0[:, 0:W], ALU.mult, ['t0'], ['t1'])
                red(sm[:, 0:H], t1[:, 0:W].rearrange("p (h d) -> p h d", h=H), ['t1'], ['sm'])
                act(sm[:, 8:8 + H], sm[:, 0:H], AF.Sqrt, ['sm'], ['sm'], bias=eps_rms[:, 0:1], scale=1.0 / 128.0)
                recip(sm[:, 16:16 + H], sm[:, 8:8 + H], ['sm'], ['sm'])
                t0v = t0[:, 0:W].rearrange("p (h d) -> p h d", h=H)
                t1v = t1[:, 0:W].rearrange("p (h d) -> p h d", h=H)
                tt('dve', t1v, t0v, sm[:, 16:16 + H].unsqueeze(2).to_broadcast([128, H, 128]), ALU.mult,
                   ['t0', 'sm'], ['t1'])
                tt('dve', t1v, t1v, gain.unsqueeze(1).to_broadcast([128, H, 128]), ALU.mult, ['t1', 'gqk_t'], ['t1'])
                rt = ropeT[rslot]
                tt('dve', t0v, t1v, rt[:, 0:128].unsqueeze(1).to_broadcast([128, H, 128]), ALU.mult,
                   ['t1', ('rope', rslot)], ['t0'])
                t1z = t1[:, 0:W].rearrange("p (h a z d) -> p h a z d", h=H, a=2, z=2)
                sz = rt[:, 128:256].rearrange("p (a z d) -> p a z d", a=2, z=2)
                q2 = qk2[:, 0:W].rearrange("p (h a z d) -> p h a z d", h=H, a=2, z=2)
                for z in range(2):
                    tt('dve', q2[:, :, :, z, :], t1z[:, :, :, 1 - z, :],
                       sz[:, :, z, :].unsqueeze(1).to_broadcast([128, H, 2, 32]), ALU.mult,
                       ['t1', ('rope', rslot)], ['qk2'])
                tt('dve', outb, t0[:, 0:W], qk2[:, 0:W], ALU.add, ['t0', 'qk2'], ['qkb'])

            qk2 = sbuf(M, "qk2", [128, 512], F32)
            eps_rms = sbuf(M, "eps_rms", [128, 2], F32)
            memset('pool', eps_rms[:, 0:1], RMS_EPS, ['eps_rms'])
            memset('pool', eps_rms[:, 1:2], 1.0, ['eps_rms'])

            def load_rope(tile):
                slot = st['rslot'] % 2
                st['rslot'] += 1
                P.dma('sp', ropeT[slot][:], rope[tile * 128:(tile + 1) * 128, :], writes=[('rope', slot)])
                return slot

            def do_kv(tiles, wslot):
                for i, tile in enumerate(tiles):
                    bank = nbank()
                    proj_tok(i, wslot, 0, 512, bank)
                    rs = load_rope(tile)
                    cp('act', VA[:, tile, :, 0:128], ps[bank][:, 256:512].rearrange("p (g d) -> p g d", g=2),
                       [], [PS(bank), 'VA'])
                    norm_rope(ps[bank][:, 0:256], bank, 2, gqk_t[:, 128:256], rs, qkb[:, 0:256])
                    tb = 6 + (rr_state['tb'] % 2)
                    rr_state['tb'] += 1
                    psb = ps[tb][:].bitcast(BF16)
                    for g in range(2):
                        tr(psb[:, g * 128:(g + 1) * 128], qkb[:, g * 128:(g + 1) * 128], ['qkb'], [PS(tb)])
                    cp('act', KT[:, :, tile * 128:(tile + 1) * 128], psb[:, 0:256].rearrange("p (g t) -> p g t", g=2),
                       [], [PS(tb), 'KT'])

            def do_gates(n):
                for i in range(n):
                    bank = nbank()
                    for c in range(16):
                        mm(ps[bank][:, 0:16], xT[:, c, i * 128:(i + 1) * 128], wg[:, c, :], c == 0, c == 15,
                           ['xT', 'wg'], [PS(bank)])
                    stt(ig[:, i, :], ps[bank][:, 0:8], -0.5 * math.log(128.0), bgt[:, 0:8], ALU.add, ALU.add,
                        ['bgt'], [PS(bank), 'ig'])
                    tt('dve', sm[:, 32:40], ps[bank][:, 8:16], bgt[:, 8:16], ALU.add, ['bgt'], [PS(bank), 'sm'])
                    act(sm[:, 40:48], sm[:, 32:40], AF.Exp, ['sm'], ['sm'], scale=-1.0)
                    act(sm[:, 48:56], sm[:, 40:48], AF.Ln, ['sm'], ['sm'], bias=eps_rms[:, 1:2], scale=1.0)
                    ts('dve', lf[:, i, :], sm[:, 48:56], -1.0, 0.0, ALU.mult, ALU.add, ['sm'], ['lf'])

            def do_mk_tok(n, wslot):
                for i in range(n):
                    bank = nbank()
                    proj_tok(i, wslot, 0, 512, bank)
                    cp('act' if i % 2 == 0 else 'dve', mk_tok[:, i, :], ps[bank][:, 0:512], [], [PS(bank), 'mk_tok'])

            def do_mv(n, wslot, half):
                for i in range(n):
                    bank = nbank()
                    proj_tok(i, wslot, 0, 512, bank)
                    cp('dve' if i % 2 == 0 else 'act', mv_aug[:, i, half * 2:half * 2 + 2, 0:256],
                       ps[bank][:, 0:512].rearrange("p (h d) -> p h d", h=2), [], [PS(bank), 'mv_aug'])

            def do_feat(dst, dst_res, wslot):
                for j in range(4):
                    bank = nbank()
                    proj_feat(j, wslot, bank)
                    cp('act' if j % 2 == 0 else 'dve', dst[:, j, :], ps[bank][:, 0:512], [], [PS(bank), dst_res])

            def mlstm_tile(i, d, with_out, add_hbl):
                MASK = Umask if d == 0 else Lmask
                NEGM = NEGU if d == 0 else NEGL
                last = 127 if d == 0 else 0
                for h in range(4):
                    j = d * 4 + h
                    k = st['mi'] % 2
                    st['mi'] += 1
                    bX, bY, bZ = (0, 1, 2) if k == 0 else (3, 4, 5)
                    fcol = lf[:, i, j:j + 1]
                    icol = ig[:, i, j:j + 1]
                    ts('dve', Fm[k][:], MASK, fcol, 0.0, ALU.mult, ALU.add, ['cst', 'lf'], [('Fm', k)])
                    mm(ps[bX][:, 0:128], ones_f, Fm[k][:], True, True, ['cst', ('Fm', k)], [PS(bX)])
                    if with_out:
                        mm(ps[bX][:, 128:256], ones_f, Fm[k][:], True, False, ['cst', ('Fm', k)], [PS(bX)])
                        mm(ps[bX][:, 128:256], ident_f, NEGM, False, True, ['cst'], [PS(bX)])
                    mm(ps[bX][:, 256:257], MASK, fcol, True, True, ['cst', 'lf'], [PS(bX)])
                    tt('dve', mc[k][:, 0:1], icol, ps[bX][:, 256:257], ALU.subtract, ['ig'], [PS(bX), ('mc', k)])
                    act(mc[k][:, 1:2], ps[bX][:, last:last + 1], AF.Exp, [('mc', k)], [PS(bX), ('mc', k)], bias=mc[k][:, 0:1], scale=1.0)
                    act(mc[k][:, 2:3], ps[bX][:, last:last + 1], AF.Exp, [('mc', k)], [PS(bX), ('mc', k)])
                    if with_out:
                        act(EB[k][:], ps[bX][:, 0:128], AF.Exp, [], [PS(bX), ('EB', k)])
                        act(DT[k][:], ps[bX][:, 128:256], AF.Exp, [('mc', k)], [PS(bX), ('DT', k)], bias=mc[k][:, 0:1], scale=1.0)
                        mm(ps[bX][:, 384:512], mkT[:, h, i * 128:(i + 1) * 128], mqT[:, h, i * 128:(i + 1) * 128],
                           True, True, ['mkT', 'mqT'], [PS(bX)])
                        tt('dve', PTm[k][:], ps[bX][:, 384:512], DT[k][:], ALU.mult, [('DT', k)], [PS(bX), ('PTm', k)])
                        tt('dve', qsT[k][:], mqT[:, h, i * 128:(i + 1) * 128], EB[k][:], ALU.mult, ['mqT', ('EB', k)], [('qsT', k)])
                        mm(ps[bY][:, 0:257], PTm[k][:], mv_aug[:, i, h, 0:257], True, False, [('PTm', k), 'mv_aug'], [PS(bY)])
                        mm(ps[bY][:, 0:257], qsT[k][:], Cb[d][:, h, 0:257], False, True, [('qsT', k), ('Cb', d)], [PS(bY)])
                        act(mc[k][:, 3:4], ps[bY][:, 256:257], AF.Abs, [('mc', k)], [PS(bY), ('mc', k)])
                        ts('dve', mc[k][:, 4:5], mc[k][:, 3:4], 1.0, 0.0, ALU.max, ALU.add, [('mc', k)], [('mc', k)])
                        recip(mc[k][:, 5:6], mc[k][:, 4:5], [('mc', k)], [('mc', k)])
                        if add_hbl:
                            stt(hout[:, h * 256:(h + 1) * 256], ps[bY][:, 0:256], mc[k][:, 5:6], hbl[:, h * 256:(h + 1) * 256],
                                ALU.mult, ALU.add, [('mc', k), 'hbl'], [PS(bY), 'hout'])
                        else:
                            ts('dve', hout[:, h * 256:(h + 1) * 256], ps[bY][:, 0:256], mc[k][:, 5:6], 0.0, ALU.mult, ALU.add,
                               [('mc', k)], [PS(bY), 'hout'])
                    ts('dve', kw[k][:], mk_tok[:, i, h * 128:(h + 1) * 128], mc[k][:, 1:2], 0.0, ALU.mult, ALU.add,
                       ['mk_tok', ('mc', k)], [('kw', k)])
                    mm(ps[bZ][:, 0:257], kw[k][:], mv_aug[:, i, h, 0:257], True, True, [('kw', k), 'mv_aug'], [PS(bZ)])
                    stt(Cf[d][:, h, :], Cf[d][:, h, :], mc[k][:, 2:3], ps[bZ][:, 0:257], ALU.mult, ALU.add,
                        [('mc', k)], [PS(bZ), ('Cf', d)])
                    cp('act', Cb[d][:, h, 0:257], Cf[d][:, h, :], [('Cf', d)], [('Cb', d)])

            for grp in range(7, 3, -1):
                tiles = [grp * 4 + i for i in range(4)]
                load_group(tiles)
                if stage == 'all':
                    cast_uv(14)
                ws = load_w(1024, 512)
                do_kv(tiles, ws)
                do_gates(4)
                ws = load_w(2048, 512)
                do_mk_tok(4, ws)
                ws = load_w(2560, 512)
                do_mv(4, ws, 0)
                ws = load_w(3072, 512)
                do_mv(4, ws, 1)
                for i in range(3, -1, -1):
                    mlstm_tile(i, 1, False, False)

            for grp in range(3, -1, -1):
                tiles = [grp * 4 + i for i in range(4)]
                load_group(tiles)
                if stage == 'all':
                    cast_uv(14)
                ws = load_w(1024, 512)
                do_kv(tiles, ws)
                do_gates(4)
                ws = load_w(1536, 512)
                do_feat(mqT, 'mqT', ws)
                ws = load_w(2048, 512)
                do_feat(mkT, 'mkT', ws)
                do_mk_tok(4, ws)
                ws = load_w(2560, 512)
                do_mv(4, ws, 0)
                ws = load_w(3072, 512)
                do_mv(4, ws, 1)
                for i in range(3, -1, -1):
                    mlstm_tile(i, 1, True, False)
                    tile = tiles[i]
                    P.dma('sp', hb_d[tile * 128:(tile + 1) * 128, :], hout[:], reads=['hout'], writes=[('hb', tile)],
                          semkey='hout_st')

            for grp in range(4):
                tiles = [grp * 4 + i for i in range(4)]
                load_group(tiles)
                if stage == 'all':
                    cast_uv(14)
                for blk in range(2):
                    ws = load_w(blk * 512, 512)
                    for i, tile in enumerate(tiles):
                        bank = nbank()
                        proj_tok(i, ws, 0, 512, bank)
                        rs = load_rope(tile)
                        norm_rope(ps[bank][:, 0:512], bank, 4, gqk_t[:, 0:128], rs, qkb[:, 0:512])
                        tb = 6 + (rr_state['tb'] % 2)
                        rr_state['tb'] += 1
                        psb = ps[tb][:].bitcast(BF16)
                        for hh in range(4):
                            tr(psb[:, hh * 128:(hh + 1) * 128], qkb[:, hh * 128:(hh + 1) * 128], ['qkb'], [PS(tb)])
                        cp('act', QT[:, blk * 4:(blk + 1) * 4, i * 128:(i + 1) * 128],
                           psb[:, 0:512].rearrange("p (h t) -> p h t", h=4), [], [PS(tb), 'QT'])
                do_gates(4)
                ws = load_w(1536, 512)
                do_feat(mqT, 'mqT', ws)
                ws = load_w(2048, 512)
                do_feat(mkT, 'mkT', ws)
                do_mk_tok(4, ws)
                ws = load_w(2560, 512)
                do_mv(4, ws, 0)
                ws = load_w(3072, 512)
                do_mv(4, ws, 1)
                for blk in range(2):
                    ws = load_w(3584 + blk * 512, 512)
                    for i in range(4):
                        bank = nbank()
                        proj_tok(i, ws, 0, 512, bank)
                        act(mo_sig[:, i, blk * 512:(blk + 1) * 512], ps[bank][:, 0:512], AF.Sigmoid, [], [PS(bank), 'mo_sig'])
                it = 0
                for hq in range(8):
                    g = hq // 4
                    for kt in range(32):
                        bS = 4 + (it % 2)
                        pt = it % 3
                        it += 1
                        mm(ps[bS][:, 0:512], KT[:, g, kt * 128:(kt + 1) * 128], QT[:, hq, :], True, True, ['KT', 'QT'], [PS(bS)])
                        act(PTb[pt][:], ps[bS][:, 0:512], AF.Exp, [], [PS(bS), ('PTb', pt)], scale=128.0 ** -0.5)
                        for qs in range(4):
                            mm(ps[qs][:, 0:129], PTb[pt][:, qs * 128:(qs + 1) * 128], VA[:, kt, g, 0:129], kt == 0, kt == 31,
                               [('PTb', pt), 'VA'], [PS(qs)])
                    for qs in range(4):
                        recip(sm[:, 56 + qs:57 + qs], ps[qs][:, 128:129], [], [PS(qs), 'sm'])
                        ts('dve', att_g[:, qs, hq * 128:(hq + 1) * 128], ps[qs][:, 0:128], sm[:, 56 + qs:57 + qs], 0.0,
                           ALU.mult, ALU.add, ['sm'], [PS(qs), 'att_g'])
                for i, tile in enumerate(tiles):
                    P.dma('sp', hbl[:], hb_d[tile * 128:(tile + 1) * 128, :], reads=[('hb', tile)], writes=['hbl'])
                    mlstm_tile(i, 0, True, True)
                    tt('dve', t0[:], hout[:], hout[:], ALU.mult, ['hout'], ['t0'])
                    red(sm[:, 0:4], t0[:].rearrange("p (h d) -> p h d", h=4), ['t0'], ['sm'])
                    act(sm[:, 8:12], sm[:, 0:4], AF.Sqrt, ['sm'], ['sm'], bias=eps_rms[:, 0:1], scale=1.0 / 256.0)
                    recip(sm[:, 16:20], sm[:, 8:12], ['sm'], ['sm'])
                    tt('dve', t0[:].rearrange("p (h d) -> p h d", h=4), hout[:].rearrange("p (h d) -> p h d", h=4),
                       sm[:, 16:20].unsqueeze(2).to_broadcast([128, 4, 256]), ALU.mult, ['hout', 'sm'], ['t0'])
                    tt('pool', t1[:], t0[:], mln_t[:], ALU.mult, ['t0', 'mln_t'], ['t1'])
                    ms = i % 2
                    tt('pool', mlb[ms][:], t1[:], mo_sig[:, i, :], ALU.mult, ['t1', 'mo_sig'], [('mlb', ms)])
                    P.dma('sp', cat_d[tile * 128:(tile + 1) * 128, 1024:2048], mlb[ms][:], reads=[('mlb', ms)],
                          writes=[('cat_ml', tile)], semkey=('mlb_st', ms))
                    P.dma('sp', cat_d[tile * 128:(tile + 1) * 128, 0:1024], att_g[:, i, :], reads=['att_g'],
                          writes=[('cat_att', tile)], semkey='att_st')
            P.barrier()

        if stage == 'mix':
            with contextlib.ExitStack() as Dg:
                cb = sbuf(Dg, "dbg_cb", [128, 2048], BF16)
                for tile in range(16):
                    P.dma('sp', cb[:], cat_d[tile * 128:(tile + 1) * 128, :], reads=[('cat_ml', tile), ('cat_att', tile)], writes=['dbg_cb'])
                    final.append(P.dma('sp', out_d[tile * 128:(tile + 1) * 128, :], cb[:], reads=['dbg_cb'], writes=[('out', tile)],
                                       semkey='dbg_out'))
            P.emit(final_tokens=final[-1:])
            return nc

        with contextlib.ExitStack() as E:
            xbE = [sbuf(E, "xbE%d" % i, [128, 2048], BF16) for i in range(2)]
            aT = sbuf(E, "aT", [128, 16, 512], BF16)
            wbD = [sbuf(E, "wbD%d" % i, [128, 16, 512], BF16) for i in range(2)]
            r_g = sbuf(E, "r_g", [128, 4, 2048], F32)
            ln_g = sbuf(E, "ln_g", [128, 2048], F32)
            ln_b = sbuf(E, "ln_b", [128, 2048], F32)
            st6 = sbuf(E, "st6", [128, 4, 6], F32)
            lmv = sbuf(E, "lmv", [128, 8], F32)
            eps_ln = sbuf(E, "eps_ln", [128, 1], F32)
            memT = sbuf(E, "memT", [128, 16, 256], BF16)
            kmT = sbuf(E, "kmT", [128, 16, 256], BF16)
            vm = sbuf(E, "vm", [128, 2, 2048], BF16)
            q1T = sbuf(E, "q1T", [128, 16, 512], BF16)
            o_g = sbuf(E, "o_g", [128, 4, 2048], BF16)
            PTx = [sbuf(E, "PTx%d" % i, [128, 512], BF16) for i in range(4)]
            rrx = sbuf(E, "rrx", [128, 8], F32)
            memset('pool', eps_ln[:], LN_EPS, ['eps_ln'])
            sE = {'w': 0, 'x': 0, 'bank': 0, 'pt': 0}

            def load_wE(W, col0):
                slot = sE['w'] % 2
                sE['w'] += 1
                Wv = W.rearrange("(c p) n -> p c n", p=128)
                for q in range(4):
                    P.dma('pool', wbD[slot][:, q * 4:(q + 1) * 4, :], Wv[:, q * 4:(q + 1) * 4, col0:col0 + 512],
                          writes=[('wbD', slot)])
                return slot

            def nbE():
                b = sE['bank'] % 6
                sE['bank'] += 1
                return b

            def load_T(src_rows, dst, dst_res, col0, cast, extra_reads=()):
                slot = sE['x'] % 2
                sE['x'] += 1
                P.dma('pool' if cast else 'sp', xbE[slot][:], src_rows, reads=list(extra_reads), writes=[('xbE', slot)])
                transpose16(xbE[slot], ('xbE', slot), dst, dst_res, col0)

            def load_ln(k):
                P.dma('sp', ln_g[:], lnp[2 * k:2 * k + 1, :].partition_broadcast(128), writes=['ln_g'])
                P.dma('sp', ln_b[:], lnp[2 * k + 1:2 * k + 2, :].partition_broadcast(128), writes=['ln_b'])

            def dense_ln(W, tiles, out_d_, out_key):
                for cbk in range(4):
                    ws = load_wE(W, cbk * 512)
                    for i in range(4):
                        bank = nbE()
                        for c in range(16):
                            mm(ps[bank][:, 0:512], aT[:, c, i * 128:(i + 1) * 128], wbD[ws][:, c, :], c == 0, c == 15,
                               ['aT', ('wbD', ws)], [PS(bank)])
                        stt(r_g[:, i, cbk * 512:(cbk + 1) * 512], r_g[:, i, cbk * 512:(cbk + 1) * 512], ALPHA,
                            ps[bank][:, 0:512], ALU.mult, ALU.add, [], [PS(bank), ('r_g', i)])
                for i, tile in enumerate(tiles):
                    for q in range(4):
                        op_bnstats(st6[:, q, :], r_g[:, i, q * 512:(q + 1) * 512], [('r_g', i)], ['st6'])
                    op_bnaggr(lmv[:, 0:2], st6[:].rearrange("p a b -> p (a b)"), ['st6'], ['lmv'])
                    act(lmv[:, 2:3], lmv[:, 1:2], AF.Sqrt, ['lmv', 'eps_ln'], ['lmv'], bias=eps_ln[:, 0:1], scale=1.0)
                    recip(lmv[:, 3:4], lmv[:, 2:3], ['lmv'], ['lmv'])
                    ts('dve', r_g[:, i, :], r_g[:, i, :], lmv[:, 0:1], lmv[:, 3:4], ALU.subtract, ALU.mult, ['lmv'], [('r_g', i)])
                    tt('pool', r_g[:, i, :], r_g[:, i, :], ln_g[:], ALU.mult, ['ln_g'], [('r_g', i)])
                    tt('pool', r_g[:, i, :], r_g[:, i, :], ln_b[:], ALU.add, ['ln_b'], [('r_g', i)])
                    tk = P.dma('sp', out_d_[tile * 128:(tile + 1) * 128, :], r_g[:, i, :], reads=[('r_g', i)],
                               writes=[(out_key, tile)], semkey=('r_g_st', i))
                    if out_key == 'out':
                        final.append(tk)

            load_ln(0)
            for grp in range(4):
                tiles = [grp * 4 + i for i in range(4)]
                if stage == 'all':
                    cast_uv(11)
                for i, tile in enumerate(tiles):
                    P.dma('sp', r_g[:, i, :], xl[tile * 128:(tile + 1) * 128, :], writes=[('r_g', i)])
                    load_T(cat_d[tile * 128:(tile + 1) * 128, :], aT, 'aT', i * 128, False,
                           extra_reads=[('cat_ml', tile), ('cat_att', tile)])
                dense_ln(w_out, tiles, x1_d, 'x1')

            load_ln(1)
            for mt in range(2):
                load_T(memb[mt * 128:(mt + 1) * 128, :], memT, 'memT', mt * 128, True)
            for cbk in range(4):
                ws = load_wE(xa_wk, cbk * 512)
                for j in range(4):
                    bank = nbE()
                    for c in range(16):
                        mm(ps[bank][:, 0:256], wbD[ws][:, c, j * 128:(j + 1) * 128], memT[:, c, :], c == 0, c == 15,
                           ['memT', ('wbD', ws)], [PS(bank)])
                    cp('act' if j % 2 == 0 else 'dve', kmT[:, cbk * 4 + j, :], ps[bank][:, 0:256], [], [PS(bank), 'kmT'])
            for cbk in range(4):
                ws = load_wE(xa_wv, cbk * 512)
                for mt in range(2):
                    bank = nbE()
                    for c in range(16):
                        mm(ps[bank][:, 0:512], memT[:, c, mt * 128:(mt + 1) * 128], wbD[ws][:, c, :], c == 0, c == 15,
                           ['memT', ('wbD', ws)], [PS(bank)])
                    cp('act' if mt % 2 == 0 else 'dve', vm[:, mt, cbk * 512:(cbk + 1) * 512], ps[bank][:, 0:512], [], [PS(bank), 'vm'])
            for grp in range(4):
                tiles = [grp * 4 + i for i in range(4)]
                if stage == 'all':
                    cast_uv(11)
                for i, tile in enumerate(tiles):
                    P.dma('sp', r_g[:, i, :], x1_d[tile * 128:(tile + 1) * 128, :], reads=[('x1', tile)], writes=[('r_g', i)])
                    load_T(x1_d[tile * 128:(tile + 1) * 128, :], aT, 'aT', i * 128, True, extra_reads=[('x1', tile)])
                for cbk in range(4):
                    ws = load_wE(xa_wq, cbk * 512)
                    for j in range(4):
                        bank = nbE()
                        for c in range(16):
                            mm(ps[bank][:, 0:512], wbD[ws][:, c, j * 128:(j + 1) * 128], aT[:, c, :], c == 0, c == 15,
                               ['aT', ('wbD', ws)], [PS(bank)])
                        cp('act' if j % 2 == 0 else 'dve', q1T[:, cbk * 4 + j, :], ps[bank][:, 0:512], [], [PS(bank), 'q1T'])
                for h in range(4):
                    pts = []
                    for mt in range(2):
                        bank = nbE()
                        for dc in range(4):
                            mm(ps[bank][:, 0:512], kmT[:, h * 4 + dc, mt * 128:(mt + 1) * 128], q1T[:, h * 4 + dc, :],
                               dc == 0, dc == 3, ['kmT', 'q1T'], [PS(bank)])
                        pt = sE['pt'] % 4
                        sE['pt'] += 1
                        act(PTx[pt][:], ps[bank][:, 0:512], AF.Exp, [], [PS(bank), ('PTx', pt)], scale=512.0 ** -0.5)
                        pts.append(pt)
                    for qs in range(4):
                        bank = nbE()
                        for mt in range(2):
                            mm(ps[bank][:, 0:512], PTx[pts[mt]][:, qs * 128:(qs + 1) * 128], vm[:, mt, h * 512:(h + 1) * 512],
                               mt == 0, mt == 1, [('PTx', pts[mt]), 'vm'], [PS(bank)])
                        b2 = 6 + (qs % 2)
                        for mt in range(2):
                            mm(ps[b2][:, 0:1], PTx[pts[mt]][:, qs * 128:(qs + 1) * 128], ones_b[:, 0:1],
                               mt == 0, mt == 1, [('PTx', pts[mt]), 'ones_b'], [PS(b2)])
                        recip(rrx[:, qs:qs + 1], ps[b2][:, 0:1], [], [PS(b2), 'rrx'])
                        ts('dve', o_g[:, qs, h * 512:(h + 1) * 512], ps[bank][:, 0:512], rrx[:, qs:qs + 1], 0.0, ALU.mult, ALU.add,
                           ['rrx'], [PS(bank), ('o_g', qs)])
                for i in range(4):
                    transpose16(o_g[:, i, :], ('o_g', i), aT, 'aT', i * 128)
                dense_ln(xa_wo, tiles, x2_d, 'x2')
            P.barrier()

        if stage == 'de':
            with contextlib.ExitStack() as Dg:
                cb2 = sbuf(Dg, "dbg_cb2", [128, 2048], F32)
                for tile in range(16):
                    P.dma('sp', cb2[:], x2_d[tile * 128:(tile + 1) * 128, :], reads=[('x2', tile)], writes=['dbg_cb2'])
                    final.append(P.dma('sp', out_d[tile * 128:(tile + 1) * 128, :], cb2[:], reads=['dbg_cb2'], writes=[('out', tile)],
                                       semkey='dbg_out'))
            P.emit(final_tokens=final[-1:])
            return nc

        NSUB = 2
        TG = NSUB * 128
        JB = 4
        with contextlib.ExitStack() as Fp:
            Wbig = sbuf(Fp, "Wbig", [128, 128, TG], BF16)
            acc_sb = sbuf(Fp, "acc_sb", [128, NSUB, 2048], F32)
            x2T = sbuf(Fp, "x2T", [128, 16, TG], BF16)
            cast_uv(256)
            P.barrier(new_epoch=False)
            for grp in range(2048 // TG):
                tiles = [grp * NSUB + i for i in range(NSUB)]
                with contextlib.ExitStack() as F1:
                    gname = "g%d_" % grp
                    xbF = [sbuf(F1, gname + "xbF%d" % i, [128, 2048], BF16) for i in range(2)]
                    wbF = [sbuf(F1, gname + "wbF%d" % i, [128, 16, 128], BF16) for i in range(2)]
                    qpT = sbuf(F1, gname + "qpT", [128, 16, TG], F32)
                    skS = sbuf(F1, gname + "skS", [128, 16 * 128], F32)
                    s_sb = sbuf(F1, gname + "s_sb", [128, 16, 128], F32)
                    s2 = sbuf(F1, gname + "s2", [128, 256], F32)
                    vals = sbuf(F1, gname + "vals", [128, 16, 16], F32)
                    idx = sbuf(F1, gname + "idx", [128, 16, 16], U32)
                    idxf = sbuf(F1, gname + "idxf", [128, 16, 16], F32)
                    cand = sbuf(F1, gname + "cand", [128, 8, 256], F32)
                    cv = sbuf(F1, gname + "cv", [128, 8, 16], F32)
                    cpos = sbuf(F1, gname + "cpos", [128, 8, 16], U32)
                    rk = sbuf(F1, gname + "rk", [128, 2, 128], U32)
                    rkf = sbuf(F1, gname + "rkf", [128, 2, 128], F32)
                    oh = sbuf(F1, gname + "oh", [128, 8, 16, 16], F32)
                    sel = sbuf(F1, gname + "sel", [128, 3, 128], F32)
                    selT = sbuf(F1, gname + "selT", [128, 3, 128], F32)
                    gz = sbuf(F1, gname + "gz", [128, 16], F32)
                    OA = [sbuf(F1, gname + "OA%d" % i, [128, 16, 128], BF16) for i in range(2)]
                    OB = [sbuf(F1, gname + "OB%d" % i, [128, 16, 128], BF16) for i in range(2)]
                    P.dma('sp', skS[:], skT, writes=['skS'])
                    for i, tile in enumerate(tiles):
                        P.dma('pool', xbF[i % 2][:], x2_d[tile * 128:(tile + 1) * 128, :], reads=[('x2', tile)], writes=[('xbF', i % 2)])
                        transpose16(xbF[i % 2], ('xbF', i % 2), x2T, 'x2T', i * 128)
                    pwq_v = pwq.rearrange("(c p) n -> p c n", p=128)
                    for hc in range(16):
                        ws = hc % 2
                        for q in range(2):
                            P.dma('pool', wbF[ws][:, q * 8:(q + 1) * 8, :], pwq_v[:, q * 8:(q + 1) * 8, hc * 128:(hc + 1) * 128],
                                  writes=[('wbF', ws)])
                        bank = hc % 4
                        for c in range(16):
                            mm(ps[bank][:, 0:TG], wbF[ws][:, c, :], x2T[:, c, :], c == 0, c == 15, ['x2T', ('wbF', ws)], [PS(bank)])
                        cp('act' if hc % 2 == 0 else 'dve', qpT[:, hc, :], ps[bank][:, 0:TG], [], [PS(bank), 'qpT'])
                    for t in range(NSUB):
                        for q4 in range(4):
                            bank = q4
                            for r4 in range(4):
                                hc = q4 * 4 + r4
                                mm(ps[bank][:, r4 * 128:(r4 + 1) * 128], qpT[:, hc, t * 128:(t + 1) * 128], skS[:, hc * 128:(hc + 1) * 128],
                                   True, True, ['qpT', 'skS'], [PS(bank)])
                            cp('act' if q4 % 2 == 0 else 'dve', s_sb[:, q4 * 4:(q4 + 1) * 4, :],
                               ps[bank][:, 0:512].rearrange("p (a k) -> p a k", a=4), [], [PS(bank), 's_sb'])
                        for hc in range(16):
                            op_max(vals[:, hc, 0:8], s_sb[:, hc, :], ['s_sb'], ['vals'])
                            op_maxidx(idx[:, hc, 0:8], vals[:, hc, 0:8], s_sb[:, hc, :], ['s_sb', 'vals'], ['idx'])
                            op_mrep(s2[:, 0:128], vals[:, hc, 0:8], s_sb[:, hc, :], ['s_sb', 'vals'], ['s2'])
                            op_max(vals[:, hc, 8:16], s2[:, 0:128], ['s2'], ['vals'])
                            op_maxidx(idx[:, hc, 8:16], vals[:, hc, 8:16], s2[:, 0:128], ['s2', 'vals'], ['idx'])
                        cp('dve', idxf[:], idx[:], ['idx'], ['idxf'])
                        vv = vals[:].rearrange("p (h c) a -> p h c a", c=2)
                        tt('dve', cand[:].rearrange("p h (a b) -> p h a b", a=16), vv[:, :, 0, :].unsqueeze(3).to_broadcast([128, 8, 16, 16]),
                           vv[:, :, 1, :].unsqueeze(2).to_broadcast([128, 8, 16, 16]), ALU.add, ['vals'], ['cand'])
                        for h in range(8):
                            op_max(cv[:, h, 0:8], cand[:, h, :], ['cand'], ['cv'])
                            op_maxidx(cpos[:, h, 0:8], cv[:, h, 0:8], cand[:, h, :], ['cand', 'cv'], ['cpos'])
                            op_mrep(s2[:], cv[:, h, 0:8], cand[:, h, :], ['cand', 'cv'], ['s2'])
                            op_max(cv[:, h, 8:16], s2[:], ['s2'], ['cv'])
                            op_maxidx(cpos[:, h, 8:16], cv[:, h, 8:16], s2[:], ['s2', 'cv'], ['cpos'])
                        g3 = sel[:, 2, :].rearrange("p (h c) -> p h c", h=8)
                        tt('dve', g3, cv[:], cv[:, :, 0:1].to_broadcast([128, 8, 16]), ALU.subtract, ['cv'], ['sel'])
                        act(g3, g3, AF.Exp, [], ['sel'])
                        red(gz[:, 0:8], g3, ['sel'], ['gz'])
                        recip(gz[:, 8:16], gz[:, 0:8], ['gz'], ['gz'])
                        tt('dve', g3, g3, gz[:, 8:16].unsqueeze(2).to_broadcast([128, 8, 16]), ALU.mult, ['gz'], ['sel'])
                        cpf = cpos[:].rearrange("p h c -> p (h c)")
                        op_tss(rk[:, 0, :], cpf, 4, ALU.logical_shift_right, ['cpos'], ['rk'])
                        op_tss(rk[:, 1, :], cpf, 15, ALU.bitwise_and, ['cpos'], ['rk'])
                        cp('dve', rkf[:], rk[:], ['rk'], ['rkf'])
                        idv = idxf[:].rearrange("p (h c) a -> p h c a", c=2)
                        for half in range(2):
                            rv = rkf[:, half, :].rearrange("p (h c) -> p h c", h=8)
                            tt('dve', oh[:], rv.unsqueeze(3).to_broadcast([128, 8, 16, 16]),
                               iota_f[:, 0:16].unsqueeze(1).unsqueeze(1).to_broadcast([128, 8, 16, 16]), ALU.is_equal, ['rkf', 'cst'], ['oh'])
                            tt('dve', oh[:], oh[:], idv[:, :, half, :].unsqueeze(2).to_broadcast([128, 8, 16, 16]), ALU.mult, ['idxf'], ['oh'])
                            red(sel[:, half, :].rearrange("p (h c) -> p h c", h=8), oh[:], ['oh'], ['sel'])
                        for q3 in range(3):
                            tr(ps[4][:, q3 * 128:(q3 + 1) * 128], sel[:, q3, :], ['sel'], [PS(4)], fp32=True)
                        cp('act', selT[:], ps[4][:, 0:384].rearrange("p (a t) -> p a t", a=3), [], [PS(4), 'selT'])
                        for tc in range(8):
                            k = tc % 2
                            tok0 = tc * 16
                            io = iota_f.unsqueeze(1).to_broadcast([128, 16, 128])
                            tt('dve', OA[k][:], io, selT[:, 0, tok0:tok0 + 16].unsqueeze(2).to_broadcast([128, 16, 128]), ALU.is_equal,
                               ['cst', 'selT'], [('OA', k)])
                            tt('dve', OA[k][:], OA[k][:], selT[:, 2, tok0:tok0 + 16].unsqueeze(2).to_broadcast([128, 16, 128]), ALU.mult,
                               ['selT'], [('OA', k)])
                            tt('dve', OB[k][:], io, selT[:, 1, tok0:tok0 + 16].unsqueeze(2).to_broadcast([128, 16, 128]), ALU.is_equal,
                               ['cst', 'selT'], [('OB', k)])
                            for q4 in range(4):
                                bank = 5 + ((tc * 4 + q4) % 3)
                                for r4 in range(4):
                                    tl = q4 * 4 + r4
                                    mm(ps[bank][:, r4 * 128:(r4 + 1) * 128], OA[k][:, tl, :], OB[k][:, tl, :], True, True,
                                       [('OA', k), ('OB', k)], [PS(bank)])
                                a0 = t * 128 + tok0 + q4 * 4
                                cp('act' if q4 % 2 == 0 else 'dve', Wbig[:, :, a0:a0 + 4].rearrange("p j t -> p t j"),
                                   ps[bank][:, 0:512].rearrange("p (t j) -> p t j", t=4), [], [PS(bank), 'Wbig'])
                P.barrier(new_epoch=False)
                with contextlib.ExitStack() as F2:
                    gname = "h%d_" % grp
                    ub = [sbuf(F2, gname + "ub%d" % i, [128, 16, 128], BF16) for i in range(2 * JB)]
                    vb = [sbuf(F2, gname + "vb%d" % i, [128, 2048], BF16) for i in range(2 * JB)]
                    ga = [sbuf(F2, gname + "ga%d" % i, [128, TG], F32) for i in range(2)]
                    aTj = [sbuf(F2, gname + "aTj%d" % i, [128, TG], BF16) for i in range(2 * JB)]
                    accset = 0
                    for jb in range(128 // JB):
                        for jj in range(JB):
                            j = jb * JB + jj
                            sl = (jb % 2) * JB + jj
                            P.dma('sp', ub[sl][:], ubf[j].rearrange("p (c i) -> p c i", c=16), writes=[('ub', sl)])
                            P.dma('sp', vb[sl][:], vbf[j], writes=[('vb', sl)])
                            bank = j % 2
                            for c in range(16):
                                mm(ps[bank][:, 0:TG], ub[sl][:, c, :], x2T[:, c, :], c == 0, c == 15, [('ub', sl), 'x2T'], [PS(bank)])
                            act(ga[j % 2][:], ps[bank][:, 0:TG], AF.Gelu, [], [PS(bank), ('ga', j % 2)])
                            tt('dve', aTj[sl][:], ga[j % 2][:], Wbig[:, j, :], ALU.mult, [('ga', j % 2), 'Wbig'], [('aTj', sl)])
                        for t in range(NSUB):
                            for half in range(2):
                                b0 = 2 + (accset % 2) * 2
                                accset += 1
                                for jj in range(JB):
                                    sl = (jb % 2) * JB + jj
                                    for cbk in range(2):
                                        col0 = (half * 2 + cbk) * 512
                                        mm(ps[b0 + cbk][:, 0:512], aTj[sl][:, t * 128:(t + 1) * 128], vb[sl][:, col0:col0 + 512],
                                           jj == 0, jj == JB - 1, [('aTj', sl), ('vb', sl)], [PS(b0 + cbk)])
                                for cbk in range(2):
                                    col0 = (half * 2 + cbk) * 512
                                    if jb == 0:
                                        cp('dve', acc_sb[:, t, col0:col0 + 512], ps[b0 + cbk][:, 0:512], [], [PS(b0 + cbk), ('acc', t)])
                                    else:
                                        tt('dve', acc_sb[:, t, col0:col0 + 512], acc_sb[:, t, col0:col0 + 512], ps[b0 + cbk][:, 0:512], ALU.add,
                                           [], [PS(b0 + cbk), ('acc', t)])
                P.barrier(new_epoch=False)
                with contextlib.ExitStack() as F3:
                    gname = "k%d_" % grp
                    rF = sbuf(F3, gname + "rF", [128, NSUB, 2048], F32)
                    lg = sbuf(F3, gname + "lg", [128, 2048], F32)
                    lb = sbuf(F3, gname + "lb", [128, 2048], F32)
                    st6f = sbuf(F3, gname + "st6f", [128, 4, 6], F32)
                    lmvf = sbuf(F3, gname + "lmvf", [128, 8], F32)
                    epsf = sbuf(F3, gname + "epsf", [128, 1], F32)
                    memset('pool', epsf[:], LN_EPS, ['epsf'])
                    P.dma('sp', lg[:], lnp[4:5, :].partition_broadcast(128), writes=['lg'])
                    P.dma('sp', lb[:], lnp[5:6, :].partition_broadcast(128), writes=['lb'])
                    for i, tile in enumerate(tiles):
                        P.dma('sp', rF[:, i, :], x2_d[tile * 128:(tile + 1) * 128, :], writes=[('rF', i)])
                        stt(rF[:, i, :], rF[:, i, :], ALPHA, acc_sb[:, i, :], ALU.mult, ALU.add, [('acc', i)], [('rF', i)])
                        for q in range(4):
                            op_bnstats(st6f[:, q, :], rF[:, i, q * 512:(q + 1) * 512], [('rF', i)], ['st6f'])
                        op_bnaggr(lmvf[:, 0:2], st6f[:].rearrange("p a b -> p (a b)"), ['st6f'], ['lmvf'])
                        act(lmvf[:, 2:3], lmvf[:, 1:2], AF.Sqrt, ['lmvf', 'epsf'], ['lmvf'], bias=epsf[:, 0:1], scale=1.0)
                        recip(lmvf[:, 3:4], lmvf[:, 2:3], ['lmvf'], ['lmvf'])
                        ts('dve', rF[:, i, :], rF[:, i, :], lmvf[:, 0:1], lmvf[:, 3:4], ALU.subtract, ALU.mult, ['lmvf'], [('rF', i)])
                        tt('pool', rF[:, i, :], rF[:, i, :], lg[:], ALU.mult, ['lg'], [('rF', i)])
                        tt('pool', rF[:, i, :], rF[:, i, :], lb[:], ALU.add, ['lb'], [('rF', i)])
                        final.append(P.dma('sp', out_d[tile * 128:(tile + 1) * 128, :], rF[:, i, :], reads=[('rF', i)],
                                           writes=[('out', tile)], semkey=('rF_st', i)))
                P.barrier(new_epoch=False)
        P.emit(final_tokens=final)
        return nc
    return nc


def _consts():
    c = np.zeros((128, 1024), np.float32)
    idx = np.arange(128)
    c[:, 0:128] = np.eye(128, dtype=np.float32)
    c[:, 128:256] = (idx[:, None] <= idx[None, :]).astype(np.float32)
    c[:, 256:384] = (idx[:, None] >= idx[None, :]).astype(np.float32)
    c[:, 384:512] = np.where(idx[:, None] <= idx[None, :], 0.0, NEG)
    c[:, 512:640] = np.where(idx[:, None] >= idx[None, :], 0.0, NEG)
    c[:, 640:768] = 1.0
    c[:, 768:896] = idx[None, :].astype(np.float32)
    return c


def _rope_table(pos):
    pos = np.asarray(pos)
    row = (pos // 64).astype(np.float32)
    col = (pos % 64).astype(np.float32)
    n_freq = 32
    inv_freq = (np.float32(10000.0) ** (-np.arange(n_freq, dtype=np.float32) / np.float32(n_freq))).astype(np.float32)
    ang_r = (row[:, None] * inv_freq).astype(np.float32)
    ang_c = (col[:, None] * inv_freq).astype(np.float32)
    cr, sr, cc, sc = np.cos(ang_r), np.sin(ang_r), np.cos(ang_c), np.sin(ang_c)
    tab = np.concatenate([cr, cr, cc, cc, -sr, sr, -sc, sc], axis=1).astype(np.float32)
    return np.ascontiguousarray(tab)


def prep_inputs(inp, cores=range(8)):
    f = lambda a: np.ascontiguousarray(np.asarray(a, dtype=np.float32))
    x = f(inp['x'])
    mem = f(inp['mem'])
    w_in = f(inp['w_in'])[0]
    w_in_sw = w_in.copy()
    w_in_sw[:, 4608:4612] = w_in[:, 4612:4616]
    w_in_sw[:, 4612:4616] = w_in[:, 4608:4612]
    w_in_sw[:, 4616:4620] = w_in[:, 4620:4624]
    w_in_sw[:, 4620:4624] = w_in[:, 4616:4620]
    bi = f(inp['b_igate'])[0]
    bf = f(inp['b_fgate'])[0]
    bg0 = np.concatenate([bi.reshape(8), bf.reshape(8)])[None, :]
    bg1 = np.concatenate([bi[::-1].reshape(8), bf[::-1].reshape(8)])[None, :]
    gqk = np.concatenate([f(inp['att_q_norm'])[0], f(inp['att_k_norm'])[0]])[None, :]
    mln = f(inp['ml_norm'])
    lnp = np.stack([f(inp['ln1_g'])[0], f(inp['ln1_b'])[0], f(inp['ln2_g'])[0], f(inp['ln2_b'])[0],
                    f(inp['ln3_g'])[0], f(inp['ln3_b'])[0]])
    sk = f(inp['peer_subkeys'])[0]
    skT = np.ascontiguousarray(sk.transpose(3, 0, 1, 2).reshape(128, 16 * 128))
    u = f(inp['peer_u'])[0]
    v = f(inp['peer_v'])[0]
    uh = np.ascontiguousarray(u.reshape(128, 128, 16, 128).transpose(1, 3, 2, 0)).reshape(128, 128, 2048)
    vh = np.ascontiguousarray(v.reshape(128, 128, 2048).transpose(1, 0, 2))
    cst = _consts()
    common = dict(cst=cst, gqk=f(gqk), mln=mln, w_out=f(inp['w_out'])[0], xa_wq=f(inp['xa_wq'])[0],
                  xa_wk=f(inp['xa_wk'])[0], xa_wv=f(inp['xa_wv'])[0], xa_wo=f(inp['xa_wo'])[0], lnp=f(lnp),
                  pwq=f(inp['peer_wq'])[0], skT=skT, uh=uh, vh=vh)
    rope0 = _rope_table(np.arange(4096))
    rope1 = _rope_table(4095 - np.arange(4096))
    maps = []
    for c in cores:
        b, half = c // 2, c % 2
        m = dict(common)
        if half == 0:
            m['xl'] = np.ascontiguousarray(x[b])
            m['rope'] = rope0
            m['w_in'] = w_in
            m['bg'] = f(bg0)
        else:
            m['xl'] = np.ascontiguousarray(x[b][::-1])
            m['rope'] = rope1
            m['w_in'] = w_in_sw
            m['bg'] = f(bg1)
        m['memb'] = np.ascontiguousarray(mem[b])
        maps.append(m)
    return maps


def assemble(results, cores=range(8)):
    out = np.zeros((4, 4096, 2048), np.float32)
    for r, c in zip(results, cores):
        b, half = c // 2, c % 2
        o = np.asarray(r["out"], dtype=np.float32)
        if half == 0:
            out[b, 0:2048] = o
        else:
            out[b, 2048:4096] = o[::-1]
    return out


_NC_CACHE = {}


def kernel(**inputs):
    if 'nc' not in _NC_CACHE:
        _NC_CACHE['nc'] = build_program('all')
    nc = _NC_CACHE['nc']
    maps = prep_inputs(inputs)
    res = run_bass_kernel_spmd(nc, maps, core_ids=list(range(8)))
    return assemble(res.results)
```

```python
import contextlib
import math
import numpy as np
import concourse.bass as bass
import concourse.mybir as mybir
from concourse.bass_utils import run_bass_kernel_spmd

F32 = mybir.dt.float32
BF16 = mybir.dt.bfloat16
U32 = mybir.dt.uint32
AF = mybir.ActivationFunctionType
ALU = mybir.AluOpType
AX = mybir.AxisListType

ALPHA = 2.0 ** 0.25
LN_EPS = 1e-5
RMS_EPS = 1e-6
NEG = -1.0e4


class Prog:
    ENGS = ('pe', 'act', 'dve', 'pool', 'sp')

    def __init__(self, nc):
        self.nc = nc
        self.ops = {e: [] for e in self.ENGS}
        self.res = {}
        self.dma_sem_count = {}
        self.epoch = 0

    def _deps(self, eng, reads, writes, is_dma):
        deps = []
        for r in reads:
            st = self.res.get(r)
            if st:
                deps.extend(st['w'])
        for w in writes:
            st = self.res.get(w)
            if st:
                for t in st['w']:
                    if not (is_dma and t[0] == 'dma'):
                        deps.append(t)
                deps.extend(st['r'])
        out = []
        for d in deps:
            if d[0] == 'eng' and d[1] == eng and eng == 'pe':
                continue
            if d not in out:
                out.append(d)
        return out

    def _commit(self, tok, reads, writes):
        for r in reads:
            st = self.res.setdefault(r, {'w': [], 'r': []})
            st['r'].append(tok)
        for w in writes:
            st = self.res.setdefault(w, {'w': [], 'r': []})
            if tok[0] == 'dma' and st['w'] and all(t[0] == 'dma' for t in st['w']) and not st['r']:
                st['w'] = [t for t in st['w'] if t[1] != tok[1]] + [tok]
            else:
                st['w'] = [tok]
            st['r'] = []

    def op(self, eng, fn, reads=(), writes=()):
        deps = self._deps(eng, reads, writes, False)
        tok = ('eng', eng, len(self.ops[eng]))
        self.ops[eng].append({'fn': fn, 'deps': deps, 'needed': False, 'dma': None, 'ep': self.epoch})
        self._commit(tok, reads, writes)
        return tok

    def dma(self, eng, out, in_, reads=(), writes=(), semkey=None):
        deps = self._deps(eng, reads, writes, True)
        sk = semkey if semkey is not None else writes[0]
        self.dma_sem_count[sk] = self.dma_sem_count.get(sk, 0) + 16
        tok = ('dma', sk, self.dma_sem_count[sk])
        self.ops[eng].append({'fn': (lambda e: e.dma_start(out=out, in_=in_)), 'deps': deps,
                              'needed': True, 'dma': sk, 'ep': self.epoch})
        self._commit(tok, reads, writes)
        return tok

    def all_tokens(self):
        toks = []
        for e in self.ENGS:
            for i in range(len(self.ops[e]) - 1, -1, -1):
                o = self.ops[e][i]
                if o['fn'] is not None and o['dma'] is None:
                    toks.append(('eng', e, i))
                    break
        for sk, v in self.dma_sem_count.items():
            toks.append(('dma', sk, v))
        return toks

    def barrier(self, new_epoch=True):
        toks = self.all_tokens()
        for e in self.ENGS:
            self.ops[e].append({'fn': None, 'deps': [t for t in toks if not (t[0] == 'eng' and t[1] == e)],
                                'needed': False, 'dma': None, 'ep': self.epoch})
        self.res = {}
        if new_epoch:
            self.epoch += 1

    def emit(self, final_tokens=()):
        nc = self.nc
        for e in self.ENGS:
            for o in self.ops[e]:
                for d in o['deps']:
                    if d[0] == 'eng':
                        self.ops[d[1]][d[2]]['needed'] = True
        self.maxval = {}
        for e in self.ENGS:
            c = {}
            for o in self.ops[e]:
                if o['dma'] is None and o['needed']:
                    c[o['ep']] = c.get(o['ep'], 0) + 1
                    o['val'] = c[o['ep']]
            self.maxval[e] = dict(c)
        with contextlib.ExitStack() as st:
            esem = {(e, ep): st.enter_context(nc.semaphore('es_%s_%d' % (e, ep)))
                    for e in self.ENGS if e != 'sp' for ep in range(self.epoch + 1)}
            dsem = {}
            for i, k in enumerate(self.dma_sem_count):
                dsem[k] = st.enter_context(nc.semaphore('ds_%d' % i))
            block = st.enter_context(nc.Block())

            def tokval(d):
                if d[0] == 'eng':
                    o = self.ops[d[1]][d[2]]
                    return ('e', d[1], o['ep']), esem[(d[1], o['ep'])], o['val']
                return ('d', d[1]), dsem[d[1]], d[2]

            def body(ename, extra_final=()):
                def f(eng):
                    seen = {}

                    def waits(deps):
                        best = {}
                        for d in deps:
                            k, s, v = tokval(d)
                            if v > best.get(k, (None, 0))[1]:
                                best[k] = (s, v)
                        for k, (s, v) in best.items():
                            if seen.get(k, 0) >= v:
                                continue
                            seen[k] = v
                            eng.wait_ge(s, v)
                    for o in self.ops[ename]:
                        waits(o['deps'])
                        if o['fn'] is None:
                            continue
                        ins = o['fn'](eng)
                        if o['dma'] is not None:
                            ins.then_inc(dsem[o['dma']], 16)
                        elif o['needed']:
                            ins.then_inc(esem[(ename, o['ep'])], 1)
                    waits(extra_final)
                return f
            block.tensor(body('pe'))
            block.scalar(body('act'))
            block.vector(body('dve'))
            block.gpsimd(body('pool'))
            block.sync(body('sp', tuple(final_tokens)))


def build_program(stage='all'):
    nc = bass.Bass("TRN2", target_bir_lowering=False)

    def din(n, s, d=F32):
        return nc.dram_tensor(n, s, d, kind="ExternalInput").ap()

    def dscr(n, s, d):
        return nc.dram_tensor(n, s, d, kind="Internal").ap()

    xl = din("xl", [4096, 2048])
    rope = din("rope", [4096, 256])
    w_in = din("w_in", [2048, 4624])
    bg = din("bg", [1, 16])
    gqk = din("gqk", [1, 256])
    mln_d = din("mln", [1, 1024])
    memb = din("memb", [256, 2048])
    cst_d = din("cst", [128, 1024])
    w_out = din("w_out", [2048, 2048])
    xa_wq = din("xa_wq", [2048, 2048])
    xa_wk = din("xa_wk", [2048, 2048])
    xa_wv = din("xa_wv", [2048, 2048])
    xa_wo = din("xa_wo", [2048, 2048])
    lnp = din("lnp", [6, 2048])
    pwq = din("pwq", [2048, 2048])
    skT = din("skT", [128, 16 * 128])
    uh = din("uh", [128, 128, 2048])
    vh = din("vh", [128, 128, 2048])

    hb_d = dscr("hb_s", [2048, 1024], F32)
    cat_d = dscr("cat_s", [2048, 2048], BF16)
    x1_d = dscr("x1_s", [2048, 2048], F32)
    x2_d = dscr("x2_s", [2048, 2048], F32)
    wd_d = dscr("wd_s", [16, 128, 128 * 128], BF16)

    if stage == 'mix':
        out_d = nc.dram_tensor("out", [2048, 2048], BF16, kind="ExternalOutput").ap()
    else:
        out_d = nc.dram_tensor("out", [2048, 2048], F32, kind="ExternalOutput").ap()

    P = Prog(nc)
    final = []

    with contextlib.ExitStack() as G:
        def sbuf(st, n, s, d):
            return st.enter_context(nc.sbuf_tensor('sb_' + n, s, d))

        ps = [G.enter_context(nc.psum_tensor("ps%d" % i, [128, 512], F32)) for i in range(8)]

        def PS(i):
            return ('ps', i)

        cst = sbuf(G, "cst", [128, 1024], F32)
        ident_b = sbuf(G, "ident_b", [128, 128], BF16)
        ones_b = sbuf(G, "ones_b", [128, 128], BF16)
        ident_f = cst[:, 0:128]
        Umask = cst[:, 128:256]
        Lmask = cst[:, 256:384]
        NEGU = cst[:, 384:512]
        NEGL = cst[:, 512:640]
        ones_f = cst[:, 640:768]
        iota_f = cst[:, 768:896]

        P.dma('sp', cst[:], cst_d, writes=['cst'])
        P.op('dve', lambda e: e.tensor_copy(out=ident_b[:], in_=ident_f), reads=['cst'], writes=['ident_b'])
        P.op('dve', lambda e: e.tensor_copy(out=ones_b[:], in_=ones_f), reads=['cst'], writes=['ones_b'])

        def mm(out, lhsT, rhs, start, stop, reads, writes):
            return P.op('pe', lambda e: e.matmul(out, lhsT, rhs, start=start, stop=stop), reads=reads, writes=writes)

        def tr(out, in_, reads, writes, fp32=False):
            idn = ident_f if fp32 else ident_b[:]
            return P.op('pe', lambda e: e.transpose(out=out, in_=in_, identity=idn),
                        reads=list(reads) + (['cst'] if fp32 else ['ident_b']), writes=writes)

        def act(out, in_, func, reads, writes, bias=None, scale=None):
            kw = {}
            if bias is not None:
                kw['bias'] = bias
            if scale is not None:
                kw['scale'] = scale
            return P.op('act', lambda e: e.activation(out=out, in_=in_, func=func, **kw), reads=reads, writes=writes)

        def tt(eng, out, in0, in1, op, reads, writes):
            return P.op(eng, lambda e: e.tensor_tensor(out=out, in0=in0, in1=in1, op=op), reads=reads, writes=writes)

        def ts(eng, out, in0, s1, s2, op0, op1, reads, writes):
            return P.op(eng, lambda e: e.tensor_scalar(out=out, in0=in0, scalar1=s1, scalar2=s2, op0=op0, op1=op1),
                        reads=reads, writes=writes)

        def stt(out, in0, scalar, in1, op0, op1, reads, writes):
            return P.op('dve', lambda e: e.scalar_tensor_tensor(out=out, in0=in0, scalar=scalar, in1=in1, op0=op0, op1=op1),
                        reads=reads, writes=writes)

        def cp(eng, out, in_, reads, writes):
            if eng == 'act':
                return P.op('act', lambda e: e.activation(out=out, in_=in_, func=AF.Copy), reads=reads, writes=writes)
            return P.op(eng, lambda e: e.tensor_copy(out=out, in_=in_), reads=reads, writes=writes)

        def red(out, in_, reads, writes, op=ALU.add):
            return P.op('dve', lambda e: e.tensor_reduce(out=out, in_=in_, axis=AX.X, op=op), reads=reads, writes=writes)

        def recip(out, in_, reads, writes):
            return P.op('dve', lambda e: e.reciprocal(out=out, in_=in_), reads=reads, writes=writes)

        def memset(eng, ap, val, writes):
            return P.op(eng, lambda e: e.memset(ap, val), writes=writes)

        def op_max(out, in_, reads, writes):
            return P.op('dve', lambda e: e.max(out=out, in_=in_), reads=reads, writes=writes)

        def op_maxidx(out, mx, vals_, reads, writes):
            return P.op('dve', lambda e: e.max_index(out=out, in_max=mx, in_values=vals_), reads=reads, writes=writes)

        def op_mrep(out, rep, vals_, reads, writes):
            return P.op('dve', lambda e: e.match_replace(out=out, in_to_replace=rep, in_values=vals_, imm_value=-1.0e30),
                        reads=reads, writes=writes)

        def op_tss(out, in_, scalar, op, reads, writes):
            return P.op('dve', lambda e: e.tensor_single_scalar(out=out, in_=in_, scalar=scalar, op=op), reads=reads, writes=writes)

        def op_bnstats(out, in_, reads, writes):
            return P.op('dve', lambda e: e.bn_stats(out=out, in_=in_), reads=reads, writes=writes)

        def op_bnaggr(out, in_, reads, writes):
            return P.op('dve', lambda e: e.bn_aggr(out=out, in_=in_), reads=reads, writes=writes)

        rr_state = {'tb': 0, 'ev': 0, 'uv': 0}

        def transpose16(src, src_res, dstT, dst_res, col0):
            for half in range(2):
                bank = 6 + (rr_state['tb'] % 2)
                rr_state['tb'] += 1
                psb = ps[bank][:].bitcast(BF16)
                for k in range(8):
                    c = half * 8 + k
                    tr(psb[:, k * 128:(k + 1) * 128], src[:, c * 128:(c + 1) * 128], [src_res], [PS(bank)])
                eng = 'act' if rr_state['ev'] % 2 == 0 else 'dve'
                rr_state['ev'] += 1
                cp(eng, dstT[:, half * 8:(half + 1) * 8, col0:col0 + 128],
                   psb.rearrange("p (k t) -> p k t", k=8), [], [PS(bank), dst_res])

        with contextlib.ExitStack() as M:
            KT = sbuf(M, "KT", [128, 2, 4096], BF16)
            VA = sbuf(M, "VA", [128, 32, 2, 130], BF16)
            xb = [sbuf(M, "xb%d" % i, [128, 2048], BF16) for i in range(2)]
            xT = sbuf(M, "xT", [128, 16, 512], BF16)
            wb = [sbuf(M, "wb%d" % i, [128, 16, 512], BF16) for i in range(2)]
            wg = sbuf(M, "wg", [128, 16, 16], BF16)
            ropeT = [sbuf(M, "ropeT%d" % i, [128, 256], F32) for i in range(2)]
            gqk_t = sbuf(M, "gqk_t", [128, 256], F32)
            bgt = sbuf(M, "bgt", [128, 16], F32)
            mln_t = sbuf(M, "mln_t", [128, 1024], F32)
            ig = sbuf(M, "ig", [128, 4, 8], F32)
            lf = sbuf(M, "lf", [128, 4, 8], F32)
            t0 = sbuf(M, "t0", [128, 1024], F32)
            t1 = sbuf(M, "t1", [128, 1024], F32)
            qkb = sbuf(M, "qkb", [128, 512], BF16)
            sm = sbuf(M, "sm", [128, 64], F32)
            QT = sbuf(M, "QT", [128, 8, 512], BF16)
            mqT = sbuf(M, "mqT", [128, 4, 512], BF16)
            mkT = sbuf(M, "mkT", [128, 4, 512], BF16)
            mk_tok = sbuf(M, "mk_tok", [128, 4, 512], BF16)
            mv_aug = sbuf(M, "mv_aug", [128, 4, 4, 258], BF16)
            mo_sig = sbuf(M, "mo_sig", [128, 4, 1024], BF16)
            att_g = sbuf(M, "att_g", [128, 4, 1024], BF16)
            mlb = [sbuf(M, "mlb%d" % i, [128, 1024], BF16) for i in range(2)]
            PTb = [sbuf(M, "PTb%d" % i, [128, 512], BF16) for i in range(3)]
            Fm = [sbuf(M, "Fm%d" % i, [128, 128], F32) for i in range(2)]
            DT = [sbuf(M, "DT%d" % i, [128, 128], F32) for i in range(2)]
            EB = [sbuf(M, "EB%d" % i, [128, 128], F32) for i in range(2)]
            PTm = [sbuf(M, "PTm%d" % i, [128, 128], BF16) for i in range(2)]
            qsT = [sbuf(M, "qsT%d" % i, [128, 128], BF16) for i in range(2)]
            kw = [sbuf(M, "kw%d" % i, [128, 128], BF16) for i in range(2)]
            mc = [sbuf(M, "mc%d" % i, [128, 8], F32) for i in range(2)]
            Cf = [sbuf(M, "Cf%d" % i, [128, 4, 257], F32) for i in range(2)]
            Cb = [sbuf(M, "Cb%d" % i, [128, 4, 258], BF16) for i in range(2)]
            hout = sbuf(M, "hout", [128, 1024], F32)
            hbl = sbuf(M, "hbl", [128, 1024], F32)

            P.dma('sp', gqk_t[:], gqk.partition_broadcast(128), writes=['gqk_t'])
            P.dma('sp', bgt[:], bg.partition_broadcast(128), writes=['bgt'])
            P.dma('sp', mln_t[:], mln_d.partition_broadcast(128), writes=['mln_t'])
            w_in_v = w_in.rearrange("(c p) n -> p c n", p=128)
            P.dma('pool', wg[:], w_in_v[:, :, 4608:4624], writes=['wg'])
            memset('pool', VA[:], 1.0, ['VA'])
            memset('pool', mv_aug[:], 1.0, ['mv_aug'])
            for d in range(2):
                memset('pool', Cf[d][:], 0.0, [('Cf', d)])
                memset('pool', Cb[d][:], 0.0, [('Cb', d)])

            st = {'xslot': 0, 'wslot': 0, 'rslot': 0, 'bank': 0, 'mi': 0}

            def load_w(col0, ncols):
                slot = st['wslot'] % 2
                st['wslot'] += 1
                for q in range(4):
                    P.dma('pool', wb[slot][:, q * 4:(q + 1) * 4, 0:ncols], w_in_v[:, q * 4:(q + 1) * 4, col0:col0 + ncols],
                          writes=[('wb', slot)])
                return slot

            def nbank():
                b = st['bank'] % 6
                st['bank'] += 1
                return b

            def load_group(tiles):
                for i, tile in enumerate(tiles):
                    slot = st['xslot'] % 2
                    st['xslot'] += 1
                    P.dma('pool', xb[slot][:], xl[tile * 128:(tile + 1) * 128, :], writes=[('xb', slot)])
                    transpose16(xb[slot], ('xb', slot), xT, 'xT', i * 128)

            def proj_tok(i, wslot, c0, ncols, bank):
                for c in range(16):
                    mm(ps[bank][:, 0:ncols], xT[:, c, i * 128:(i + 1) * 128], wb[wslot][:, c, c0:c0 + ncols],
                       c == 0, c == 15, ['xT', ('wb', wslot)], [PS(bank)])

            def proj_feat(j, wslot, bank):
                for c in range(16):
                    mm(ps[bank][:, 0:512], wb[wslot][:, c, j * 128:(j + 1) * 128], xT[:, c, :],
                       c == 0, c == 15, ['xT', ('wb', wslot)], [PS(bank)])

            def norm_rope(psap, bank, H, gain, rslot, outb):
                W = H * 128
                cp('act', t0[:, 0:W], psap, [], [PS(bank), 't0'])
                tt('dve', t1[:, 0:W], t0[:, 0:W], t0[:, 0:W], ALU.mult, ['t0'], ['t1'])
                red(sm[:, 0:H], t1[:, 0:W].rearrange("p (h d) -> p h d", h=H), ['t1'], ['sm'])
                act(sm[:, 8:8 + H], sm[:, 0:H], AF.Sqrt, ['sm'], ['sm'], bias=eps_rms[:, 0:1], scale=1.0 / 128.0)
                recip(sm[:, 16:16 + H], sm[:, 8:8 + H], ['sm'], ['sm'])
                t0v = t0[:, 0:W].rearrange("p (h d) -> p h d", h=H)
                t1v = t1[:, 0:W].rearrange("p (h d) -> p h d", h=H)
                tt('dve', t1v, t0v, sm[:, 16:16 + H].unsqueeze(2).to_broadcast([128, H, 128]), ALU.mult,
                   ['t0', 'sm'], ['t1'])
                tt('dve', t1v, t1v, gain.unsqueeze(1).to_broadcast([128, H, 128]), ALU.mult, ['t1', 'gqk_t'], ['t1'])
                rt = ropeT[rslot]
                tt('dve', t0v, t1v, rt[:, 0:128].unsqueeze(1).to_broadcast([128, H, 128]), ALU.mult,
                   ['t1', ('rope', rslot)], ['t0'])
                t1z = t1[:, 0:W].rearrange("p (h a z d) -> p h a z d", h=H, a=2, z=2)
                sz = rt[:, 128:256].rearrange("p (a z d) -> p a z d", a=2, z=2)
                q2 = qk2[:, 0:W].rearrange("p (h a z d) -> p h a z d", h=H, a=2, z=2)
                for z in range(2):
                    tt('dve', q2[:, :, :, z, :], t1z[:, :, :, 1 - z, :],
                       sz[:, :, z, :].unsqueeze(1).to_broadcast([128, H, 2, 32]), ALU.mult,
                       ['t1', ('rope', rslot)], ['qk2'])
                tt('dve', outb, t0[:, 0:W], qk2[:, 0:W], ALU.add, ['t0', 'qk2'], ['qkb'])

            qk2 = sbuf(M, "qk2", [128, 512], F32)
            eps_rms = sbuf(M, "eps_rms", [128, 2], F32)
            memset('pool', eps_rms[:, 0:1], RMS_EPS, ['eps_rms'])
            memset('pool', eps_rms[:, 1:2], 1.0, ['eps_rms'])

            def load_rope(tile):
                slot = st['rslot'] % 2
                st['rslot'] += 1
                P.dma('sp', ropeT[slot][:], rope[tile * 128:(tile + 1) * 128, :], writes=[('rope', slot)])
                return slot

            def do_kv(tiles, wslot):
                for i, tile in enumerate(tiles):
                    bank = nbank()
                    proj_tok(i, wslot, 0, 512, bank)
                    rs = load_rope(tile)
                    cp('act', VA[:, tile, :, 0:128], ps[bank][:, 256:512].rearrange("p (g d) -> p g d", g=2),
                       [], [PS(bank), 'VA'])
                    norm_rope(ps[bank][:, 0:256], bank, 2, gqk_t[:, 128:256], rs, qkb[:, 0:256])
                    tb = 6 + (rr_state['tb'] % 2)
                    rr_state['tb'] += 1
                    psb = ps[tb][:].bitcast(BF16)
                    for g in range(2):
                        tr(psb[:, g * 128:(g + 1) * 128], qkb[:, g * 128:(g + 1) * 128], ['qkb'], [PS(tb)])
                    cp('act', KT[:, :, tile * 128:(tile + 1) * 128], psb[:, 0:256].rearrange("p (g t) -> p g t", g=2),
                       [], [PS(tb), 'KT'])

            def do_gates(n):
                for i in range(n):
                    bank = nbank()
                    for c in range(16):
                        mm(ps[bank][:, 0:16], xT[:, c, i * 128:(i + 1) * 128], wg[:, c, :], c == 0, c == 15,
                           ['xT', 'wg'], [PS(bank)])
                    stt(ig[:, i, :], ps[bank][:, 0:8], -0.5 * math.log(128.0), bgt[:, 0:8], ALU.add, ALU.add,
                        ['bgt'], [PS(bank), 'ig'])
                    tt('dve', sm[:, 32:40], ps[bank][:, 8:16], bgt[:, 8:16], ALU.add, ['bgt'], [PS(bank), 'sm'])
                    act(sm[:, 40:48], sm[:, 32:40], AF.Exp, ['sm'], ['sm'], scale=-1.0)
                    act(sm[:, 48:56], sm[:, 40:48], AF.Ln, ['sm'], ['sm'], bias=eps_rms[:, 1:2], scale=1.0)
                    ts('dve', lf[:, i, :], sm[:, 48:56], -1.0, 0.0, ALU.mult, ALU.add, ['sm'], ['lf'])

            def do_mk_tok(n, wslot):
                for i in range(n):
                    bank = nbank()
                    proj_tok(i, wslot, 0, 512, bank)
                    cp('act' if i % 2 == 0 else 'dve', mk_tok[:, i, :], ps[bank][:, 0:512], [], [PS(bank), 'mk_tok'])

            def do_mv(n, wslot, half):
                for i in range(n):
                    bank = nbank()
                    proj_tok(i, wslot, 0, 512, bank)
                    cp('dve' if i % 2 == 0 else 'act', mv_aug[:, i, half * 2:half * 2 + 2, 0:256],
                       ps[bank][:, 0:512].rearrange("p (h d) -> p h d", h=2), [], [PS(bank), 'mv_aug'])

            def do_feat(dst, dst_res, wslot):
                for j in range(4):
                    bank = nbank()
                    proj_feat(j, wslot, bank)
                    cp('act' if j % 2 == 0 else 'dve', dst[:, j, :], ps[bank][:, 0:512], [], [PS(bank), dst_res])

            def mlstm_tile(i, d, with_out, add_hbl):
                MASK = Umask if d == 0 else Lmask
                NEGM = NEGU if d == 0 else NEGL
                last = 127 if d == 0 else 0
                for h in range(4):
                    j = d * 4 + h
                    k = st['mi'] % 2
                    st['mi'] += 1
                    bX, bY, bZ = (0, 1, 2) if k == 0 else (3, 4, 5)
                    fcol = lf[:, i, j:j + 1]
                    icol = ig[:, i, j:j + 1]
                    ts('dve', Fm[k][:], MASK, fcol, 0.0, ALU.mult, ALU.add, ['cst', 'lf'], [('Fm', k)])
                    mm(ps[bX][:, 0:128], ones_f, Fm[k][:], True, True, ['cst', ('Fm', k)], [PS(bX)])
                    if with_out:
                        mm(ps[bX][:, 128:256], ones_f, Fm[k][:], True, False, ['cst', ('Fm', k)], [PS(bX)])
                        mm(ps[bX][:, 128:256], ident_f, NEGM, False, True, ['cst'], [PS(bX)])
                    mm(ps[bX][:, 256:257], MASK, fcol, True, True, ['cst', 'lf'], [PS(bX)])
                    tt('dve', mc[k][:, 0:1], icol, ps[bX][:, 256:257], ALU.subtract, ['ig'], [PS(bX), ('mc', k)])
                    act(mc[k][:, 1:2], ps[bX][:, last:last + 1], AF.Exp, [('mc', k)], [PS(bX), ('mc', k)], bias=mc[k][:, 0:1], scale=1.0)
                    act(mc[k][:, 2:3], ps[bX][:, last:last + 1], AF.Exp, [('mc', k)], [PS(bX), ('mc', k)])
                    if with_out:
                        act(EB[k][:], ps[bX][:, 0:128], AF.Exp, [], [PS(bX), ('EB', k)])
                        act(DT[k][:], ps[bX][:, 128:256], AF.Exp, [('mc', k)], [PS(bX), ('DT', k)], bias=mc[k][:, 0:1], scale=1.0)
                        mm(ps[bX][:, 384:512], mkT[:, h, i * 128:(i + 1) * 128], mqT[:, h, i * 128:(i + 1) * 128],
                           True, True, ['mkT', 'mqT'], [PS(bX)])
                        tt('dve', PTm[k][:], ps[bX][:, 384:512], DT[k][:], ALU.mult, [('DT', k)], [PS(bX), ('PTm', k)])
                        tt('dve', qsT[k][:], mqT[:, h, i * 128:(i + 1) * 128], EB[k][:], ALU.mult, ['mqT', ('EB', k)], [('qsT', k)])
                        mm(ps[bY][:, 0:257], PTm[k][:], mv_aug[:, i, h, 0:257], True, False, [('PTm', k), 'mv_aug'], [PS(bY)])
                        mm(ps[bY][:, 0:257], qsT[k][:], Cb[d][:, h, 0:257], False, True, [('qsT', k), ('Cb', d)], [PS(bY)])
                        act(mc[k][:, 3:4], ps[bY][:, 256:257], AF.Abs, [('mc', k)], [PS(bY), ('mc', k)])
                        ts('dve', mc[k][:, 4:5], mc[k][:, 3:4], 1.0, 0.0, ALU.max, ALU.add, [('mc', k)], [('mc', k)])
                        recip(mc[k][:, 5:6], mc[k][:, 4:5], [('mc', k)], [('mc', k)])
                        if add_hbl:
                            stt(hout[:, h * 256:(h + 1) * 256], ps[bY][:, 0:256], mc[k][:, 5:6], hbl[:, h * 256:(h + 1) * 256],
                                ALU.mult, ALU.add, [('mc', k), 'hbl'], [PS(bY), 'hout'])
                        else:
                            ts('dve', hout[:, h * 256:(h + 1) * 256], ps[bY][:, 0:256], mc[k][:, 5:6], 0.0, ALU.mult, ALU.add,
                               [('mc', k)], [PS(bY), 'hout'])
                    ts('dve', kw[k][:], mk_tok[:, i, h * 128:(h + 1) * 128], mc[k][:, 1:2], 0.0, ALU.mult, ALU.add,
                       ['mk_tok', ('mc', k)], [('kw', k)])
                    mm(ps[bZ][:, 0:257], kw[k][:], mv_aug[:, i, h, 0:257], True, True, [('kw', k), 'mv_aug'], [PS(bZ)])
                    stt(Cf[d][:, h, :], Cf[d][:, h, :], mc[k][:, 2:3], ps[bZ][:, 0:257], ALU.mult, ALU.add,
                        [('mc', k)], [PS(bZ), ('Cf', d)])
                    cp('act', Cb[d][:, h, 0:257], Cf[d][:, h, :], [('Cf', d)], [('Cb', d)])

            for grp in range(7, 3, -1):
                tiles = [grp * 4 + i for i in range(4)]
                load_group(tiles)
                ws = load_w(1024, 512)
                do_kv(tiles, ws)
                do_gates(4)
                ws = load_w(2048, 512)
                do_mk_tok(4, ws)
                ws = load_w(2560, 512)
                do_mv(4, ws, 0)
                ws = load_w(3072, 512)
                do_mv(4, ws, 1)
                for i in range(3, -1, -1):
                    mlstm_tile(i, 1, False, False)

            for grp in range(3, -1, -1):
                tiles = [grp * 4 + i for i in range(4)]
                load_group(tiles)
                ws = load_w(1024, 512)
                do_kv(tiles, ws)
                do_gates(4)
                ws = load_w(1536, 512)
                do_feat(mqT, 'mqT', ws)
                ws = load_w(2048, 512)
                do_feat(mkT, 'mkT', ws)
                do_mk_tok(4, ws)
                ws = load_w(2560, 512)
                do_mv(4, ws, 0)
                ws = load_w(3072, 512)
                do_mv(4, ws, 1)
                for i in range(3, -1, -1):
                    mlstm_tile(i, 1, True, False)
                    tile = tiles[i]
                    P.dma('sp', hb_d[tile * 128:(tile + 1) * 128, :], hout[:], reads=['hout'], writes=[('hb', tile)],
                          semkey='hout_st')

            for grp in range(4):
                tiles = [grp * 4 + i for i in range(4)]
                load_group(tiles)
                for blk in range(2):
                    ws = load_w(blk * 512, 512)
                    for i, tile in enumerate(tiles):
                        bank = nbank()
                        proj_tok(i, ws, 0, 512, bank)
                        rs = load_rope(tile)
                        norm_rope(ps[bank][:, 0:512], bank, 4, gqk_t[:, 0:128], rs, qkb[:, 0:512])
                        tb = 6 + (rr_state['tb'] % 2)
                        rr_state['tb'] += 1
                        psb = ps[tb][:].bitcast(BF16)
                        for hh in range(4):
                            tr(psb[:, hh * 128:(hh + 1) * 128], qkb[:, hh * 128:(hh + 1) * 128], ['qkb'], [PS(tb)])
                        cp('act', QT[:, blk * 4:(blk + 1) * 4, i * 128:(i + 1) * 128],
                           psb[:, 0:512].rearrange("p (h t) -> p h t", h=4), [], [PS(tb), 'QT'])
                do_gates(4)
                ws = load_w(1536, 512)
                do_feat(mqT, 'mqT', ws)
                ws = load_w(2048, 512)
                do_feat(mkT, 'mkT', ws)
                do_mk_tok(4, ws)
                ws = load_w(2560, 512)
                do_mv(4, ws, 0)
                ws = load_w(3072, 512)
                do_mv(4, ws, 1)
                for blk in range(2):
                    ws = load_w(3584 + blk * 512, 512)
                    for i in range(4):
                        bank = nbank()
                        proj_tok(i, ws, 0, 512, bank)
                        act(mo_sig[:, i, blk * 512:(blk + 1) * 512], ps[bank][:, 0:512], AF.Sigmoid, [], [PS(bank), 'mo_sig'])
                its = [(hq, kt) for hq in range(8) for kt in range(32)]

                def issue_S(n):
                    hq, kt = its[n]
                    g = hq // 4
                    bS = 4 + (n % 2)
                    pt = n % 3
                    mm(ps[bS][:, 0:512], KT[:, g, kt * 128:(kt + 1) * 128], QT[:, hq, :], True, True, ['KT', 'QT'], [PS(bS)])
                    act(PTb[pt][:], ps[bS][:, 0:512], AF.Exp, [], [PS(bS), ('PTb', pt)], scale=128.0 ** -0.5)

                def issue_PV(n):
                    hq, kt = its[n]
                    g = hq // 4
                    pt = n % 3
                    for qs in range(4):
                        mm(ps[qs][:, 0:129], PTb[pt][:, qs * 128:(qs + 1) * 128], VA[:, kt, g, 0:129], kt == 0, kt == 31,
                           [('PTb', pt), 'VA'], [PS(qs)])
                    if kt == 31:
                        for qs in range(4):
                            recip(sm[:, 56 + qs:57 + qs], ps[qs][:, 128:129], [], [PS(qs), 'sm'])
                            ts('dve', att_g[:, qs, hq * 128:(hq + 1) * 128], ps[qs][:, 0:128], sm[:, 56 + qs:57 + qs], 0.0,
                               ALU.mult, ALU.add, ['sm'], [PS(qs), 'att_g'])

                issue_S(0)
                for n in range(len(its)):
                    if n + 1 < len(its):
                        issue_S(n + 1)
                    issue_PV(n)
                for i, tile in enumerate(tiles):
                    P.dma('sp', hbl[:], hb_d[tile * 128:(tile + 1) * 128, :], reads=[('hb', tile)], writes=['hbl'])
                    mlstm_tile(i, 0, True, True)
                    tt('dve', t0[:], hout[:], hout[:], ALU.mult, ['hout'], ['t0'])
                    red(sm[:, 0:4], t0[:].rearrange("p (h d) -> p h d", h=4), ['t0'], ['sm'])
                    act(sm[:, 8:12], sm[:, 0:4], AF.Sqrt, ['sm'], ['sm'], bias=eps_rms[:, 0:1], scale=1.0 / 256.0)
                    recip(sm[:, 16:20], sm[:, 8:12], ['sm'], ['sm'])
                    tt('dve', t0[:].rearrange("p (h d) -> p h d", h=4), hout[:].rearrange("p (h d) -> p h d", h=4),
                       sm[:, 16:20].unsqueeze(2).to_broadcast([128, 4, 256]), ALU.mult, ['hout', 'sm'], ['t0'])
                    tt('pool', t1[:], t0[:], mln_t[:], ALU.mult, ['t0', 'mln_t'], ['t1'])
                    ms = i % 2
                    tt('pool', mlb[ms][:], t1[:], mo_sig[:, i, :], ALU.mult, ['t1', 'mo_sig'], [('mlb', ms)])
                    P.dma('sp', cat_d[tile * 128:(tile + 1) * 128, 1024:2048], mlb[ms][:], reads=[('mlb', ms)],
                          writes=[('cat_ml', tile)], semkey=('mlb_st', ms))
                    P.dma('sp', cat_d[tile * 128:(tile + 1) * 128, 0:1024], att_g[:, i, :], reads=['att_g'],
                          writes=[('cat_att', tile)], semkey='att_st')
            P.barrier()

        if stage == 'mix':
            with contextlib.ExitStack() as Dg:
                cb = sbuf(Dg, "dbg_cb", [128, 2048], BF16)
                for tile in range(16):
                    P.dma('sp', cb[:], cat_d[tile * 128:(tile + 1) * 128, :], reads=[('cat_ml', tile), ('cat_att', tile)], writes=['dbg_cb'])
                    final.append(P.dma('sp', out_d[tile * 128:(tile + 1) * 128, :], cb[:], reads=['dbg_cb'], writes=[('out', tile)],
                                       semkey='dbg_out'))
            P.emit(final_tokens=final[-1:])
            return nc

        with contextlib.ExitStack() as E:
            xbE = [sbuf(E, "xbE%d" % i, [128, 2048], BF16) for i in range(2)]
            aT = sbuf(E, "aT", [128, 16, 512], BF16)
            wbD = [sbuf(E, "wbD%d" % i, [128, 16, 512], BF16) for i in range(2)]
            r_g = sbuf(E, "r_g", [128, 4, 2048], F32)
            ln_g = sbuf(E, "ln_g", [128, 2048], F32)
            ln_b = sbuf(E, "ln_b", [128, 2048], F32)
            st6 = sbuf(E, "st6", [128, 4, 6], F32)
            lmv = sbuf(E, "lmv", [128, 8], F32)
            eps_ln = sbuf(E, "eps_ln", [128, 1], F32)
            memT = sbuf(E, "memT", [128, 16, 256], BF16)
            kmT = sbuf(E, "kmT", [128, 16, 256], BF16)
            vm = sbuf(E, "vm", [128, 2, 2048], BF16)
            q1T = sbuf(E, "q1T", [128, 16, 512], BF16)
            o_g = sbuf(E, "o_g", [128, 4, 2048], BF16)
            PTx = [sbuf(E, "PTx%d" % i, [128, 512], BF16) for i in range(4)]
            rrx = sbuf(E, "rrx", [128, 8], F32)
            memset('pool', eps_ln[:], LN_EPS, ['eps_ln'])
            sE = {'w': 0, 'x': 0, 'bank': 0, 'pt': 0}

            def load_wE(W, col0):
                slot = sE['w'] % 2
                sE['w'] += 1
                Wv = W.rearrange("(c p) n -> p c n", p=128)
                for q in range(4):
                    P.dma('pool', wbD[slot][:, q * 4:(q + 1) * 4, :], Wv[:, q * 4:(q + 1) * 4, col0:col0 + 512],
                          writes=[('wbD', slot)])
                return slot

            def nbE():
                b = sE['bank'] % 6
                sE['bank'] += 1
                return b

            def load_T(src_rows, dst, dst_res, col0, cast, extra_reads=()):
                slot = sE['x'] % 2
                sE['x'] += 1
                P.dma('pool' if cast else 'sp', xbE[slot][:], src_rows, reads=list(extra_reads), writes=[('xbE', slot)])
                transpose16(xbE[slot], ('xbE', slot), dst, dst_res, col0)

            def load_ln(k):
                P.dma('sp', ln_g[:], lnp[2 * k:2 * k + 1, :].partition_broadcast(128), writes=['ln_g'])
                P.dma('sp', ln_b[:], lnp[2 * k + 1:2 * k + 2, :].partition_broadcast(128), writes=['ln_b'])

            def dense_ln(W, tiles, out_d_, out_key):
                for cbk in range(4):
                    ws = load_wE(W, cbk * 512)
                    for i in range(4):
                        bank = nbE()
                        for c in range(16):
                            mm(ps[bank][:, 0:512], aT[:, c, i * 128:(i + 1) * 128], wbD[ws][:, c, :], c == 0, c == 15,
                               ['aT', ('wbD', ws)], [PS(bank)])
                        stt(r_g[:, i, cbk * 512:(cbk + 1) * 512], r_g[:, i, cbk * 512:(cbk + 1) * 512], ALPHA,
                            ps[bank][:, 0:512], ALU.mult, ALU.add, [], [PS(bank), ('r_g', i)])
                for i, tile in enumerate(tiles):
                    for q in range(4):
                        op_bnstats(st6[:, q, :], r_g[:, i, q * 512:(q + 1) * 512], [('r_g', i)], ['st6'])
                    op_bnaggr(lmv[:, 0:2], st6[:].rearrange("p a b -> p (a b)"), ['st6'], ['lmv'])
                    act(lmv[:, 2:3], lmv[:, 1:2], AF.Sqrt, ['lmv', 'eps_ln'], ['lmv'], bias=eps_ln[:, 0:1], scale=1.0)
                    recip(lmv[:, 3:4], lmv[:, 2:3], ['lmv'], ['lmv'])
                    ts('dve', r_g[:, i, :], r_g[:, i, :], lmv[:, 0:1], lmv[:, 3:4], ALU.subtract, ALU.mult, ['lmv'], [('r_g', i)])
                    tt('pool', r_g[:, i, :], r_g[:, i, :], ln_g[:], ALU.mult, ['ln_g'], [('r_g', i)])
                    tt('pool', r_g[:, i, :], r_g[:, i, :], ln_b[:], ALU.add, ['ln_b'], [('r_g', i)])
                    tk = P.dma('sp', out_d_[tile * 128:(tile + 1) * 128, :], r_g[:, i, :], reads=[('r_g', i)],
                               writes=[(out_key, tile)], semkey=('r_g_st', i))
                    if out_key == 'out':
                        final.append(tk)

            load_ln(0)
            for grp in range(4):
                tiles = [grp * 4 + i for i in range(4)]
                for i, tile in enumerate(tiles):
                    P.dma('sp', r_g[:, i, :], xl[tile * 128:(tile + 1) * 128, :], writes=[('r_g', i)])
                    load_T(cat_d[tile * 128:(tile + 1) * 128, :], aT, 'aT', i * 128, False,
                           extra_reads=[('cat_ml', tile), ('cat_att', tile)])
                dense_ln(w_out, tiles, x1_d, 'x1')

            load_ln(1)
            for mt in range(2):
                load_T(memb[mt * 128:(mt + 1) * 128, :], memT, 'memT', mt * 128, True)
            for cbk in range(4):
                ws = load_wE(xa_wk, cbk * 512)
                for j in range(4):
                    bank = nbE()
                    for c in range(16):
                        mm(ps[bank][:, 0:256], wbD[ws][:, c, j * 128:(j + 1) * 128], memT[:, c, :], c == 0, c == 15,
                           ['memT', ('wbD', ws)], [PS(bank)])
                    cp('act' if j % 2 == 0 else 'dve', kmT[:, cbk * 4 + j, :], ps[bank][:, 0:256], [], [PS(bank), 'kmT'])
            for cbk in range(4):
                ws = load_wE(xa_wv, cbk * 512)
                for mt in range(2):
                    bank = nbE()
                    for c in range(16):
                        mm(ps[bank][:, 0:512], memT[:, c, mt * 128:(mt + 1) * 128], wbD[ws][:, c, :], c == 0, c == 15,
                           ['memT', ('wbD', ws)], [PS(bank)])
                    cp('act' if mt % 2 == 0 else 'dve', vm[:, mt, cbk * 512:(cbk + 1) * 512], ps[bank][:, 0:512], [], [PS(bank), 'vm'])
            for grp in range(4):
                tiles = [grp * 4 + i for i in range(4)]
                for i, tile in enumerate(tiles):
                    P.dma('sp', r_g[:, i, :], x1_d[tile * 128:(tile + 1) * 128, :], reads=[('x1', tile)], writes=[('r_g', i)])
                    load_T(x1_d[tile * 128:(tile + 1) * 128, :], aT, 'aT', i * 128, True, extra_reads=[('x1', tile)])
                for cbk in range(4):
                    ws = load_wE(xa_wq, cbk * 512)
                    for j in range(4):
                        bank = nbE()
                        for c in range(16):
                            mm(ps[bank][:, 0:512], wbD[ws][:, c, j * 128:(j + 1) * 128], aT[:, c, :], c == 0, c == 15,
                               ['aT', ('wbD', ws)], [PS(bank)])
                        cp('act' if j % 2 == 0 else 'dve', q1T[:, cbk * 4 + j, :], ps[bank][:, 0:512], [], [PS(bank), 'q1T'])
                for h in range(4):
                    pts = []
                    for mt in range(2):
                        bank = nbE()
                        for dc in range(4):
                            mm(ps[bank][:, 0:512], kmT[:, h * 4 + dc, mt * 128:(mt + 1) * 128], q1T[:, h * 4 + dc, :],
                               dc == 0, dc == 3, ['kmT', 'q1T'], [PS(bank)])
                        pt = sE['pt'] % 4
                        sE['pt'] += 1
                        act(PTx[pt][:], ps[bank][:, 0:512], AF.Exp, [], [PS(bank), ('PTx', pt)], scale=512.0 ** -0.5)
                        pts.append(pt)
                    for qs in range(4):
                        bank = nbE()
                        for mt in range(2):
                            mm(ps[bank][:, 0:512], PTx[pts[mt]][:, qs * 128:(qs + 1) * 128], vm[:, mt, h * 512:(h + 1) * 512],
                               mt == 0, mt == 1, [('PTx', pts[mt]), 'vm'], [PS(bank)])
                        b2 = 6 + (qs % 2)
                        for mt in range(2):
                            mm(ps[b2][:, 0:1], PTx[pts[mt]][:, qs * 128:(qs + 1) * 128], ones_b[:, 0:1],
                               mt == 0, mt == 1, [('PTx', pts[mt]), 'ones_b'], [PS(b2)])
                        recip(rrx[:, qs:qs + 1], ps[b2][:, 0:1], [], [PS(b2), 'rrx'])
                        ts('dve', o_g[:, qs, h * 512:(h + 1) * 512], ps[bank][:, 0:512], rrx[:, qs:qs + 1], 0.0, ALU.mult, ALU.add,
                           ['rrx'], [PS(bank), ('o_g', qs)])
                for i in range(4):
                    transpose16(o_g[:, i, :], ('o_g', i), aT, 'aT', i * 128)
                dense_ln(xa_wo, tiles, x2_d, 'x2')
            P.barrier()

        if stage == 'de':
            with contextlib.ExitStack() as Dg:
                cb2 = sbuf(Dg, "dbg_cb2", [128, 2048], F32)
                for tile in range(16):
                    P.dma('sp', cb2[:], x2_d[tile * 128:(tile + 1) * 128, :], reads=[('x2', tile)], writes=['dbg_cb2'])
                    final.append(P.dma('sp', out_d[tile * 128:(tile + 1) * 128, :], cb2[:], reads=['dbg_cb2'], writes=[('out', tile)],
                                       semkey='dbg_out'))
            P.emit(final_tokens=final[-1:])
            return nc

        Wd4 = wd_d.rearrange("s p (j t) -> s p j t", j=128)
        with contextlib.ExitStack() as F1:
            xbF = [sbuf(F1, "xbF%d" % i, [128, 2048], BF16) for i in range(2)]
            x2Ta = sbuf(F1, "x2Ta", [128, 16, 256], BF16)
            wbF = [sbuf(F1, "wbF%d" % i, [128, 16, 128], BF16) for i in range(2)]
            qpT = sbuf(F1, "qpT", [128, 16, 256], F32)
            skS = sbuf(F1, "skS", [128, 16 * 128], F32)
            s_sb = sbuf(F1, "s_sb", [128, 16, 128], F32)
            s2 = sbuf(F1, "s2", [128, 256], F32)
            vals = sbuf(F1, "vals", [128, 16, 16], F32)
            idx = sbuf(F1, "idx", [128, 16, 16], U32)
            idxf = sbuf(F1, "idxf", [128, 16, 16], F32)
            cand = sbuf(F1, "cand", [128, 8, 256], F32)
            cv = sbuf(F1, "cv", [128, 8, 16], F32)
            cpos = sbuf(F1, "cpos", [128, 8, 16], U32)
            rk = sbuf(F1, "rk", [128, 2, 128], U32)
            rkf = sbuf(F1, "rkf", [128, 2, 128], F32)
            oh = sbuf(F1, "oh", [128, 8, 16, 16], F32)
            sel = sbuf(F1, "sel", [128, 3, 128], F32)
            selT = sbuf(F1, "selT", [128, 3, 128], F32)
            gz = sbuf(F1, "gz", [128, 16], F32)
            OA = [sbuf(F1, "OA%d" % i, [128, 16, 128], BF16) for i in range(2)]
            OB = [sbuf(F1, "OB%d" % i, [128, 16, 128], BF16) for i in range(2)]
            Wt = [sbuf(F1, "Wt%d" % i, [128, 128, 128], BF16) for i in range(2)]
            P.dma('sp', skS[:], skT, writes=['skS'])
            pwq_v = pwq.rearrange("(c p) n -> p c n", p=128)
            xcnt = 0
            for grp in range(8):
                tiles = [grp * 2, grp * 2 + 1]
                for i, tile in enumerate(tiles):
                    xs = xcnt % 2
                    xcnt += 1
                    P.dma('pool', xbF[xs][:], x2_d[tile * 128:(tile + 1) * 128, :], writes=[('xbF', xs)])
                    transpose16(xbF[xs], ('xbF', xs), x2Ta, 'x2Ta', i * 128)
                for hc in range(16):
                    ws = hc % 2
                    for q in range(2):
                        P.dma('pool', wbF[ws][:, q * 8:(q + 1) * 8, :], pwq_v[:, q * 8:(q + 1) * 8, hc * 128:(hc + 1) * 128],
                              writes=[('wbF', ws)])
                    bank = hc % 4
                    for c in range(16):
                        mm(ps[bank][:, 0:256], wbF[ws][:, c, :], x2Ta[:, c, :], c == 0, c == 15, ['x2Ta', ('wbF', ws)], [PS(bank)])
                    cp('act' if hc % 2 == 0 else 'dve', qpT[:, hc, :], ps[bank][:, 0:256], [], [PS(bank), 'qpT'])
                for t in range(2):
                    tile = tiles[t]
                    wsl = tile % 2
                    for q4 in range(4):
                        bank = q4
                        for r4 in range(4):
                            hc = q4 * 4 + r4
                            mm(ps[bank][:, r4 * 128:(r4 + 1) * 128], qpT[:, hc, t * 128:(t + 1) * 128], skS[:, hc * 128:(hc + 1) * 128],
                               True, True, ['qpT', 'skS'], [PS(bank)])
                        cp('act', s_sb[:, q4 * 4:(q4 + 1) * 4, :],
                           ps[bank][:, 0:512].rearrange("p (a k) -> p a k", a=4), [], [PS(bank), 's_sb'])
                    for hc in range(16):
                        op_max(vals[:, hc, 0:8], s_sb[:, hc, :], ['s_sb'], ['vals'])
                        op_maxidx(idx[:, hc, 0:8], vals[:, hc, 0:8], s_sb[:, hc, :], ['s_sb', 'vals'], ['idx'])
                        op_mrep(s2[:, 0:128], vals[:, hc, 0:8], s_sb[:, hc, :], ['s_sb', 'vals'], ['s2'])
                        op_max(vals[:, hc, 8:16], s2[:, 0:128], ['s2'], ['vals'])
                        op_maxidx(idx[:, hc, 8:16], vals[:, hc, 8:16], s2[:, 0:128], ['s2', 'vals'], ['idx'])
                    cp('dve', idxf[:], idx[:], ['idx'], ['idxf'])
                    vv = vals[:].rearrange("p (h c) a -> p h c a", c=2)
                    tt('dve', cand[:].rearrange("p h (a b) -> p h a b", a=16), vv[:, :, 0, :].unsqueeze(3).to_broadcast([128, 8, 16, 16]),
                       vv[:, :, 1, :].unsqueeze(2).to_broadcast([128, 8, 16, 16]), ALU.add, ['vals'], ['cand'])
                    for h in range(8):
                        op_max(cv[:, h, 0:8], cand[:, h, :], ['cand'], ['cv'])
                        op_maxidx(cpos[:, h, 0:8], cv[:, h, 0:8], cand[:, h, :], ['cand', 'cv'], ['cpos'])
                        op_mrep(s2[:], cv[:, h, 0:8], cand[:, h, :], ['cand', 'cv'], ['s2'])
                        op_max(cv[:, h, 8:16], s2[:], ['s2'], ['cv'])
                        op_maxidx(cpos[:, h, 8:16], cv[:, h, 8:16], s2[:], ['s2', 'cv'], ['cpos'])
                    g3 = sel[:, 2, :].rearrange("p (h c) -> p h c", h=8)
                    tt('dve', g3, cv[:], cv[:, :, 0:1].to_broadcast([128, 8, 16]), ALU.subtract, ['cv'], ['sel'])
                    act(g3, g3, AF.Exp, [], ['sel'])
                    red(gz[:, 0:8], g3, ['sel'], ['gz'])
                    recip(gz[:, 8:16], gz[:, 0:8], ['gz'], ['gz'])
                    tt('dve', g3, g3, gz[:, 8:16].unsqueeze(2).to_broadcast([128, 8, 16]), ALU.mult, ['gz'], ['sel'])
                    cpf = cpos[:].rearrange("p h c -> p (h c)")
                    op_tss(rk[:, 0, :], cpf, 4, ALU.logical_shift_right, ['cpos'], ['rk'])
                    op_tss(rk[:, 1, :], cpf, 15, ALU.bitwise_and, ['cpos'], ['rk'])
                    cp('dve', rkf[:], rk[:], ['rk'], ['rkf'])
                    idv = idxf[:].rearrange("p (h c) a -> p h c a", c=2)
                    for half in range(2):
                        rv = rkf[:, half, :].rearrange("p (h c) -> p h c", h=8)
                        tt('dve', oh[:], rv.unsqueeze(3).to_broadcast([128, 8, 16, 16]),
                           iota_f[:, 0:16].unsqueeze(1).unsqueeze(1).to_broadcast([128, 8, 16, 16]), ALU.is_equal, ['rkf', 'cst'], ['oh'])
                        tt('dve', oh[:], oh[:], idv[:, :, half, :].unsqueeze(2).to_broadcast([128, 8, 16, 16]), ALU.mult, ['idxf'], ['oh'])
                        red(sel[:, half, :].rearrange("p (h c) -> p h c", h=8), oh[:], ['oh'], ['sel'])
                    for q3 in range(3):
                        tr(ps[4][:, q3 * 128:(q3 + 1) * 128], sel[:, q3, :], ['sel'], [PS(4)], fp32=True)
                    cp('act', selT[:], ps[4][:, 0:384].rearrange("p (a t) -> p a t", a=3), [], [PS(4), 'selT'])
                    for tc in range(8):
                        k = tc % 2
                        tok0 = tc * 16
                        io = iota_f.unsqueeze(1).to_broadcast([128, 16, 128])
                        tt('dve', OA[k][:], io, selT[:, 0, tok0:tok0 + 16].unsqueeze(2).to_broadcast([128, 16, 128]), ALU.is_equal,
                           ['cst', 'selT'], [('OA', k)])
                        tt('dve', OA[k][:], OA[k][:], selT[:, 2, tok0:tok0 + 16].unsqueeze(2).to_broadcast([128, 16, 128]), ALU.mult,
                           ['selT'], [('OA', k)])
                        tt('dve', OB[k][:], io, selT[:, 1, tok0:tok0 + 16].unsqueeze(2).to_broadcast([128, 16, 128]), ALU.is_equal,
                           ['cst', 'selT'], [('OB', k)])
                        for q4 in range(4):
                            bank = 5 + ((tc * 4 + q4) % 3)
                            for r4 in range(4):
                                tl = q4 * 4 + r4
                                mm(ps[bank][:, r4 * 128:(r4 + 1) * 128], OA[k][:, tl, :], OB[k][:, tl, :], True, True,
                                   [('OA', k), ('OB', k)], [PS(bank)])
                            a0 = tok0 + q4 * 4
                            cp('act', Wt[wsl][:, :, a0:a0 + 4].rearrange("p j t -> p t j"),
                               ps[bank][:, 0:512].rearrange("p (t j) -> p t j", t=4), [], [PS(bank), ('Wt', wsl)])
                    P.dma('sp', wd_d[tile], Wt[wsl][:].rearrange("p j t -> p (j t)"), reads=[('Wt', wsl)], writes=[('wd', tile)],
                          semkey=('Wt_st', wsl))
            P.barrier(new_epoch=False)

        for pp in range(2):
            with contextlib.ExitStack() as P2:
                pn = "p%d_" % pp
                x2T = sbuf(P2, pn + "x2T", [128, 16, 1024], BF16)
                acc_sb = sbuf(P2, pn + "acc", [128, 8, 2048], F32)
                with contextlib.ExitStack() as F2:
                    xb2 = [sbuf(F2, pn + "xb2%d" % i, [128, 2048], BF16) for i in range(2)]
                    ub = [sbuf(F2, pn + "ub%d" % i, [128, 16, 128], BF16) for i in range(4)]
                    vb = [sbuf(F2, pn + "vb%d" % i, [128, 2048], BF16) for i in range(4)]
                    Wj4 = [sbuf(F2, pn + "Wj4%d" % i, [128, 8, 4, 128], BF16) for i in range(2)]
                    ga = [sbuf(F2, pn + "ga%d" % i, [128, 512], F32) for i in range(2)]
                    aTj = [sbuf(F2, pn + "aTj%d" % i, [128, 1024], BF16) for i in range(4)]
                    for s8 in range(8):
                        tile = pp * 8 + s8
                        P.dma('pool', xb2[s8 % 2][:], x2_d[tile * 128:(tile + 1) * 128, :], writes=[('xb2', s8 % 2)])
                        transpose16(xb2[s8 % 2], ('xb2', s8 % 2), x2T, 'x2T', s8 * 128)
                    accset = 0
                    gcnt = 0
                    for jb in range(64):
                        if jb % 2 == 0:
                            wq4 = (jb // 2) % 2
                            j0 = jb * 2
                            for sh in range(2):
                                P.dma('sp', Wj4[wq4][:, sh * 4:(sh + 1) * 4, :, :],
                                      Wd4[pp * 8 + sh * 4:pp * 8 + sh * 4 + 4, :, j0:j0 + 4, :].rearrange("s p j t -> p s j t"),
                                      writes=[('Wj4', wq4)])
                        for jj in range(2):
                            j = jb * 2 + jj
                            sl = (jb % 2) * 2 + jj
                            jq = j % 4
                            P.dma('pool', ub[sl][:], uh[j].rearrange("p (c i) -> p c i", c=16), writes=[('ub', sl)])
                            P.dma('pool', vb[sl][:], vh[j], writes=[('vb', sl)])
                            for half in range(2):
                                bank = (j % 2) * 2 + half
                                for c in range(16):
                                    mm(ps[bank][:, 0:512], ub[sl][:, c, :], x2T[:, c, half * 512:(half + 1) * 512], c == 0, c == 15,
                                       [('ub', sl), 'x2T'], [PS(bank)])
                                gs = gcnt % 2
                                gcnt += 1
                                act(ga[gs][:], ps[bank][:, 0:512], AF.Gelu, [], [PS(bank), ('ga', gs)])
                                tt('dve', aTj[sl][:, half * 512:(half + 1) * 512].rearrange("p (s t) -> p s t", s=4),
                                   ga[gs][:].rearrange("p (s t) -> p s t", s=4), Wj4[wq4][:, half * 4:(half + 1) * 4, jq, :], ALU.mult,
                                   [('ga', gs), ('Wj4', wq4)], [('aTj', sl)])
                        for s8 in range(8):
                            for cpair in range(2):
                                b0 = 4 + (accset % 2) * 2
                                accset += 1
                                for jj in range(2):
                                    sl = (jb % 2) * 2 + jj
                                    for cbk in range(2):
                                        col0 = (cpair * 2 + cbk) * 512
                                        mm(ps[b0 + cbk][:, 0:512], aTj[sl][:, s8 * 128:(s8 + 1) * 128], vb[sl][:, col0:col0 + 512],
                                           jj == 0, jj == 1, [('aTj', sl), ('vb', sl)], [PS(b0 + cbk)])
                                for cbk in range(2):
                                    col0 = (cpair * 2 + cbk) * 512
                                    if jb == 0:
                                        cp('dve', acc_sb[:, s8, col0:col0 + 512], ps[b0 + cbk][:, 0:512], [], [PS(b0 + cbk), ('acc', s8)])
                                    else:
                                        tt('dve', acc_sb[:, s8, col0:col0 + 512], acc_sb[:, s8, col0:col0 + 512], ps[b0 + cbk][:, 0:512], ALU.add,
                                           [], [PS(b0 + cbk), ('acc', s8)])
                    P.barrier(new_epoch=False)
                with contextlib.ExitStack() as F3:
                    rF = [sbuf(F3, pn + "rF%d" % i, [128, 2048], F32) for i in range(2)]
                    lg = sbuf(F3, pn + "lg", [128, 2048], F32)
                    lb = sbuf(F3, pn + "lb", [128, 2048], F32)
                    st6f = sbuf(F3, pn + "st6f", [128, 4, 6], F32)
                    lmvf = sbuf(F3, pn + "lmvf", [128, 8], F32)
                    epsf = sbuf(F3, pn + "epsf", [128, 1], F32)
                    memset('pool', epsf[:], LN_EPS, ['epsf'])
                    P.dma('sp', lg[:], lnp[4:5, :].partition_broadcast(128), writes=['lg'])
                    P.dma('sp', lb[:], lnp[5:6, :].partition_broadcast(128), writes=['lb'])
                    for s8 in range(8):
                        tile = pp * 8 + s8
                        r_ = rF[s8 % 2]
                        rk_ = ('rF', s8 % 2)
                        P.dma('sp', r_[:], x2_d[tile * 128:(tile + 1) * 128, :], writes=[rk_])
                        stt(r_[:], r_[:], ALPHA, acc_sb[:, s8, :], ALU.mult, ALU.add, [('acc', s8)], [rk_])
                        for q in range(4):
                            op_bnstats(st6f[:, q, :], r_[:, q * 512:(q + 1) * 512], [rk_], ['st6f'])
                        op_bnaggr(lmvf[:, 0:2], st6f[:].rearrange("p a b -> p (a b)"), ['st6f'], ['lmvf'])
                        act(lmvf[:, 2:3], lmvf[:, 1:2], AF.Sqrt, ['lmvf', 'epsf'], ['lmvf'], bias=epsf[:, 0:1], scale=1.0)
                        recip(lmvf[:, 3:4], lmvf[:, 2:3], ['lmvf'], ['lmvf'])
                        ts('dve', r_[:], r_[:], lmvf[:, 0:1], lmvf[:, 3:4], ALU.subtract, ALU.mult, ['lmvf'], [rk_])
                        tt('pool', r_[:], r_[:], lg[:], ALU.mult, ['lg'], [rk_])
                        tt('pool', r_[:], r_[:], lb[:], ALU.add, ['lb'], [rk_])
                        final.append(P.dma('sp', out_d[tile * 128:(tile + 1) * 128, :], r_[:], reads=[rk_],
                                           writes=[('out', tile)], semkey=('rF_st', s8 % 2)))
                    P.barrier(new_epoch=False)
        P.emit(final_tokens=final)
        return nc
    return nc


def _consts():
    c = np.zeros((128, 1024), np.float32)
    idx = np.arange(128)
    c[:, 0:128] = np.eye(128, dtype=np.float32)
    c[:, 128:256] = (idx[:, None] <= idx[None, :]).astype(np.float32)
    c[:, 256:384] = (idx[:, None] >= idx[None, :]).astype(np.float32)
    c[:, 384:512] = np.where(idx[:, None] <= idx[None, :], 0.0, NEG)
    c[:, 512:640] = np.where(idx[:, None] >= idx[None, :], 0.0, NEG)
    c[:, 640:768] = 1.0
    c[:, 768:896] = idx[None, :].astype(np.float32)
    return c


def _rope_table(pos):
    pos = np.asarray(pos)
    row = (pos // 64).astype(np.float32)
    col = (pos % 64).astype(np.float32)
    n_freq = 32
    inv_freq = (np.float32(10000.0) ** (-np.arange(n_freq, dtype=np.float32) / np.float32(n_freq))).astype(np.float32)
    ang_r = (row[:, None] * inv_freq).astype(np.float32)
    ang_c = (col[:, None] * inv_freq).astype(np.float32)
    cr, sr, cc, sc = np.cos(ang_r), np.sin(ang_r), np.cos(ang_c), np.sin(ang_c)
    tab = np.concatenate([cr, cr, cc, cc, -sr, sr, -sc, sc], axis=1).astype(np.float32)
    return np.ascontiguousarray(tab)


def prep_inputs(inp, cores=range(8)):
    f = lambda a: np.ascontiguousarray(np.asarray(a, dtype=np.float32))
    x = f(inp['x'])
    mem = f(inp['mem'])
    w_in = f(inp['w_in'])[0]
    w_in_sw = w_in.copy()
    w_in_sw[:, 4608:4612] = w_in[:, 4612:4616]
    w_in_sw[:, 4612:4616] = w_in[:, 4608:4612]
    w_in_sw[:, 4616:4620] = w_in[:, 4620:4624]
    w_in_sw[:, 4620:4624] = w_in[:, 4616:4620]
    bi = f(inp['b_igate'])[0]
    bf = f(inp['b_fgate'])[0]
    bg0 = np.concatenate([bi.reshape(8), bf.reshape(8)])[None, :]
    bg1 = np.concatenate([bi[::-1].reshape(8), bf[::-1].reshape(8)])[None, :]
    gqk = np.concatenate([f(inp['att_q_norm'])[0], f(inp['att_k_norm'])[0]])[None, :]
    mln = f(inp['ml_norm'])
    lnp = np.stack([f(inp['ln1_g'])[0], f(inp['ln1_b'])[0], f(inp['ln2_g'])[0], f(inp['ln2_b'])[0],
                    f(inp['ln3_g'])[0], f(inp['ln3_b'])[0]])
    sk = f(inp['peer_subkeys'])[0]
    skT = np.ascontiguousarray(sk.transpose(3, 0, 1, 2).reshape(128, 16 * 128))
    u = f(inp['peer_u'])[0]
    v = f(inp['peer_v'])[0]
    uh = np.ascontiguousarray(u.reshape(128, 128, 16, 128).transpose(1, 3, 2, 0)).reshape(128, 128, 2048)
    vh = np.ascontiguousarray(v.reshape(128, 128, 2048).transpose(1, 0, 2))
    cst = _consts()
    common = dict(cst=cst, gqk=f(gqk), mln=mln, w_out=f(inp['w_out'])[0], xa_wq=f(inp['xa_wq'])[0],
                  xa_wk=f(inp['xa_wk'])[0], xa_wv=f(inp['xa_wv'])[0], xa_wo=f(inp['xa_wo'])[0], lnp=f(lnp),
                  pwq=f(inp['peer_wq'])[0], skT=skT, uh=uh, vh=vh)
    rope0 = _rope_table(np.arange(4096))
    rope1 = _rope_table(4095 - np.arange(4096))
    maps = []
    for c in cores:
        b, half = c // 2, c % 2
        m = dict(common)
        if half == 0:
            m['xl'] = np.ascontiguousarray(x[b])
            m['rope'] = rope0
            m['w_in'] = w_in
            m['bg'] = f(bg0)
        else:
            m['xl'] = np.ascontiguousarray(x[b][::-1])
            m['rope'] = rope1
            m['w_in'] = w_in_sw
            m['bg'] = f(bg1)
        m['memb'] = np.ascontiguousarray(mem[b])
        maps.append(m)
    return maps


def assemble(results, cores=range(8)):
    out = np.zeros((4, 4096, 2048), np.float32)
    for r, c in zip(results, cores):
        b, half = c // 2, c % 2
        o = np.asarray(r["out"], dtype=np.float32)
        if half == 0:
            out[b, 0:2048] = o
        else:
            out[b, 2048:4096] = o[::-1]
    return out


_NC_CACHE = {}


def kernel(**inputs):
    if 'nc' not in _NC_CACHE:
        _NC_CACHE['nc'] = build_program('all')
    nc = _NC_CACHE['nc']
    maps = prep_inputs(inputs)
    res = run_bass_kernel_spmd(nc, maps, core_ids=list(range(8)))
    return assemble(res.results)
```

```python
import contextlib
import math
import numpy as np
import concourse.bass as bass
import concourse.mybir as mybir
from concourse.bass_utils import run_bass_kernel_spmd

F32 = mybir.dt.float32
BF16 = mybir.dt.bfloat16
U32 = mybir.dt.uint32
AF = mybir.ActivationFunctionType
ALU = mybir.AluOpType
AX = mybir.AxisListType

ALPHA = 2.0 ** 0.25
LN_EPS = 1e-5
RMS_EPS = 1e-6
NEG = -1.0e4


class Prog:
    ENGS = ('pe', 'act', 'dve', 'pool', 'sp')

    def __init__(self, nc):
        self.nc = nc
        self.ops = {e: [] for e in self.ENGS}
        self.res = {}
        self.dma_sem_count = {}
        self.epoch = 0

    def _deps(self, eng, reads, writes, is_dma):
        deps = []
        for r in reads:
            st = self.res.get(r)
            if st:
                deps.extend(st['w'])
        for w in writes:
            st = self.res.get(w)
            if st:
                for t in st['w']:
                    if not (is_dma and t[0] == 'dma'):
                        deps.append(t)
                deps.extend(st['r'])
        out = []
        for d in deps:
            if d[0] == 'eng' and d[1] == eng and eng == 'pe':
                continue
            if d not in out:
                out.append(d)
        return out

    def _commit(self, tok, reads, writes):
        for r in reads:
            st = self.res.setdefault(r, {'w': [], 'r': []})
            st['r'].append(tok)
        for w in writes:
            st = self.res.setdefault(w, {'w': [], 'r': []})
            if tok[0] == 'dma' and st['w'] and all(t[0] == 'dma' for t in st['w']) and not st['r']:
                st['w'] = [t for t in st['w'] if t[1] != tok[1]] + [tok]
            else:
                st['w'] = [tok]
            st['r'] = []

    def op(self, eng, fn, reads=(), writes=()):
        deps = self._deps(eng, reads, writes, False)
        tok = ('eng', eng, len(self.ops[eng]))
        self.ops[eng].append({'fn': fn, 'deps': deps, 'needed': False, 'dma': None, 'ep': self.epoch})
        self._commit(tok, reads, writes)
        return tok

    def dma(self, eng, out, in_, reads=(), writes=(), semkey=None):
        deps = self._deps(eng, reads, writes, True)
        sk = semkey if semkey is not None else (writes[0], eng)
        self.dma_sem_count[sk] = self.dma_sem_count.get(sk, 0) + 16
        tok = ('dma', sk, self.dma_sem_count[sk])
        self.ops[eng].append({'fn': (lambda e: e.dma_start(out=out, in_=in_)), 'deps': deps,
                              'needed': True, 'dma': sk, 'ep': self.epoch})
        self._commit(tok, reads, writes)
        return tok

    def all_tokens(self):
        toks = []
        for e in self.ENGS:
            for i in range(len(self.ops[e]) - 1, -1, -1):
                o = self.ops[e][i]
                if o['fn'] is not None and o['dma'] is None:
                    toks.append(('eng', e, i))
                    break
        for sk, v in self.dma_sem_count.items():
            toks.append(('dma', sk, v))
        return toks

    def barrier(self, new_epoch=True):
        toks = self.all_tokens()
        for e in self.ENGS:
            self.ops[e].append({'fn': None, 'deps': [t for t in toks if not (t[0] == 'eng' and t[1] == e)],
                                'needed': False, 'dma': None, 'ep': self.epoch})
        self.res = {}
        if new_epoch:
            self.epoch += 1

    def emit(self, final_tokens=()):
        nc = self.nc
        for e in self.ENGS:
            for o in self.ops[e]:
                for d in o['deps']:
                    if d[0] == 'eng':
                        self.ops[d[1]][d[2]]['needed'] = True
        self.maxval = {}
        for e in self.ENGS:
            c = {}
            for o in self.ops[e]:
                if o['dma'] is None and o['needed']:
                    c[o['ep']] = c.get(o['ep'], 0) + 1
                    o['val'] = c[o['ep']]
            self.maxval[e] = dict(c)
        with contextlib.ExitStack() as st:
            esem = {(e, ep): st.enter_context(nc.semaphore('es_%s_%d' % (e, ep)))
                    for e in self.ENGS if e != 'sp' for ep in range(self.epoch + 1)}
            dsem = {}
            for i, k in enumerate(self.dma_sem_count):
                dsem[k] = st.enter_context(nc.semaphore('ds_%d' % i))
            block = st.enter_context(nc.Block())

            def tokval(d):
                if d[0] == 'eng':
                    o = self.ops[d[1]][d[2]]
                    return ('e', d[1], o['ep']), esem[(d[1], o['ep'])], o['val']
                return ('d', d[1]), dsem[d[1]], d[2]

            def body(ename, extra_final=()):
                def f(eng):
                    seen = {}

                    def waits(deps):
                        best = {}
                        for d in deps:
                            k, s, v = tokval(d)
                            if v > best.get(k, (None, 0))[1]:
                                best[k] = (s, v)
                        for k, (s, v) in best.items():
                            if seen.get(k, 0) >= v:
                                continue
                            seen[k] = v
                            eng.wait_ge(s, v)
                    for o in self.ops[ename]:
                        waits(o['deps'])
                        if o['fn'] is None:
                            continue
                        ins = o['fn'](eng)
                        if o['dma'] is not None:
                            ins.then_inc(dsem[o['dma']], 16)
                        elif o['needed']:
                            ins.then_inc(esem[(ename, o['ep'])], 1)
                    waits(extra_final)
                return f
            block.tensor(body('pe'))
            block.scalar(body('act'))
            block.vector(body('dve'))
            block.gpsimd(body('pool'))
            block.sync(body('sp', tuple(final_tokens)))


def build_program(stage='all'):
    nc = bass.Bass("TRN2", target_bir_lowering=False)

    def din(n, s, d=F32):
        return nc.dram_tensor(n, s, d, kind="ExternalInput").ap()

    def dscr(n, s, d):
        return nc.dram_tensor(n, s, d, kind="Internal").ap()

    xl = din("xl", [4096, 2048])
    rope = din("rope", [4096, 256])
    w_in = din("w_in", [2048, 4624])
    bg = din("bg", [1, 16])
    gqk = din("gqk", [1, 256])
    mln_d = din("mln", [1, 1024])
    memb = din("memb", [256, 2048])
    cst_d = din("cst", [128, 1024])
    w_out = din("w_out", [2048, 2048])
    xa_wq = din("xa_wq", [2048, 2048])
    xa_wk = din("xa_wk", [2048, 2048])
    xa_wv = din("xa_wv", [2048, 2048])
    xa_wo = din("xa_wo", [2048, 2048])
    lnp = din("lnp", [6, 2048])
    pwq = din("pwq", [2048, 2048])
    skT = din("skT", [128, 16 * 128])
    uh = din("uh", [128, 128, 2048])
    vh = din("vh", [128, 128, 2048])

    hb_d = dscr("hb_s", [2048, 1024], F32)
    cat_d = dscr("cat_s", [2048, 2048], BF16)
    x1_d = dscr("x1_s", [2048, 2048], F32)
    x2_d = dscr("x2_s", [2048, 2048], F32)
    wd_d = dscr("wd_s", [16, 128, 128 * 128], BF16)
    w_in_b = dscr("w_in_b", [2048, 4624], BF16)
    w_out_b = dscr("w_out_b", [2048, 2048], BF16)
    xa_wq_b = dscr("xa_wq_b", [2048, 2048], BF16)
    xa_wo_b = dscr("xa_wo_b", [2048, 2048], BF16)
    pwq_b = dscr("pwq_b", [2048, 2048], BF16)

    if stage == 'mix':
        out_d = nc.dram_tensor("out", [2048, 2048], BF16, kind="ExternalOutput").ap()
    else:
        out_d = nc.dram_tensor("out", [2048, 2048], F32, kind="ExternalOutput").ap()

    P = Prog(nc)
    final = []

    with contextlib.ExitStack() as G:
        def sbuf(st, n, s, d):
            return st.enter_context(nc.sbuf_tensor('sb_' + n, s, d))

        ps = [G.enter_context(nc.psum_tensor("ps%d" % i, [128, 512], F32)) for i in range(8)]

        def PS(i):
            return ('ps', i)

        cst = sbuf(G, "cst", [128, 1024], F32)
        ident_b = sbuf(G, "ident_b", [128, 128], BF16)
        ones_b = sbuf(G, "ones_b", [128, 128], BF16)
        ident_f = cst[:, 0:128]
        Umask = cst[:, 128:256]
        Lmask = cst[:, 256:384]
        NEGU = cst[:, 384:512]
        NEGL = cst[:, 512:640]
        ones_f = cst[:, 640:768]
        iota_f = cst[:, 768:896]

        P.dma('sp', cst[:], cst_d, writes=['cst'])
        P.op('dve', lambda e: e.tensor_copy(out=ident_b[:], in_=ident_f), reads=['cst'], writes=['ident_b'])
        P.op('dve', lambda e: e.tensor_copy(out=ones_b[:], in_=ones_f), reads=['cst'], writes=['ones_b'])

        def mm(out, lhsT, rhs, start, stop, reads, writes):
            return P.op('pe', lambda e: e.matmul(out, lhsT, rhs, start=start, stop=stop), reads=reads, writes=writes)

        def tr(out, in_, reads, writes, fp32=False):
            idn = ident_f if fp32 else ident_b[:]
            return P.op('pe', lambda e: e.transpose(out=out, in_=in_, identity=idn),
                        reads=list(reads) + (['cst'] if fp32 else ['ident_b']), writes=writes)

        def act(out, in_, func, reads, writes, bias=None, scale=None):
            kw = {}
            if bias is not None:
                kw['bias'] = bias
            if scale is not None:
                kw['scale'] = scale
            return P.op('act', lambda e: e.activation(out=out, in_=in_, func=func, **kw), reads=reads, writes=writes)

        def tt(eng, out, in0, in1, op, reads, writes):
            return P.op(eng, lambda e: e.tensor_tensor(out=out, in0=in0, in1=in1, op=op), reads=reads, writes=writes)

        def ts(eng, out, in0, s1, s2, op0, op1, reads, writes):
            return P.op(eng, lambda e: e.tensor_scalar(out=out, in0=in0, scalar1=s1, scalar2=s2, op0=op0, op1=op1),
                        reads=reads, writes=writes)

        def stt(out, in0, scalar, in1, op0, op1, reads, writes):
            return P.op('dve', lambda e: e.scalar_tensor_tensor(out=out, in0=in0, scalar=scalar, in1=in1, op0=op0, op1=op1),
                        reads=reads, writes=writes)

        def cp(eng, out, in_, reads, writes):
            if eng == 'act':
                return P.op('act', lambda e: e.activation(out=out, in_=in_, func=AF.Copy), reads=reads, writes=writes)
            return P.op(eng, lambda e: e.tensor_copy(out=out, in_=in_), reads=reads, writes=writes)

        def red(out, in_, reads, writes, op=ALU.add):
            return P.op('dve', lambda e: e.tensor_reduce(out=out, in_=in_, axis=AX.X, op=op), reads=reads, writes=writes)

        def recip(out, in_, reads, writes):
            return P.op('dve', lambda e: e.reciprocal(out=out, in_=in_), reads=reads, writes=writes)

        def memset(eng, ap, val, writes):
            return P.op(eng, lambda e: e.memset(ap, val), writes=writes)

        def op_max(out, in_, reads, writes):
            return P.op('dve', lambda e: e.max(out=out, in_=in_), reads=reads, writes=writes)

        def op_maxidx(out, mx, vals_, reads, writes):
            return P.op('dve', lambda e: e.max_index(out=out, in_max=mx, in_values=vals_), reads=reads, writes=writes)

        def op_mrep(out, rep, vals_, reads, writes):
            return P.op('dve', lambda e: e.match_replace(out=out, in_to_replace=rep, in_values=vals_, imm_value=-1.0e30),
                        reads=reads, writes=writes)

        def op_tss(out, in_, scalar, op, reads, writes):
            return P.op('dve', lambda e: e.tensor_single_scalar(out=out, in_=in_, scalar=scalar, op=op), reads=reads, writes=writes)

        def op_bnstats(out, in_, reads, writes):
            return P.op('dve', lambda e: e.bn_stats(out=out, in_=in_), reads=reads, writes=writes)

        def op_bnaggr(out, in_, reads, writes):
            return P.op('dve', lambda e: e.bn_aggr(out=out, in_=in_), reads=reads, writes=writes)

        rr_state = {'tb': 0, 'ev': 0, 'uv': 0, 'wc': 0}
        for rb in range(16):
            P.dma('pool', w_in_b[rb * 128:(rb + 1) * 128, :], w_in[rb * 128:(rb + 1) * 128, :], writes=['w_in_b'], semkey='wcast_in')
        wcast_list = [(dst, src, rb) for (dst, src) in ((w_out_b, w_out), (xa_wq_b, xa_wq), (xa_wo_b, xa_wo), (pwq_b, pwq)) for rb in range(16)]

        def cast_w(n):
            for _ in range(n):
                k = rr_state['wc']
                if k >= len(wcast_list):
                    return
                rr_state['wc'] += 1
                dst, src, rb = wcast_list[k]
                P.dma('pool', dst[rb * 128:(rb + 1) * 128, :], src[rb * 128:(rb + 1) * 128, :], writes=[('wcast', k)], semkey='wcast_rest')

        def transpose16(src, src_res, dstT, dst_res, col0):
            for half in range(2):
                bank = 6 + (rr_state['tb'] % 2)
                rr_state['tb'] += 1
                psb = ps[bank][:].bitcast(BF16)
                for k in range(8):
                    c = half * 8 + k
                    tr(psb[:, k * 128:(k + 1) * 128], src[:, c * 128:(c + 1) * 128], [src_res], [PS(bank)])
                eng = 'act' if rr_state['ev'] % 2 == 0 else 'dve'
                rr_state['ev'] += 1
                cp(eng, dstT[:, half * 8:(half + 1) * 8, col0:col0 + 128],
                   psb.rearrange("p (k t) -> p k t", k=8), [], [PS(bank), dst_res])

        with contextlib.ExitStack() as M:
            KT = sbuf(M, "KT", [128, 2, 4096], BF16)
            VA = sbuf(M, "VA", [128, 32, 2, 130], BF16)
            xb = [sbuf(M, "xb%d" % i, [128, 2048], BF16) for i in range(2)]
            xT = sbuf(M, "xT", [128, 16, 512], BF16)
            wb = [sbuf(M, "wb%d" % i, [128, 16, 512], BF16) for i in range(2)]
            wg = sbuf(M, "wg", [128, 16, 16], BF16)
            ropeT = [sbuf(M, "ropeT%d" % i, [128, 256], F32) for i in range(2)]
            gqk_t = sbuf(M, "gqk_t", [128, 256], F32)
            bgt = sbuf(M, "bgt", [128, 16], F32)
            mln_t = sbuf(M, "mln_t", [128, 1024], F32)
            ig = sbuf(M, "ig", [128, 4, 8], F32)
            lf = sbuf(M, "lf", [128, 4, 8], F32)
            t0 = sbuf(M, "t0", [128, 1024], F32)
            t1 = sbuf(M, "t1", [128, 1024], F32)
            qkb = sbuf(M, "qkb", [128, 512], BF16)
            sm = sbuf(M, "sm", [128, 64], F32)
            QT = sbuf(M, "QT", [128, 8, 512], BF16)
            mqT = sbuf(M, "mqT", [128, 4, 512], BF16)
            mkT = sbuf(M, "mkT", [128, 4, 512], BF16)
            mk_tok = sbuf(M, "mk_tok", [128, 4, 512], BF16)
            mv_aug = sbuf(M, "mv_aug", [128, 4, 4, 258], BF16)
            mo_sig = sbuf(M, "mo_sig", [128, 4, 1024], BF16)
            att_g = sbuf(M, "att_g", [128, 4, 1024], BF16)
            mlb = [sbuf(M, "mlb%d" % i, [128, 1024], BF16) for i in range(2)]
            PTb = [sbuf(M, "PTb%d" % i, [128, 512], BF16) for i in range(3)]
            Fm = [sbuf(M, "Fm%d" % i, [128, 128], F32) for i in range(2)]
            DT = [sbuf(M, "DT%d" % i, [128, 128], F32) for i in range(2)]
            EB = [sbuf(M, "EB%d" % i, [128, 128], F32) for i in range(2)]
            PTm = [sbuf(M, "PTm%d" % i, [128, 128], BF16) for i in range(2)]
            qsT = [sbuf(M, "qsT%d" % i, [128, 128], BF16) for i in range(2)]
            kw = [sbuf(M, "kw%d" % i, [128, 128], BF16) for i in range(2)]
            mc = [sbuf(M, "mc%d" % i, [128, 8], F32) for i in range(2)]
            Cf = [sbuf(M, "Cf%d" % i, [128, 4, 257], F32) for i in range(2)]
            Cb = [sbuf(M, "Cb%d" % i, [128, 4, 258], BF16) for i in range(2)]
            hout = sbuf(M, "hout", [128, 1024], F32)
            hbl = sbuf(M, "hbl", [128, 1024], F32)

            P.dma('sp', gqk_t[:], gqk.partition_broadcast(128), writes=['gqk_t'])
            P.dma('sp', bgt[:], bg.partition_broadcast(128), writes=['bgt'])
            P.dma('sp', mln_t[:], mln_d.partition_broadcast(128), writes=['mln_t'])
            w_in_v = w_in_b.rearrange("(c p) n -> p c n", p=128)
            P.dma('sp', wg[:], w_in_v[:, :, 4608:4624], reads=['w_in_b'], writes=['wg'])
            memset('pool', VA[:], 1.0, ['VA'])
            memset('pool', mv_aug[:], 1.0, ['mv_aug'])
            for d in range(2):
                memset('pool', Cf[d][:], 0.0, [('Cf', d, h) for h in range(4)])
                memset('pool', Cb[d][:], 0.0, [('Cb', d, h) for h in range(4)])

            st = {'xslot': 0, 'wslot': 0, 'rslot': 0, 'bank': 0, 'mi': 0}

            def load_w(col0, ncols):
                slot = st['wslot'] % 2
                st['wslot'] += 1
                for q in range(4):
                    P.dma('sp', wb[slot][:, q * 4:(q + 1) * 4, 0:ncols], w_in_v[:, q * 4:(q + 1) * 4, col0:col0 + ncols],
                          reads=['w_in_b'], writes=[('wb', slot)])
                return slot

            def nbank():
                b = st['bank'] % 6
                st['bank'] += 1
                return b

            def load_group(tiles):
                for i, tile in enumerate(tiles):
                    slot = st['xslot'] % 2
                    st['xslot'] += 1
                    P.dma('pool', xb[slot][:], xl[tile * 128:(tile + 1) * 128, :], writes=[('xb', slot)])
                    transpose16(xb[slot], ('xb', slot), xT, 'xT', i * 128)

            def proj_tok(i, wslot, c0, ncols, bank):
                for c in range(16):
                    mm(ps[bank][:, 0:ncols], xT[:, c, i * 128:(i + 1) * 128], wb[wslot][:, c, c0:c0 + ncols],
                       c == 0, c == 15, ['xT', ('wb', wslot)], [PS(bank)])

            def proj_feat(j, wslot, bank):
                for c in range(16):
                    mm(ps[bank][:, 0:512], wb[wslot][:, c, j * 128:(j + 1) * 128], xT[:, c, :],
                       c == 0, c == 15, ['xT', ('wb', wslot)], [PS(bank)])

            def norm_rope(psap, bank, H, gain, rslot, outb):
                W = H * 128
                cp('act', t0[:, 0:W], psap, [], [PS(bank), 't0'])
                tt('dve', t1[:, 0:W], t0[:, 0:W], t0[:, 0:W], ALU.mult, ['t0'], ['t1'])
                red(sm[:, 0:H], t1[:, 0:W].rearrange("p (h d) -> p h d", h=H), ['t1'], ['sm'])
                act(sm[:, 8:8 + H], sm[:, 0:H], AF.Sqrt, ['sm'], ['sm'], bias=eps_rms[:, 0:1], scale=1.0 / 128.0)
                recip(sm[:, 16:16 + H], sm[:, 8:8 + H], ['sm'], ['sm'])
                t0v = t0[:, 0:W].rearrange("p (h d) -> p h d", h=H)
                t1v = t1[:, 0:W].rearrange("p (h d) -> p h d", h=H)
                tt('dve', t1v, t0v, sm[:, 16:16 + H].unsqueeze(2).to_broadcast([128, H, 128]), ALU.mult,
                   ['t0', 'sm'], ['t1'])
                tt('dve', t1v, t1v, gain.unsqueeze(1).to_broadcast([128, H, 128]), ALU.mult, ['t1', 'gqk_t'], ['t1'])
                rt = ropeT[rslot]
                tt('dve', t0v, t1v, rt[:, 0:128].unsqueeze(1).to_broadcast([128, H, 128]), ALU.mult,
                   ['t1', ('rope', rslot)], ['t0'])
                t1z = t1[:, 0:W].rearrange("p (h a z d) -> p h a z d", h=H, a=2, z=2)
                sz = rt[:, 128:256].rearrange("p (a z d) -> p a z d", a=2, z=2)
                q2 = qk2[:, 0:W].rearrange("p (h a z d) -> p h a z d", h=H, a=2, z=2)
                for z in range(2):
                    tt('dve', q2[:, :, :, z, :], t1z[:, :, :, 1 - z, :],
                       sz[:, :, z, :].unsqueeze(1).to_broadcast([128, H, 2, 32]), ALU.mult,
                       ['t1', ('rope', rslot)], ['qk2'])
                tt('dve', outb, t0[:, 0:W], qk2[:, 0:W], ALU.add, ['t0', 'qk2'], ['qkb'])

            qk2 = sbuf(M, "qk2", [128, 512], F32)
            eps_rms = sbuf(M, "eps_rms", [128, 2], F32)
            memset('pool', eps_rms[:, 0:1], RMS_EPS, ['eps_rms'])
            memset('pool', eps_rms[:, 1:2], 1.0, ['eps_rms'])

            def load_rope(tile):
                slot = st['rslot'] % 2
                st['rslot'] += 1
                P.dma('sp', ropeT[slot][:], rope[tile * 128:(tile + 1) * 128, :], writes=[('rope', slot)])
                return slot

            def do_kv(tiles, wslot):
                for i, tile in enumerate(tiles):
                    bank = nbank()
                    proj_tok(i, wslot, 0, 512, bank)
                    rs = load_rope(tile)
                    cp('act', VA[:, tile, :, 0:128], ps[bank][:, 256:512].rearrange("p (g d) -> p g d", g=2),
                       [], [PS(bank), 'VA'])
                    norm_rope(ps[bank][:, 0:256], bank, 2, gqk_t[:, 128:256], rs, qkb[:, 0:256])
                    tb = 6 + (rr_state['tb'] % 2)
                    rr_state['tb'] += 1
                    psb = ps[tb][:].bitcast(BF16)
                    for g in range(2):
                        tr(psb[:, g * 128:(g + 1) * 128], qkb[:, g * 128:(g + 1) * 128], ['qkb'], [PS(tb)])
                    cp('act', KT[:, :, tile * 128:(tile + 1) * 128], psb[:, 0:256].rearrange("p (g t) -> p g t", g=2),
                       [], [PS(tb), 'KT'])

            def do_gates(n):
                for i in range(n):
                    bank = nbank()
                    for c in range(16):
                        mm(ps[bank][:, 0:16], xT[:, c, i * 128:(i + 1) * 128], wg[:, c, :], c == 0, c == 15,
                           ['xT', 'wg'], [PS(bank)])
                    stt(ig[:, i, :], ps[bank][:, 0:8], -0.5 * math.log(128.0), bgt[:, 0:8], ALU.add, ALU.add,
                        ['bgt'], [PS(bank), 'ig'])
                    tt('dve', sm[:, 32:40], ps[bank][:, 8:16], bgt[:, 8:16], ALU.add, ['bgt'], [PS(bank), 'sm'])
                    act(sm[:, 40:48], sm[:, 32:40], AF.Exp, ['sm'], ['sm'], scale=-1.0)
                    act(sm[:, 48:56], sm[:, 40:48], AF.Ln, ['sm'], ['sm'], bias=eps_rms[:, 1:2], scale=1.0)
                    ts('dve', lf[:, i, :], sm[:, 48:56], -1.0, 0.0, ALU.mult, ALU.add, ['sm'], ['lf'])

            def do_mk_tok(n, wslot):
                for i in range(n):
                    bank = nbank()
                    proj_tok(i, wslot, 0, 512, bank)
                    cp('act' if i % 2 == 0 else 'dve', mk_tok[:, i, :], ps[bank][:, 0:512], [], [PS(bank), 'mk_tok'])

            def do_mv(n, wslot, half):
                for i in range(n):
                    bank = nbank()
                    proj_tok(i, wslot, 0, 512, bank)
                    cp('dve' if i % 2 == 0 else 'act', mv_aug[:, i, half * 2:half * 2 + 2, 0:256],
                       ps[bank][:, 0:512].rearrange("p (h d) -> p h d", h=2), [], [PS(bank), 'mv_aug'])

            def do_feat(dst, dst_res, wslot):
                for j in range(4):
                    bank = nbank()
                    proj_feat(j, wslot, bank)
                    cp('act' if j % 2 == 0 else 'dve', dst[:, j, :], ps[bank][:, 0:512], [], [PS(bank), dst_res])

            def mlstm_tile(i, d, with_out, add_hbl):
                MASK = Umask if d == 0 else Lmask
                NEGM = NEGU if d == 0 else NEGL
                last = 127 if d == 0 else 0
                for h in range(4):
                    j = d * 4 + h
                    k = st['mi'] % 2
                    st['mi'] += 1
                    bX, bY, bZ = (0, 1, 2) if k == 0 else (3, 4, 5)
                    fcol = lf[:, i, j:j + 1]
                    icol = ig[:, i, j:j + 1]
                    ts('dve', Fm[k][:], MASK, fcol, 0.0, ALU.mult, ALU.add, ['cst', 'lf'], [('Fm', k)])
                    mm(ps[bX][:, 0:128], ones_f, Fm[k][:], True, True, ['cst', ('Fm', k)], [PS(bX)])
                    if with_out:
                        mm(ps[bX][:, 128:256], ones_f, Fm[k][:], True, False, ['cst', ('Fm', k)], [PS(bX)])
                        mm(ps[bX][:, 128:256], ident_f, NEGM, False, True, ['cst'], [PS(bX)])
                    mm(ps[bX][:, 256:257], MASK, fcol, True, True, ['cst', 'lf'], [PS(bX)])
                    tt('dve', mc[k][:, 0:1], icol, ps[bX][:, 256:257], ALU.subtract, ['ig'], [PS(bX), ('mc', k)])
                    act(mc[k][:, 1:2], ps[bX][:, last:last + 1], AF.Exp, [('mc', k)], [PS(bX), ('mc', k)], bias=mc[k][:, 0:1], scale=1.0)
                    act(mc[k][:, 2:3], ps[bX][:, last:last + 1], AF.Exp, [('mc', k)], [PS(bX), ('mc', k)])
                    if with_out:
                        act(EB[k][:], ps[bX][:, 0:128], AF.Exp, [], [PS(bX), ('EB', k)])
                        act(DT[k][:], ps[bX][:, 128:256], AF.Exp, [('mc', k)], [PS(bX), ('DT', k)], bias=mc[k][:, 0:1], scale=1.0)
                        mm(ps[bX][:, 384:512], mkT[:, h, i * 128:(i + 1) * 128], mqT[:, h, i * 128:(i + 1) * 128],
                           True, True, ['mkT', 'mqT'], [PS(bX)])
                        tt('dve', PTm[k][:], ps[bX][:, 384:512], DT[k][:], ALU.mult, [('DT', k)], [PS(bX), ('PTm', k)])
                        tt('dve', qsT[k][:], mqT[:, h, i * 128:(i + 1) * 128], EB[k][:], ALU.mult, ['mqT', ('EB', k)], [('qsT', k)])
                        mm(ps[bY][:, 0:257], PTm[k][:], mv_aug[:, i, h, 0:257], True, False, [('PTm', k), 'mv_aug'], [PS(bY)])
                        mm(ps[bY][:, 0:257], qsT[k][:], Cb[d][:, h, 0:257], False, True, [('qsT', k), ('Cb', d, h)], [PS(bY)])
                        act(mc[k][:, 3:4], ps[bY][:, 256:257], AF.Abs, [('mc', k)], [PS(bY), ('mc', k)])
                        ts('dve', mc[k][:, 4:5], mc[k][:, 3:4], 1.0, 0.0, ALU.max, ALU.add, [('mc', k)], [('mc', k)])
                        recip(mc[k][:, 5:6], mc[k][:, 4:5], [('mc', k)], [('mc', k)])
                        if add_hbl:
                            stt(hout[:, h * 256:(h + 1) * 256], ps[bY][:, 0:256], mc[k][:, 5:6], hbl[:, h * 256:(h + 1) * 256],
                                ALU.mult, ALU.add, [('mc', k), 'hbl'], [PS(bY), ('hout', h)])
                        else:
                            ts('dve', hout[:, h * 256:(h + 1) * 256], ps[bY][:, 0:256], mc[k][:, 5:6], 0.0, ALU.mult, ALU.add,
                               [('mc', k)], [PS(bY), ('hout', h)])
                    ts('dve', kw[k][:], mk_tok[:, i, h * 128:(h + 1) * 128], mc[k][:, 1:2], 0.0, ALU.mult, ALU.add,
                       ['mk_tok', ('mc', k)], [('kw', k)])
                    mm(ps[bZ][:, 0:257], kw[k][:], mv_aug[:, i, h, 0:257], True, True, [('kw', k), 'mv_aug'], [PS(bZ)])
                    stt(Cf[d][:, h, :], Cf[d][:, h, :], mc[k][:, 2:3], ps[bZ][:, 0:257], ALU.mult, ALU.add,
                        [('mc', k)], [PS(bZ), ('Cf', d, h)])
                    cp('act', Cb[d][:, h, 0:257], Cf[d][:, h, :], [('Cf', d, h)], [('Cb', d, h)])

            for grp in range(7, 3, -1):
                tiles = [grp * 4 + i for i in range(4)]
                load_group(tiles)
                cast_w(6)
                ws = load_w(1024, 512)
                do_kv(tiles, ws)
                do_gates(4)
                ws = load_w(2048, 512)
                do_mk_tok(4, ws)
                ws = load_w(2560, 512)
                do_mv(4, ws, 0)
                ws = load_w(3072, 512)
                do_mv(4, ws, 1)
                for i in range(3, -1, -1):
                    mlstm_tile(i, 1, False, False)

            for grp in range(3, -1, -1):
                tiles = [grp * 4 + i for i in range(4)]
                load_group(tiles)
                cast_w(6)
                ws = load_w(1024, 512)
                do_kv(tiles, ws)
                do_gates(4)
                ws = load_w(1536, 512)
                do_feat(mqT, 'mqT', ws)
                ws = load_w(2048, 512)
                do_feat(mkT, 'mkT', ws)
                do_mk_tok(4, ws)
                ws = load_w(2560, 512)
                do_mv(4, ws, 0)
                ws = load_w(3072, 512)
                do_mv(4, ws, 1)
                for i in range(3, -1, -1):
                    mlstm_tile(i, 1, True, False)
                    tile = tiles[i]
                    P.dma('sp', hb_d[tile * 128:(tile + 1) * 128, :], hout[:], reads=[('hout', 0), ('hout', 1), ('hout', 2), ('hout', 3)], writes=[('hb', tile)],
                          semkey='hout_st')

            for grp in range(4):
                tiles = [grp * 4 + i for i in range(4)]
                load_group(tiles)
                cast_w(6)
                for blk in range(2):
                    ws = load_w(blk * 512, 512)
                    for i, tile in enumerate(tiles):
                        bank = nbank()
                        proj_tok(i, ws, 0, 512, bank)
                        rs = load_rope(tile)
                        norm_rope(ps[bank][:, 0:512], bank, 4, gqk_t[:, 0:128], rs, qkb[:, 0:512])
                        tb = 6 + (rr_state['tb'] % 2)
                        rr_state['tb'] += 1
                        psb = ps[tb][:].bitcast(BF16)
                        for hh in range(4):
                            tr(psb[:, hh * 128:(hh + 1) * 128], qkb[:, hh * 128:(hh + 1) * 128], ['qkb'], [PS(tb)])
                        cp('act', QT[:, blk * 4:(blk + 1) * 4, i * 128:(i + 1) * 128],
                           psb[:, 0:512].rearrange("p (h t) -> p h t", h=4), [], [PS(tb), 'QT'])
                do_gates(4)
                ws = load_w(1536, 512)
                do_feat(mqT, 'mqT', ws)
                ws = load_w(2048, 512)
                do_feat(mkT, 'mkT', ws)
                do_mk_tok(4, ws)
                ws = load_w(2560, 512)
                do_mv(4, ws, 0)
                ws = load_w(3072, 512)
                do_mv(4, ws, 1)
                for blk in range(2):
                    ws = load_w(3584 + blk * 512, 512)
                    for i in range(4):
                        bank = nbank()
                        proj_tok(i, ws, 0, 512, bank)
                        act(mo_sig[:, i, blk * 512:(blk + 1) * 512], ps[bank][:, 0:512], AF.Sigmoid, [], [PS(bank), 'mo_sig'])
                its = [(hq, kt) for hq in range(8) for kt in range(32)]

                def issue_S(n):
                    hq, kt = its[n]
                    g = hq // 4
                    bS = 4 + (n % 2)
                    pt = n % 3
                    mm(ps[bS][:, 0:512], KT[:, g, kt * 128:(kt + 1) * 128], QT[:, hq, :], True, True, ['KT', 'QT'], [PS(bS)])
                    act(PTb[pt][:], ps[bS][:, 0:512], AF.Exp, [], [PS(bS), ('PTb', pt)], scale=128.0 ** -0.5)

                def issue_PV(n):
                    hq, kt = its[n]
                    g = hq // 4
                    pt = n % 3
                    for qs in range(4):
                        mm(ps[qs][:, 0:129], PTb[pt][:, qs * 128:(qs + 1) * 128], VA[:, kt, g, 0:129], kt == 0, kt == 31,
                           [('PTb', pt), 'VA'], [PS(qs)])
                    if kt == 31:
                        for qs in range(4):
                            recip(sm[:, 56 + qs:57 + qs], ps[qs][:, 128:129], [], [PS(qs), 'sm'])
                            ts('dve', att_g[:, qs, hq * 128:(hq + 1) * 128], ps[qs][:, 0:128], sm[:, 56 + qs:57 + qs], 0.0,
                               ALU.mult, ALU.add, ['sm'], [PS(qs), 'att_g'])

                issue_S(0)
                for n in range(len(its)):
                    if n + 1 < len(its):
                        issue_S(n + 1)
                    issue_PV(n)
                for i, tile in enumerate(tiles):
                    P.dma('sp', hbl[:], hb_d[tile * 128:(tile + 1) * 128, :], reads=[('hb', tile)], writes=['hbl'])
                    mlstm_tile(i, 0, True, True)
                    tt('dve', t0[:], hout[:], hout[:], ALU.mult, [('hout', 0), ('hout', 1), ('hout', 2), ('hout', 3)], ['t0'])
                    red(sm[:, 0:4], t0[:].rearrange("p (h d) -> p h d", h=4), ['t0'], ['sm'])
                    act(sm[:, 8:12], sm[:, 0:4], AF.Sqrt, ['sm'], ['sm'], bias=eps_rms[:, 0:1], scale=1.0 / 256.0)
                    recip(sm[:, 16:20], sm[:, 8:12], ['sm'], ['sm'])
                    tt('dve', t0[:].rearrange("p (h d) -> p h d", h=4), hout[:].rearrange("p (h d) -> p h d", h=4),
                       sm[:, 16:20].unsqueeze(2).to_broadcast([128, 4, 256]), ALU.mult, [('hout', 0), ('hout', 1), ('hout', 2), ('hout', 3)] + ['sm'], ['t0'])
                    tt('pool', t1[:], t0[:], mln_t[:], ALU.mult, ['t0', 'mln_t'], ['t1'])
                    ms = i % 2
                    tt('pool', mlb[ms][:], t1[:], mo_sig[:, i, :], ALU.mult, ['t1', 'mo_sig'], [('mlb', ms)])
                    P.dma('sp', cat_d[tile * 128:(tile + 1) * 128, 1024:2048], mlb[ms][:], reads=[('mlb', ms)],
                          writes=[('cat_ml', tile)], semkey=('mlb_st', ms))
                    P.dma('sp', cat_d[tile * 128:(tile + 1) * 128, 0:1024], att_g[:, i, :], reads=['att_g'],
                          writes=[('cat_att', tile)], semkey='att_st')
            P.barrier()

        if stage == 'mix':
            with contextlib.ExitStack() as Dg:
                cb = sbuf(Dg, "dbg_cb", [128, 2048], BF16)
                for tile in range(16):
                    P.dma('sp', cb[:], cat_d[tile * 128:(tile + 1) * 128, :], reads=[('cat_ml', tile), ('cat_att', tile)], writes=['dbg_cb'])
                    final.append(P.dma('sp', out_d[tile * 128:(tile + 1) * 128, :], cb[:], reads=['dbg_cb'], writes=[('out', tile)],
                                       semkey='dbg_out'))
            P.emit(final_tokens=final[-1:])
            return nc

        with contextlib.ExitStack() as E:
            xbE = [sbuf(E, "xbE%d" % i, [128, 2048], BF16) for i in range(2)]
            aT = sbuf(E, "aT", [128, 16, 512], BF16)
            wbD = [sbuf(E, "wbD%d" % i, [128, 16, 512], BF16) for i in range(2)]
            r_g = sbuf(E, "r_g", [128, 4, 2048], F32)
            ln_g = sbuf(E, "ln_g", [128, 2048], F32)
            ln_b = sbuf(E, "ln_b", [128, 2048], F32)
            st6 = sbuf(E, "st6", [128, 4, 6], F32)
            lmv = sbuf(E, "lmv", [128, 8], F32)
            eps_ln = sbuf(E, "eps_ln", [128, 1], F32)
            memT = sbuf(E, "memT", [128, 16, 256], BF16)
            kmT = sbuf(E, "kmT", [128, 16, 256], BF16)
            vm = sbuf(E, "vm", [128, 2, 2048], BF16)
            q1T = sbuf(E, "q1T", [128, 16, 512], BF16)
            o_g = sbuf(E, "o_g", [128, 4, 2048], BF16)
            PTx = [sbuf(E, "PTx%d" % i, [128, 512], BF16) for i in range(4)]
            rrx = sbuf(E, "rrx", [128, 8], F32)
            memset('pool', eps_ln[:], LN_EPS, ['eps_ln'])
            sE = {'w': 0, 'x': 0, 'bank': 0, 'pt': 0}

            def load_wE(W, col0, bf=False):
                slot = sE['w'] % 2
                sE['w'] += 1
                Wv = W.rearrange("(c p) n -> p c n", p=128)
                for q in range(4):
                    P.dma('sp' if bf else 'pool', wbD[slot][:, q * 4:(q + 1) * 4, :], Wv[:, q * 4:(q + 1) * 4, col0:col0 + 512],
                          writes=[('wbD', slot)], semkey=('wbD', slot, bf))
                return slot

            def nbE():
                b = sE['bank'] % 6
                sE['bank'] += 1
                return b

            def load_T(src_rows, dst, dst_res, col0, cast, extra_reads=()):
                slot = sE['x'] % 2
                sE['x'] += 1
                P.dma('pool' if cast else 'sp', xbE[slot][:], src_rows, reads=list(extra_reads), writes=[('xbE', slot)])
                transpose16(xbE[slot], ('xbE', slot), dst, dst_res, col0)

            def load_ln(k):
                P.dma('sp', ln_g[:], lnp[2 * k:2 * k + 1, :].partition_broadcast(128), writes=['ln_g'])
                P.dma('sp', ln_b[:], lnp[2 * k + 1:2 * k + 2, :].partition_broadcast(128), writes=['ln_b'])

            def dense_ln(W, tiles, out_d_, out_key, bf=True):
                for cbk in range(4):
                    ws = load_wE(W, cbk * 512, bf)
                    for i in range(4):
                        bank = nbE()
                        for c in range(16):
                            mm(ps[bank][:, 0:512], aT[:, c, i * 128:(i + 1) * 128], wbD[ws][:, c, :], c == 0, c == 15,
                               ['aT', ('wbD', ws)], [PS(bank)])
                        stt(r_g[:, i, cbk * 512:(cbk + 1) * 512], r_g[:, i, cbk * 512:(cbk + 1) * 512], ALPHA,
                            ps[bank][:, 0:512], ALU.mult, ALU.add, [], [PS(bank), ('r_g', i)])
                for i, tile in enumerate(tiles):
                    for q in range(4):
                        op_bnstats(st6[:, q, :], r_g[:, i, q * 512:(q + 1) * 512], [('r_g', i)], ['st6'])
                    op_bnaggr(lmv[:, 0:2], st6[:].rearrange("p a b -> p (a b)"), ['st6'], ['lmv'])
                    act(lmv[:, 2:3], lmv[:, 1:2], AF.Sqrt, ['lmv', 'eps_ln'], ['lmv'], bias=eps_ln[:, 0:1], scale=1.0)
                    recip(lmv[:, 3:4], lmv[:, 2:3], ['lmv'], ['lmv'])
                    ts('dve', r_g[:, i, :], r_g[:, i, :], lmv[:, 0:1], lmv[:, 3:4], ALU.subtract, ALU.mult, ['lmv'], [('r_g', i)])
                    tt('pool', r_g[:, i, :], r_g[:, i, :], ln_g[:], ALU.mult, ['ln_g'], [('r_g', i)])
                    tt('pool', r_g[:, i, :], r_g[:, i, :], ln_b[:], ALU.add, ['ln_b'], [('r_g', i)])
                    tk = P.dma('sp', out_d_[tile * 128:(tile + 1) * 128, :], r_g[:, i, :], reads=[('r_g', i)],
                               writes=[(out_key, tile)], semkey=('r_g_st', i))
                    if out_key == 'out':
                        final.append(tk)

            load_ln(0)
            for grp in range(4):
                tiles = [grp * 4 + i for i in range(4)]
                for i, tile in enumerate(tiles):
                    P.dma('sp', r_g[:, i, :], xl[tile * 128:(tile + 1) * 128, :], writes=[('r_g', i)])
                    load_T(cat_d[tile * 128:(tile + 1) * 128, :], aT, 'aT', i * 128, False,
                           extra_reads=[('cat_ml', tile), ('cat_att', tile)])
                dense_ln(w_out_b, tiles, x1_d, 'x1')

            load_ln(1)
            for mt in range(2):
                load_T(memb[mt * 128:(mt + 1) * 128, :], memT, 'memT', mt * 128, True)
            for cbk in range(4):
                ws = load_wE(xa_wk, cbk * 512)
                for j in range(4):
                    bank = nbE()
                    for c in range(16):
                        mm(ps[bank][:, 0:256], wbD[ws][:, c, j * 128:(j + 1) * 128], memT[:, c, :], c == 0, c == 15,
                           ['memT', ('wbD', ws)], [PS(bank)])
                    cp('act' if j % 2 == 0 else 'dve', kmT[:, cbk * 4 + j, :], ps[bank][:, 0:256], [], [PS(bank), 'kmT'])
            for cbk in range(4):
                ws = load_wE(xa_wv, cbk * 512)
                for mt in range(2):
                    bank = nbE()
                    for c in range(16):
                        mm(ps[bank][:, 0:512], memT[:, c, mt * 128:(mt + 1) * 128], wbD[ws][:, c, :], c == 0, c == 15,
                           ['memT', ('wbD', ws)], [PS(bank)])
                    cp('act' if mt % 2 == 0 else 'dve', vm[:, mt, cbk * 512:(cbk + 1) * 512], ps[bank][:, 0:512], [], [PS(bank), 'vm'])
            for grp in range(4):
                tiles = [grp * 4 + i for i in range(4)]
                for i, tile in enumerate(tiles):
                    P.dma('sp', r_g[:, i, :], x1_d[tile * 128:(tile + 1) * 128, :], reads=[('x1', tile)], writes=[('r_g', i)])
                    load_T(x1_d[tile * 128:(tile + 1) * 128, :], aT, 'aT', i * 128, True, extra_reads=[('x1', tile)])
                for cbk in range(4):
                    ws = load_wE(xa_wq_b, cbk * 512, True)
                    for j in range(4):
                        bank = nbE()
                        for c in range(16):
                            mm(ps[bank][:, 0:512], wbD[ws][:, c, j * 128:(j + 1) * 128], aT[:, c, :], c == 0, c == 15,
                               ['aT', ('wbD', ws)], [PS(bank)])
                        cp('act' if j % 2 == 0 else 'dve', q1T[:, cbk * 4 + j, :], ps[bank][:, 0:512], [], [PS(bank), 'q1T'])
                for h in range(4):
                    pts = []
                    for mt in range(2):
                        bank = nbE()
                        for dc in range(4):
                            mm(ps[bank][:, 0:512], kmT[:, h * 4 + dc, mt * 128:(mt + 1) * 128], q1T[:, h * 4 + dc, :],
                               dc == 0, dc == 3, ['kmT', 'q1T'], [PS(bank)])
                        pt = sE['pt'] % 4
                        sE['pt'] += 1
                        act(PTx[pt][:], ps[bank][:, 0:512], AF.Exp, [], [PS(bank), ('PTx', pt)], scale=512.0 ** -0.5)
                        pts.append(pt)
                    for qs in range(4):
                        bank = nbE()
                        for mt in range(2):
                            mm(ps[bank][:, 0:512], PTx[pts[mt]][:, qs * 128:(qs + 1) * 128], vm[:, mt, h * 512:(h + 1) * 512],
                               mt == 0, mt == 1, [('PTx', pts[mt]), 'vm'], [PS(bank)])
                        b2 = 6 + (qs % 2)
                        for mt in range(2):
                            mm(ps[b2][:, 0:1], PTx[pts[mt]][:, qs * 128:(qs + 1) * 128], ones_b[:, 0:1],
                               mt == 0, mt == 1, [('PTx', pts[mt]), 'ones_b'], [PS(b2)])
                        recip(rrx[:, qs:qs + 1], ps[b2][:, 0:1], [], [PS(b2), 'rrx'])
                        ts('dve', o_g[:, qs, h * 512:(h + 1) * 512], ps[bank][:, 0:512], rrx[:, qs:qs + 1], 0.0, ALU.mult, ALU.add,
                           ['rrx'], [PS(bank), ('o_g', qs)])
                for i in range(4):
                    transpose16(o_g[:, i, :], ('o_g', i), aT, 'aT', i * 128)
                dense_ln(xa_wo_b, tiles, x2_d, 'x2')
            P.barrier()

        if stage == 'de':
            with contextlib.ExitStack() as Dg:
                cb2 = sbuf(Dg, "dbg_cb2", [128, 2048], F32)
                for tile in range(16):
                    P.dma('sp', cb2[:], x2_d[tile * 128:(tile + 1) * 128, :], reads=[('x2', tile)], writes=['dbg_cb2'])
                    final.append(P.dma('sp', out_d[tile * 128:(tile + 1) * 128, :], cb2[:], reads=['dbg_cb2'], writes=[('out', tile)],
                                       semkey='dbg_out'))
            P.emit(final_tokens=final[-1:])
            return nc

        Wd4 = wd_d.rearrange("s p (j t) -> s p j t", j=128)
        with contextlib.ExitStack() as F1:
            xbF = [sbuf(F1, "xbF%d" % i, [128, 2048], BF16) for i in range(2)]
            x2Ta = sbuf(F1, "x2Ta", [128, 16, 256], BF16)
            wbF = [sbuf(F1, "wbF%d" % i, [128, 16, 128], BF16) for i in range(2)]
            qpT = sbuf(F1, "qpT", [128, 16, 256], F32)
            skS = sbuf(F1, "skS", [128, 16 * 128], F32)
            s_sb = sbuf(F1, "s_sb", [128, 16, 128], F32)
            s2 = sbuf(F1, "s2", [128, 256], F32)
            vals = sbuf(F1, "vals", [128, 16, 16], F32)
            idx = sbuf(F1, "idx", [128, 16, 16], U32)
            idxf = sbuf(F1, "idxf", [128, 16, 16], F32)
            cand = sbuf(F1, "cand", [128, 8, 256], F32)
            cv = sbuf(F1, "cv", [128, 8, 16], F32)
            cpos = sbuf(F1, "cpos", [128, 8, 16], U32)
            rk = sbuf(F1, "rk", [128, 2, 128], U32)
            rkf = sbuf(F1, "rkf", [128, 2, 128], F32)
            oh = sbuf(F1, "oh", [128, 8, 16, 16], F32)
            sel = sbuf(F1, "sel", [128, 3, 128], F32)
            selT = sbuf(F1, "selT", [128, 3, 128], F32)
            gz = sbuf(F1, "gz", [128, 16], F32)
            OA = [sbuf(F1, "OA%d" % i, [128, 16, 128], BF16) for i in range(2)]
            OB = [sbuf(F1, "OB%d" % i, [128, 16, 128], BF16) for i in range(2)]
            Wt = [sbuf(F1, "Wt%d" % i, [128, 128, 128], BF16) for i in range(2)]
            P.dma('sp', skS[:], skT, writes=['skS'])
            pwq_v = pwq_b.rearrange("(c p) n -> p c n", p=128)
            xcnt = 0
            for grp in range(8):
                tiles = [grp * 2, grp * 2 + 1]
                for i, tile in enumerate(tiles):
                    xs = xcnt % 2
                    xcnt += 1
                    P.dma('pool', xbF[xs][:], x2_d[tile * 128:(tile + 1) * 128, :], writes=[('xbF', xs)])
                    transpose16(xbF[xs], ('xbF', xs), x2Ta, 'x2Ta', i * 128)
                for hc in range(16):
                    ws = hc % 2
                    for q in range(2):
                        P.dma('sp', wbF[ws][:, q * 8:(q + 1) * 8, :], pwq_v[:, q * 8:(q + 1) * 8, hc * 128:(hc + 1) * 128],
                              writes=[('wbF', ws)])
                    bank = hc % 4
                    for c in range(16):
                        mm(ps[bank][:, 0:256], wbF[ws][:, c, :], x2Ta[:, c, :], c == 0, c == 15, ['x2Ta', ('wbF', ws)], [PS(bank)])
                    cp('act' if hc % 2 == 0 else 'dve', qpT[:, hc, :], ps[bank][:, 0:256], [], [PS(bank), 'qpT'])
                for t in range(2):
                    tile = tiles[t]
                    wsl = tile % 2
                    for q4 in range(4):
                        bank = q4
                        for r4 in range(4):
                            hc = q4 * 4 + r4
                            mm(ps[bank][:, r4 * 128:(r4 + 1) * 128], qpT[:, hc, t * 128:(t + 1) * 128], skS[:, hc * 128:(hc + 1) * 128],
                               True, True, ['qpT', 'skS'], [PS(bank)])
                        cp('act', s_sb[:, q4 * 4:(q4 + 1) * 4, :],
                           ps[bank][:, 0:512].rearrange("p (a k) -> p a k", a=4), [], [PS(bank), 's_sb'])
                    for hc in range(16):
                        op_max(vals[:, hc, 0:8], s_sb[:, hc, :], ['s_sb'], ['vals'])
                        op_maxidx(idx[:, hc, 0:8], vals[:, hc, 0:8], s_sb[:, hc, :], ['s_sb', 'vals'], ['idx'])
                        op_mrep(s2[:, 0:128], vals[:, hc, 0:8], s_sb[:, hc, :], ['s_sb', 'vals'], ['s2'])
                        op_max(vals[:, hc, 8:16], s2[:, 0:128], ['s2'], ['vals'])
                        op_maxidx(idx[:, hc, 8:16], vals[:, hc, 8:16], s2[:, 0:128], ['s2', 'vals'], ['idx'])
                    cp('dve', idxf[:], idx[:], ['idx'], ['idxf'])
                    vv = vals[:].rearrange("p (h c) a -> p h c a", c=2)
                    tt('dve', cand[:].rearrange("p h (a b) -> p h a b", a=16), vv[:, :, 0, :].unsqueeze(3).to_broadcast([128, 8, 16, 16]),
                       vv[:, :, 1, :].unsqueeze(2).to_broadcast([128, 8, 16, 16]), ALU.add, ['vals'], ['cand'])
                    for h in range(8):
                        op_max(cv[:, h, 0:8], cand[:, h, :], ['cand'], ['cv'])
                        op_maxidx(cpos[:, h, 0:8], cv[:, h, 0:8], cand[:, h, :], ['cand', 'cv'], ['cpos'])
                        op_mrep(s2[:], cv[:, h, 0:8], cand[:, h, :], ['cand', 'cv'], ['s2'])
                        op_max(cv[:, h, 8:16], s2[:], ['s2'], ['cv'])
                        op_maxidx(cpos[:, h, 8:16], cv[:, h, 8:16], s2[:], ['s2', 'cv'], ['cpos'])
                    g3 = sel[:, 2, :].rearrange("p (h c) -> p h c", h=8)
                    tt('dve', g3, cv[:], cv[:, :, 0:1].to_broadcast([128, 8, 16]), ALU.subtract, ['cv'], ['sel'])
                    act(g3, g3, AF.Exp, [], ['sel'])
                    red(gz[:, 0:8], g3, ['sel'], ['gz'])
                    recip(gz[:, 8:16], gz[:, 0:8], ['gz'], ['gz'])
                    tt('dve', g3, g3, gz[:, 8:16].unsqueeze(2).to_broadcast([128, 8, 16]), ALU.mult, ['gz'], ['sel'])
                    cpf = cpos[:].rearrange("p h c -> p (h c)")
                    op_tss(rk[:, 0, :], cpf, 4, ALU.logical_shift_right, ['cpos'], ['rk'])
                    op_tss(rk[:, 1, :], cpf, 15, ALU.bitwise_and, ['cpos'], ['rk'])
                    cp('dve', rkf[:], rk[:], ['rk'], ['rkf'])
                    idv = idxf[:].rearrange("p (h c) a -> p h c a", c=2)
                    for half in range(2):
                        rv = rkf[:, half, :].rearrange("p (h c) -> p h c", h=8)
                        tt('dve', oh[:], rv.unsqueeze(3).to_broadcast([128, 8, 16, 16]),
                           iota_f[:, 0:16].unsqueeze(1).unsqueeze(1).to_broadcast([128, 8, 16, 16]), ALU.is_equal, ['rkf', 'cst'], ['oh'])
                        tt('dve', oh[:], oh[:], idv[:, :, half, :].unsqueeze(2).to_broadcast([128, 8, 16, 16]), ALU.mult, ['idxf'], ['oh'])
                        red(sel[:, half, :].rearrange("p (h c) -> p h c", h=8), oh[:], ['oh'], ['sel'])
                    for q3 in range(3):
                        tr(ps[4][:, q3 * 128:(q3 + 1) * 128], sel[:, q3, :], ['sel'], [PS(4)], fp32=True)
                    cp('act', selT[:], ps[4][:, 0:384].rearrange("p (a t) -> p a t", a=3), [], [PS(4), 'selT'])
                    for tc in range(8):
                        k = tc % 2
                        tok0 = tc * 16
                        io = iota_f.unsqueeze(1).to_broadcast([128, 16, 128])
                        tt('dve', OA[k][:], io, selT[:, 0, tok0:tok0 + 16].unsqueeze(2).to_broadcast([128, 16, 128]), ALU.is_equal,
                           ['cst', 'selT'], [('OA', k)])
                        tt('dve', OA[k][:], OA[k][:], selT[:, 2, tok0:tok0 + 16].unsqueeze(2).to_broadcast([128, 16, 128]), ALU.mult,
                           ['selT'], [('OA', k)])
                        tt('dve', OB[k][:], io, selT[:, 1, tok0:tok0 + 16].unsqueeze(2).to_broadcast([128, 16, 128]), ALU.is_equal,
                           ['cst', 'selT'], [('OB', k)])
                        for q4 in range(4):
                            bank = 5 + ((tc * 4 + q4) % 3)
                            for r4 in range(4):
                                tl = q4 * 4 + r4
                                mm(ps[bank][:, r4 * 128:(r4 + 1) * 128], OA[k][:, tl, :], OB[k][:, tl, :], True, True,
                                   [('OA', k), ('OB', k)], [PS(bank)])
                            a0 = tok0 + q4 * 4
                            cp('act', Wt[wsl][:, :, a0:a0 + 4],
                               ps[bank][:, 0:512].rearrange("p (t j) -> p j t", t=4), [], [PS(bank), ('Wt', wsl)])
                    P.dma('sp', wd_d[tile], Wt[wsl][:].rearrange("p j t -> p (j t)"), reads=[('Wt', wsl)], writes=[('wd', tile)],
                          semkey=('Wt_st', wsl))
            P.barrier(new_epoch=False)

        for pp in range(2):
            with contextlib.ExitStack() as P2:
                pn = "p%d_" % pp
                x2T = sbuf(P2, pn + "x2T", [128, 16, 1024], BF16)
                acc_sb = sbuf(P2, pn + "acc", [128, 8, 2048], F32)
                with contextlib.ExitStack() as F2:
                    JB = 4
                    NS = 6
                    ub = [sbuf(F2, pn + "ub%d" % i, [128, 16, 128], BF16) for i in range(NS)]
                    vb = [sbuf(F2, pn + "vb%d" % i, [128, 2048], BF16) for i in range(NS)]
                    Wj4 = [sbuf(F2, pn + "Wj4%d" % i, [128, 8, 4, 128], BF16) for i in range(2)]
                    ga = [sbuf(F2, pn + "ga%d" % i, [128, 512], F32) for i in range(2)]
                    aTj = [sbuf(F2, pn + "aTj%d" % i, [128, 1024], BF16) for i in range(2 * JB)]
                    for s8 in range(8):
                        tile = pp * 8 + s8
                        P.dma('pool', vb[s8 % 2][:], x2_d[tile * 128:(tile + 1) * 128, :], writes=[('vb', s8 % 2)])
                        transpose16(vb[s8 % 2], ('vb', s8 % 2), x2T, 'x2T', s8 * 128)
                    accset = 0
                    gcnt = 0
                    for jb in range(128 // JB):
                        wq4 = jb % 2
                        j0 = jb * JB
                        for sh in range(2):
                            P.dma('sp', Wj4[wq4][:, sh * 4:(sh + 1) * 4, :, :],
                                  Wd4[pp * 8 + sh * 4:pp * 8 + sh * 4 + 4, :, j0:j0 + 4, :].rearrange("s p j t -> p s j t"),
                                  writes=[('Wj4', wq4)])
                        for jj in range(JB):
                            j = jb * JB + jj
                            sl = j % NS
                            sa = (jb % 2) * JB + jj
                            P.dma('pool', ub[sl][:], uh[j].rearrange("p (c i) -> p c i", c=16), writes=[('ub', sl)])
                            P.dma('pool', vb[sl][:], vh[j], writes=[('vb', sl)])
                            for half in range(2):
                                bank = (j % 2) * 2 + half
                                for c in range(16):
                                    mm(ps[bank][:, 0:512], ub[sl][:, c, :], x2T[:, c, half * 512:(half + 1) * 512], c == 0, c == 15,
                                       [('ub', sl), 'x2T'], [PS(bank)])
                                gs = gcnt % 2
                                gcnt += 1
                                act(ga[gs][:], ps[bank][:, 0:512], AF.Gelu, [], [PS(bank), ('ga', gs)])
                                tt('dve', aTj[sa][:, half * 512:(half + 1) * 512].rearrange("p (s t) -> p s t", s=4),
                                   ga[gs][:].rearrange("p (s t) -> p s t", s=4), Wj4[wq4][:, half * 4:(half + 1) * 4, jj, :], ALU.mult,
                                   [('ga', gs), ('Wj4', wq4)], [('aTj', sa)])
                        for s8 in range(8):
                            for cpair in range(2):
                                b0 = 4 + (accset % 2) * 2
                                accset += 1
                                for jj in range(JB):
                                    sl = (jb * JB + jj) % NS
                                    sa = (jb % 2) * JB + jj
                                    for cbk in range(2):
                                        col0 = (cpair * 2 + cbk) * 512
                                        mm(ps[b0 + cbk][:, 0:512], aTj[sa][:, s8 * 128:(s8 + 1) * 128], vb[sl][:, col0:col0 + 512],
                                           jj == 0, jj == JB - 1, [('aTj', sa), ('vb', sl)], [PS(b0 + cbk)])
                                for cbk in range(2):
                                    col0 = (cpair * 2 + cbk) * 512
                                    if jb == 0:
                                        cp('dve', acc_sb[:, s8, col0:col0 + 512], ps[b0 + cbk][:, 0:512], [], [PS(b0 + cbk), ('acc', s8)])
                                    else:
                                        tt('dve', acc_sb[:, s8, col0:col0 + 512], acc_sb[:, s8, col0:col0 + 512], ps[b0 + cbk][:, 0:512], ALU.add,
                                           [], [PS(b0 + cbk), ('acc', s8)])
                    P.barrier(new_epoch=False)
                with contextlib.ExitStack() as F3:
                    rF = [sbuf(F3, pn + "rF%d" % i, [128, 2048], F32) for i in range(2)]
                    lg = sbuf(F3, pn + "lg", [128, 2048], F32)
                    lb = sbuf(F3, pn + "lb", [128, 2048], F32)
                    st6f = sbuf(F3, pn + "st6f", [128, 4, 6], F32)
                    lmvf = sbuf(F3, pn + "lmvf", [128, 8], F32)
                    epsf = sbuf(F3, pn + "epsf", [128, 1], F32)
                    memset('pool', epsf[:], LN_EPS, ['epsf'])
                    P.dma('sp', lg[:], lnp[4:5, :].partition_broadcast(128), writes=['lg'])
                    P.dma('sp', lb[:], lnp[5:6, :].partition_broadcast(128), writes=['lb'])
                    for s8 in range(8):
                        tile = pp * 8 + s8
                        r_ = rF[s8 % 2]
                        rk_ = ('rF', s8 % 2)
                        P.dma('sp', r_[:], x2_d[tile * 128:(tile + 1) * 128, :], writes=[rk_])
                        stt(r_[:], r_[:], ALPHA, acc_sb[:, s8, :], ALU.mult, ALU.add, [('acc', s8)], [rk_])
                        for q in range(4):
                            op_bnstats(st6f[:, q, :], r_[:, q * 512:(q + 1) * 512], [rk_], ['st6f'])
                        op_bnaggr(lmvf[:, 0:2], st6f[:].rearrange("p a b -> p (a b)"), ['st6f'], ['lmvf'])
                        act(lmvf[:, 2:3], lmvf[:, 1:2], AF.Sqrt, ['lmvf', 'epsf'], ['lmvf'], bias=epsf[:, 0:1], scale=1.0)
                        recip(lmvf[:, 3:4], lmvf[:, 2:3], ['lmvf'], ['lmvf'])
                        ts('dve', r_[:], r_[:], lmvf[:, 0:1], lmvf[:, 3:4], ALU.subtract, ALU.mult, ['lmvf'], [rk_])
                        tt('pool', r_[:], r_[:], lg[:], ALU.mult, ['lg'], [rk_])
                        tt('pool', r_[:], r_[:], lb[:], ALU.add, ['lb'], [rk_])
                        final.append(P.dma('sp', out_d[tile * 128:(tile + 1) * 128, :], r_[:], reads=[rk_],
                                           writes=[('out', tile)], semkey=('rF_st', s8 % 2)))
                    P.barrier(new_epoch=False)
        P.emit(final_tokens=final)
        return nc
    return nc


def _consts():
    c = np.zeros((128, 1024), np.float32)
    idx = np.arange(128)
    c[:, 0:128] = np.eye(128, dtype=np.float32)
    c[:, 128:256] = (idx[:, None] <= idx[None, :]).astype(np.float32)
    c[:, 256:384] = (idx[:, None] >= idx[None, :]).astype(np.float32)
    c[:, 384:512] = np.where(idx[:, None] <= idx[None, :], 0.0, NEG)
    c[:, 512:640] = np.where(idx[:, None] >= idx[None, :], 0.0, NEG)
    c[:, 640:768] = 1.0
    c[:, 768:896] = idx[None, :].astype(np.float32)
    return c


def _rope_table(pos):
    pos = np.asarray(pos)
    row = (pos // 64).astype(np.float32)
    col = (pos % 64).astype(np.float32)
    n_freq = 32
    inv_freq = (np.float32(10000.0) ** (-np.arange(n_freq, dtype=np.float32) / np.float32(n_freq))).astype(np.float32)
    ang_r = (row[:, None] * inv_freq).astype(np.float32)
    ang_c = (col[:, None] * inv_freq).astype(np.float32)
    cr, sr, cc, sc = np.cos(ang_r), np.sin(ang_r), np.cos(ang_c), np.sin(ang_c)
    tab = np.concatenate([cr, cr, cc, cc, -sr, sr, -sc, sc], axis=1).astype(np.float32)
    return np.ascontiguousarray(tab)


def prep_inputs(inp, cores=range(8)):
    f = lambda a: np.ascontiguousarray(np.asarray(a, dtype=np.float32))
    x = f(inp['x'])
    mem = f(inp['mem'])
    w_in = f(inp['w_in'])[0]
    w_in_sw = w_in.copy()
    w_in_sw[:, 4608:4612] = w_in[:, 4612:4616]
    w_in_sw[:, 4612:4616] = w_in[:, 4608:4612]
    w_in_sw[:, 4616:4620] = w_in[:, 4620:4624]
    w_in_sw[:, 4620:4624] = w_in[:, 4616:4620]
    bi = f(inp['b_igate'])[0]
    bf = f(inp['b_fgate'])[0]
    bg0 = np.concatenate([bi.reshape(8), bf.reshape(8)])[None, :]
    bg1 = np.concatenate([bi[::-1].reshape(8), bf[::-1].reshape(8)])[None, :]
    gqk = np.concatenate([f(inp['att_q_norm'])[0], f(inp['att_k_norm'])[0]])[None, :]
    mln = f(inp['ml_norm'])
    lnp = np.stack([f(inp['ln1_g'])[0], f(inp['ln1_b'])[0], f(inp['ln2_g'])[0], f(inp['ln2_b'])[0],
                    f(inp['ln3_g'])[0], f(inp['ln3_b'])[0]])
    sk = f(inp['peer_subkeys'])[0]
    skT = np.ascontiguousarray(sk.transpose(3, 0, 1, 2).reshape(128, 16 * 128))
    u = f(inp['peer_u'])[0]
    v = f(inp['peer_v'])[0]
    uh = np.ascontiguousarray(u.reshape(128, 128, 16, 128).transpose(1, 3, 2, 0)).reshape(128, 128, 2048)
    vh = np.ascontiguousarray(v.reshape(128, 128, 2048).transpose(1, 0, 2))
    cst = _consts()
    common = dict(cst=cst, gqk=f(gqk), mln=mln, w_out=f(inp['w_out'])[0], xa_wq=f(inp['xa_wq'])[0],
                  xa_wk=f(inp['xa_wk'])[0], xa_wv=f(inp['xa_wv'])[0], xa_wo=f(inp['xa_wo'])[0], lnp=f(lnp),
                  pwq=f(inp['peer_wq'])[0], skT=skT, uh=uh, vh=vh)
    rope0 = _rope_table(np.arange(4096))
    rope1 = _rope_table(4095 - np.arange(4096))
    maps = []
    for c in cores:
        b, half = c // 2, c % 2
        m = dict(common)
        if half == 0:
            m['xl'] = np.ascontiguousarray(x[b])
            m['rope'] = rope0
            m['w_in'] = w_in
            m['bg'] = f(bg0)
        else:
            m['xl'] = np.ascontiguousarray(x[b][::-1])
            m['rope'] = rope1
            m['w_in'] = w_in_sw
            m['bg'] = f(bg1)
        m['memb'] = np.ascontiguousarray(mem[b])
        maps.append(m)
    return maps


def assemble(results, cores=range(8)):
    out = np.zeros((4, 4096, 2048), np.float32)
    for r, c in zip(results, cores):
        b, half = c // 2, c % 2
        o = np.asarray(r["out"], dtype=np.float32)
        if half == 0:
            out[b, 0:2048] = o
        else:
            out[b, 2048:4096] = o[::-1]
    return out


_NC_CACHE = {}


def kernel(**inputs):
    if 'nc' not in _NC_CACHE:
        _NC_CACHE['nc'] = build_program('all')
    nc = _NC_CACHE['nc']
    maps = prep_inputs(inputs)
    res = run_bass_kernel_spmd(nc, maps, core_ids=list(range(8)))
    return assemble(res.results)
```

```python
import contextlib
import math
import numpy as np
import concourse.bass as bass
import concourse.mybir as mybir
from concourse.bass_utils import run_bass_kernel_spmd

F32 = mybir.dt.float32
BF16 = mybir.dt.bfloat16
U32 = mybir.dt.uint32
AF = mybir.ActivationFunctionType
ALU = mybir.AluOpType
AX = mybir.AxisListType

ALPHA = 2.0 ** 0.25
LN_EPS = 1e-5
RMS_EPS = 1e-6
NEG = -1.0e4


class Prog:
    ENGS = ('pe', 'act', 'dve', 'pool', 'sp')

    def __init__(self, nc):
        self.nc = nc
        self.ops = {e: [] for e in self.ENGS}
        self.res = {}
        self.dma_sem_count = {}
        self.epoch = 0

    def _deps(self, eng, reads, writes, is_dma):
        deps = []
        for r in reads:
            st = self.res.get(r)
            if st:
                deps.extend(st['w'])
        for w in writes:
            st = self.res.get(w)
            if st:
                for t in st['w']:
                    if not (is_dma and t[0] == 'dma'):
                        deps.append(t)
                deps.extend(st['r'])
        out = []
        for d in deps:
            if d[0] == 'eng' and d[1] == eng and eng == 'pe':
                continue
            if d not in out:
                out.append(d)
        return out

    def _commit(self, tok, reads, writes):
        for r in reads:
            st = self.res.setdefault(r, {'w': [], 'r': []})
            st['r'].append(tok)
        for w in writes:
            st = self.res.setdefault(w, {'w': [], 'r': []})
            if tok[0] == 'dma' and st['w'] and all(t[0] == 'dma' for t in st['w']) and not st['r']:
                st['w'] = [t for t in st['w'] if t[1] != tok[1]] + [tok]
            else:
                st['w'] = [tok]
            st['r'] = []

    def op(self, eng, fn, reads=(), writes=()):
        deps = self._deps(eng, reads, writes, False)
        tok = ('eng', eng, len(self.ops[eng]))
        self.ops[eng].append({'fn': fn, 'deps': deps, 'needed': False, 'dma': None, 'ep': self.epoch})
        self._commit(tok, reads, writes)
        return tok

    def dma(self, eng, out, in_, reads=(), writes=(), semkey=None):
        deps = self._deps(eng, reads, writes, True)
        sk = semkey if semkey is not None else (writes[0], eng)
        self.dma_sem_count[sk] = self.dma_sem_count.get(sk, 0) + 16
        tok = ('dma', sk, self.dma_sem_count[sk])
        self.ops[eng].append({'fn': (lambda e: e.dma_start(out=out, in_=in_)), 'deps': deps,
                              'needed': True, 'dma': sk, 'ep': self.epoch})
        self._commit(tok, reads, writes)
        return tok

    def all_tokens(self):
        toks = []
        for e in self.ENGS:
            for i in range(len(self.ops[e]) - 1, -1, -1):
                o = self.ops[e][i]
                if o['fn'] is not None and o['dma'] is None:
                    toks.append(('eng', e, i))
                    break
        for sk, v in self.dma_sem_count.items():
            toks.append(('dma', sk, v))
        return toks

    def barrier(self, new_epoch=True):
        toks = self.all_tokens()
        for e in self.ENGS:
            self.ops[e].append({'fn': None, 'deps': [t for t in toks if not (t[0] == 'eng' and t[1] == e)],
                                'needed': False, 'dma': None, 'ep': self.epoch})
        self.res = {}
        if new_epoch:
            self.epoch += 1

    def emit(self, final_tokens=()):
        nc = self.nc
        for e in self.ENGS:
            for o in self.ops[e]:
                for d in o['deps']:
                    if d[0] == 'eng':
                        self.ops[d[1]][d[2]]['needed'] = True
        self.maxval = {}
        for e in self.ENGS:
            c = {}
            for o in self.ops[e]:
                if o['dma'] is None and o['needed']:
                    c[o['ep']] = c.get(o['ep'], 0) + 1
                    o['val'] = c[o['ep']]
            self.maxval[e] = dict(c)
        with contextlib.ExitStack() as st:
            esem = {(e, ep): st.enter_context(nc.semaphore('es_%s_%d' % (e, ep)))
                    for e in self.ENGS if e != 'sp' for ep in range(self.epoch + 1)}
            dsem = {}
            for i, k in enumerate(self.dma_sem_count):
                dsem[k] = st.enter_context(nc.semaphore('ds_%d' % i))
            block = st.enter_context(nc.Block())

            def tokval(d):
                if d[0] == 'eng':
                    o = self.ops[d[1]][d[2]]
                    return ('e', d[1], o['ep']), esem[(d[1], o['ep'])], o['val']
                return ('d', d[1]), dsem[d[1]], d[2]

            def body(ename, extra_final=()):
                def f(eng):
                    seen = {}

                    def waits(deps):
                        best = {}
                        for d in deps:
                            k, s, v = tokval(d)
                            if v > best.get(k, (None, 0))[1]:
                                best[k] = (s, v)
                        for k, (s, v) in best.items():
                            if seen.get(k, 0) >= v:
                                continue
                            seen[k] = v
                            eng.wait_ge(s, v)
                    for o in self.ops[ename]:
                        waits(o['deps'])
                        if o['fn'] is None:
                            continue
                        ins = o['fn'](eng)
                        if o['dma'] is not None:
                            ins.then_inc(dsem[o['dma']], 16)
                        elif o['needed']:
                            ins.then_inc(esem[(ename, o['ep'])], 1)
                    waits(extra_final)
                return f
            block.tensor(body('pe'))
            block.scalar(body('act'))
            block.vector(body('dve'))
            block.gpsimd(body('pool'))
            block.sync(body('sp', tuple(final_tokens)))


def build_program(stage='all'):
    nc = bass.Bass("TRN2", target_bir_lowering=False)

    def din(n, s, d=F32):
        return nc.dram_tensor(n, s, d, kind="ExternalInput").ap()

    def dscr(n, s, d):
        return nc.dram_tensor(n, s, d, kind="Internal").ap()

    xl = din("xl", [4096, 2048])
    rope = din("rope", [4096, 256])
    w_in = din("w_in", [2048, 4624])
    bg = din("bg", [1, 16])
    gqk = din("gqk", [1, 256])
    mln_d = din("mln", [1, 1024])
    memb = din("memb", [256, 2048])
    cst_d = din("cst", [128, 1024])
    w_out = din("w_out", [2048, 2048])
    xa_wq = din("xa_wq", [2048, 2048])
    xa_wk = din("xa_wk", [2048, 2048])
    xa_wv = din("xa_wv", [2048, 2048])
    xa_wo = din("xa_wo", [2048, 2048])
    lnp = din("lnp", [6, 2048])
    pwq = din("pwq", [2048, 2048])
    skT = din("skT", [128, 16 * 128])
    uh = din("uh", [128, 128, 2048])
    vh = din("vh", [128, 128, 2048])

    hb_d = dscr("hb_s", [2048, 1024], F32)
    cat_d = dscr("cat_s", [2048, 2048], BF16)
    x1_d = dscr("x1_s", [2048, 2048], F32)
    x2_d = dscr("x2_s", [2048, 2048], F32)
    wd_d = dscr("wd_s", [16, 128, 128 * 128], BF16)
    w_in_b = dscr("w_in_b", [2048, 4624], BF16)
    w_out_b = dscr("w_out_b", [2048, 2048], BF16)
    xa_wq_b = dscr("xa_wq_b", [2048, 2048], BF16)
    xa_wo_b = dscr("xa_wo_b", [2048, 2048], BF16)
    pwq_b = dscr("pwq_b", [2048, 2048], BF16)

    if stage == 'mix':
        out_d = nc.dram_tensor("out", [2048, 2048], BF16, kind="ExternalOutput").ap()
    else:
        out_d = nc.dram_tensor("out", [2048, 2048], F32, kind="ExternalOutput").ap()

    P = Prog(nc)
    final = []

    with contextlib.ExitStack() as G:
        def sbuf(st, n, s, d):
            return st.enter_context(nc.sbuf_tensor('sb_' + n, s, d))

        ps = [G.enter_context(nc.psum_tensor("ps%d" % i, [128, 512], F32)) for i in range(8)]

        def PS(i):
            return ('ps', i)

        cst = sbuf(G, "cst", [128, 1024], F32)
        ident_b = sbuf(G, "ident_b", [128, 128], BF16)
        ones_b = sbuf(G, "ones_b", [128, 128], BF16)
        ident_f = cst[:, 0:128]
        Umask = cst[:, 128:256]
        Lmask = cst[:, 256:384]
        NEGU = cst[:, 384:512]
        NEGL = cst[:, 512:640]
        ones_f = cst[:, 640:768]
        iota_f = cst[:, 768:896]

        P.dma('sp', cst[:], cst_d, writes=['cst'])
        P.op('dve', lambda e: e.tensor_copy(out=ident_b[:], in_=ident_f), reads=['cst'], writes=['ident_b'])
        P.op('dve', lambda e: e.tensor_copy(out=ones_b[:], in_=ones_f), reads=['cst'], writes=['ones_b'])

        def mm(out, lhsT, rhs, start, stop, reads, writes):
            return P.op('pe', lambda e: e.matmul(out, lhsT, rhs, start=start, stop=stop), reads=reads, writes=writes)

        def tr(out, in_, reads, writes, fp32=False):
            idn = ident_f if fp32 else ident_b[:]
            return P.op('pe', lambda e: e.transpose(out=out, in_=in_, identity=idn),
                        reads=list(reads) + (['cst'] if fp32 else ['ident_b']), writes=writes)

        def act(out, in_, func, reads, writes, bias=None, scale=None):
            kw = {}
            if bias is not None:
                kw['bias'] = bias
            if scale is not None:
                kw['scale'] = scale
            return P.op('act', lambda e: e.activation(out=out, in_=in_, func=func, **kw), reads=reads, writes=writes)

        def tt(eng, out, in0, in1, op, reads, writes):
            return P.op(eng, lambda e: e.tensor_tensor(out=out, in0=in0, in1=in1, op=op), reads=reads, writes=writes)

        def ts(eng, out, in0, s1, s2, op0, op1, reads, writes):
            return P.op(eng, lambda e: e.tensor_scalar(out=out, in0=in0, scalar1=s1, scalar2=s2, op0=op0, op1=op1),
                        reads=reads, writes=writes)

        def stt(out, in0, scalar, in1, op0, op1, reads, writes):
            return P.op('dve', lambda e: e.scalar_tensor_tensor(out=out, in0=in0, scalar=scalar, in1=in1, op0=op0, op1=op1),
                        reads=reads, writes=writes)

        def cp(eng, out, in_, reads, writes):
            if eng == 'act':
                return P.op('act', lambda e: e.activation(out=out, in_=in_, func=AF.Copy), reads=reads, writes=writes)
            return P.op(eng, lambda e: e.tensor_copy(out=out, in_=in_), reads=reads, writes=writes)

        def red(out, in_, reads, writes, op=ALU.add):
            return P.op('dve', lambda e: e.tensor_reduce(out=out, in_=in_, axis=AX.X, op=op), reads=reads, writes=writes)

        def recip(out, in_, reads, writes):
            return P.op('dve', lambda e: e.reciprocal(out=out, in_=in_), reads=reads, writes=writes)

        def memset(eng, ap, val, writes):
            return P.op(eng, lambda e: e.memset(ap, val), writes=writes)

        def op_max(out, in_, reads, writes):
            return P.op('dve', lambda e: e.max(out=out, in_=in_), reads=reads, writes=writes)

        def op_maxidx(out, mx, vals_, reads, writes):
            return P.op('dve', lambda e: e.max_index(out=out, in_max=mx, in_values=vals_), reads=reads, writes=writes)

        def op_mrep(out, rep, vals_, reads, writes):
            return P.op('dve', lambda e: e.match_replace(out=out, in_to_replace=rep, in_values=vals_, imm_value=-1.0e30),
                        reads=reads, writes=writes)

        def op_tss(out, in_, scalar, op, reads, writes):
            return P.op('dve', lambda e: e.tensor_single_scalar(out=out, in_=in_, scalar=scalar, op=op), reads=reads, writes=writes)

        def op_bnstats(out, in_, reads, writes):
            return P.op('dve', lambda e: e.bn_stats(out=out, in_=in_), reads=reads, writes=writes)

        def op_bnaggr(out, in_, reads, writes):
            return P.op('dve', lambda e: e.bn_aggr(out=out, in_=in_), reads=reads, writes=writes)

        rr_state = {'tb': 0, 'ev': 0, 'uv': 0, 'wc': 0}
        for rb in range(16):
            P.dma('pool', w_in_b[rb * 128:(rb + 1) * 128, :], w_in[rb * 128:(rb + 1) * 128, :], writes=['w_in_b'], semkey='wcast_in')
        wcast_list = [(dst, src, rb) for (dst, src) in ((w_out_b, w_out), (xa_wq_b, xa_wq), (xa_wo_b, xa_wo), (pwq_b, pwq)) for rb in range(16)]

        def cast_w(n):
            for _ in range(n):
                k = rr_state['wc']
                if k >= len(wcast_list):
                    return
                rr_state['wc'] += 1
                dst, src, rb = wcast_list[k]
                P.dma('pool', dst[rb * 128:(rb + 1) * 128, :], src[rb * 128:(rb + 1) * 128, :], writes=[('wcast', k)], semkey='wcast_rest')

        def transpose16(src, src_res, dstT, dst_res, col0):
            for half in range(2):
                bank = 6 + (rr_state['tb'] % 2)
                rr_state['tb'] += 1
                psb = ps[bank][:].bitcast(BF16)
                for k in range(8):
                    c = half * 8 + k
                    tr(psb[:, k * 128:(k + 1) * 128], src[:, c * 128:(c + 1) * 128], [src_res], [PS(bank)])
                eng = 'act' if rr_state['ev'] % 2 == 0 else 'dve'
                rr_state['ev'] += 1
                cp(eng, dstT[:, half * 8:(half + 1) * 8, col0:col0 + 128],
                   psb.rearrange("p (k t) -> p k t", k=8), [], [PS(bank), dst_res])

        with contextlib.ExitStack() as M:
            KT = sbuf(M, "KT", [128, 2, 4096], BF16)
            VA = sbuf(M, "VA", [128, 32, 2, 130], BF16)
            xb = [sbuf(M, "xb%d" % i, [128, 2048], BF16) for i in range(2)]
            xT = sbuf(M, "xT", [128, 16, 512], BF16)
            wb = [sbuf(M, "wb%d" % i, [128, 16, 512], BF16) for i in range(2)]
            wg = sbuf(M, "wg", [128, 16, 16], BF16)
            ropeT = [sbuf(M, "ropeT%d" % i, [128, 256], F32) for i in range(2)]
            gqk_t = sbuf(M, "gqk_t", [128, 256], F32)
            bgt = sbuf(M, "bgt", [128, 16], F32)
            mln_t = sbuf(M, "mln_t", [128, 1024], F32)
            ig = sbuf(M, "ig", [128, 4, 8], F32)
            lf = sbuf(M, "lf", [128, 4, 8], F32)
            t0 = sbuf(M, "t0", [128, 1024], F32)
            t1 = sbuf(M, "t1", [128, 1024], F32)
            qkb = sbuf(M, "qkb", [128, 512], BF16)
            sm = sbuf(M, "sm", [128, 64], F32)
            QT = sbuf(M, "QT", [128, 8, 512], BF16)
            mqT = sbuf(M, "mqT", [128, 4, 512], BF16)
            mkT = sbuf(M, "mkT", [128, 4, 512], BF16)
            mk_tok = sbuf(M, "mk_tok", [128, 4, 512], BF16)
            mv_aug = sbuf(M, "mv_aug", [128, 4, 4, 258], BF16)
            mo_sig = sbuf(M, "mo_sig", [128, 4, 1024], BF16)
            att_g = sbuf(M, "att_g", [128, 4, 1024], BF16)
            mlb = [sbuf(M, "mlb%d" % i, [128, 1024], BF16) for i in range(2)]
            PTb = [sbuf(M, "PTb%d" % i, [128, 512], BF16) for i in range(3)]
            Fm = [sbuf(M, "Fm%d" % i, [128, 128], F32) for i in range(2)]
            DT = [sbuf(M, "DT%d" % i, [128, 128], F32) for i in range(2)]
            EB = [sbuf(M, "EB%d" % i, [128, 128], F32) for i in range(2)]
            PTm = [sbuf(M, "PTm%d" % i, [128, 128], BF16) for i in range(2)]
            qsT = [sbuf(M, "qsT%d" % i, [128, 128], BF16) for i in range(2)]
            kw = [sbuf(M, "kw%d" % i, [128, 128], BF16) for i in range(2)]
            mc = [sbuf(M, "mc%d" % i, [128, 8], F32) for i in range(2)]
            Cf = [sbuf(M, "Cf%d" % i, [128, 4, 257], F32) for i in range(2)]
            Cb = [sbuf(M, "Cb%d" % i, [128, 4, 258], BF16) for i in range(2)]
            hout = sbuf(M, "hout", [128, 1024], F32)
            hbl = sbuf(M, "hbl", [128, 1024], F32)

            P.dma('sp', gqk_t[:], gqk.partition_broadcast(128), writes=['gqk_t'])
            P.dma('sp', bgt[:], bg.partition_broadcast(128), writes=['bgt'])
            P.dma('sp', mln_t[:], mln_d.partition_broadcast(128), writes=['mln_t'])
            w_in_v = w_in_b.rearrange("(c p) n -> p c n", p=128)
            P.dma('sp', wg[:], w_in_v[:, :, 4608:4624], reads=['w_in_b'], writes=['wg'])
            memset('pool', VA[:], 1.0, ['VA'])
            memset('pool', mv_aug[:], 1.0, ['mv_aug'])
            for d in range(2):
                memset('pool', Cf[d][:], 0.0, [('Cf', d, h) for h in range(4)])
                memset('pool', Cb[d][:], 0.0, [('Cb', d, h) for h in range(4)])

            st = {'xslot': 0, 'wslot': 0, 'rslot': 0, 'bank': 0, 'mi': 0}

            def load_w(col0, ncols):
                slot = st['wslot'] % 2
                st['wslot'] += 1
                for q in range(4):
                    P.dma('sp', wb[slot][:, q * 4:(q + 1) * 4, 0:ncols], w_in_v[:, q * 4:(q + 1) * 4, col0:col0 + ncols],
                          reads=['w_in_b'], writes=[('wb', slot)])
                return slot

            def nbank():
                b = st['bank'] % 6
                st['bank'] += 1
                return b

            def load_group(tiles):
                for i, tile in enumerate(tiles):
                    slot = st['xslot'] % 2
                    st['xslot'] += 1
                    P.dma('pool', xb[slot][:], xl[tile * 128:(tile + 1) * 128, :], writes=[('xb', slot)])
                    transpose16(xb[slot], ('xb', slot), xT, 'xT', i * 128)

            def proj_tok(i, wslot, c0, ncols, bank):
                for c in range(16):
                    mm(ps[bank][:, 0:ncols], xT[:, c, i * 128:(i + 1) * 128], wb[wslot][:, c, c0:c0 + ncols],
                       c == 0, c == 15, ['xT', ('wb', wslot)], [PS(bank)])

            def proj_feat(j, wslot, bank):
                for c in range(16):
                    mm(ps[bank][:, 0:512], wb[wslot][:, c, j * 128:(j + 1) * 128], xT[:, c, :],
                       c == 0, c == 15, ['xT', ('wb', wslot)], [PS(bank)])

            def norm_rope(psap, bank, H, gain, rslot, outb):
                W = H * 128
                cp('act', t0[:, 0:W], psap, [], [PS(bank), 't0'])
                tt('dve', t1[:, 0:W], t0[:, 0:W], t0[:, 0:W], ALU.mult, ['t0'], ['t1'])
                red(sm[:, 0:H], t1[:, 0:W].rearrange("p (h d) -> p h d", h=H), ['t1'], ['sm'])
                act(sm[:, 8:8 + H], sm[:, 0:H], AF.Sqrt, ['sm'], ['sm'], bias=eps_rms[:, 0:1], scale=1.0 / 128.0)
                recip(sm[:, 16:16 + H], sm[:, 8:8 + H], ['sm'], ['sm'])
                t0v = t0[:, 0:W].rearrange("p (h d) -> p h d", h=H)
                t1v = t1[:, 0:W].rearrange("p (h d) -> p h d", h=H)
                tt('dve', t1v, t0v, sm[:, 16:16 + H].unsqueeze(2).to_broadcast([128, H, 128]), ALU.mult,
                   ['t0', 'sm'], ['t1'])
                tt('dve', t1v, t1v, gain.unsqueeze(1).to_broadcast([128, H, 128]), ALU.mult, ['t1', 'gqk_t'], ['t1'])
                rt = ropeT[rslot]
                tt('dve', t0v, t1v, rt[:, 0:128].unsqueeze(1).to_broadcast([128, H, 128]), ALU.mult,
                   ['t1', ('rope', rslot)], ['t0'])
                t1z = t1[:, 0:W].rearrange("p (h a z d) -> p h a z d", h=H, a=2, z=2)
                sz = rt[:, 128:256].rearrange("p (a z d) -> p a z d", a=2, z=2)
                q2 = qk2[:, 0:W].rearrange("p (h a z d) -> p h a z d", h=H, a=2, z=2)
                for z in range(2):
                    tt('dve', q2[:, :, :, z, :], t1z[:, :, :, 1 - z, :],
                       sz[:, :, z, :].unsqueeze(1).to_broadcast([128, H, 2, 32]), ALU.mult,
                       ['t1', ('rope', rslot)], ['qk2'])
                tt('dve', outb, t0[:, 0:W], qk2[:, 0:W], ALU.add, ['t0', 'qk2'], ['qkb'])

            qk2 = sbuf(M, "qk2", [128, 512], F32)
            eps_rms = sbuf(M, "eps_rms", [128, 2], F32)
            memset('pool', eps_rms[:, 0:1], RMS_EPS, ['eps_rms'])
            memset('pool', eps_rms[:, 1:2], 1.0, ['eps_rms'])

            def load_rope(tile):
                slot = st['rslot'] % 2
                st['rslot'] += 1
                P.dma('sp', ropeT[slot][:], rope[tile * 128:(tile + 1) * 128, :], writes=[('rope', slot)])
                return slot

            def do_kv(tiles, wslot):
                for i, tile in enumerate(tiles):
                    bank = nbank()
                    proj_tok(i, wslot, 0, 512, bank)
                    rs = load_rope(tile)
                    cp('act', VA[:, tile, :, 0:128], ps[bank][:, 256:512].rearrange("p (g d) -> p g d", g=2),
                       [], [PS(bank), 'VA'])
                    norm_rope(ps[bank][:, 0:256], bank, 2, gqk_t[:, 128:256], rs, qkb[:, 0:256])
                    tb = 6 + (rr_state['tb'] % 2)
                    rr_state['tb'] += 1
                    psb = ps[tb][:].bitcast(BF16)
                    for g in range(2):
                        tr(psb[:, g * 128:(g + 1) * 128], qkb[:, g * 128:(g + 1) * 128], ['qkb'], [PS(tb)])
                    cp('act', KT[:, :, tile * 128:(tile + 1) * 128], psb[:, 0:256].rearrange("p (g t) -> p g t", g=2),
                       [], [PS(tb), 'KT'])

            def do_gates(n):
                for i in range(n):
                    bank = nbank()
                    for c in range(16):
                        mm(ps[bank][:, 0:16], xT[:, c, i * 128:(i + 1) * 128], wg[:, c, :], c == 0, c == 15,
                           ['xT', 'wg'], [PS(bank)])
                    stt(ig[:, i, :], ps[bank][:, 0:8], -0.5 * math.log(128.0), bgt[:, 0:8], ALU.add, ALU.add,
                        ['bgt'], [PS(bank), 'ig'])
                    tt('dve', sm[:, 32:40], ps[bank][:, 8:16], bgt[:, 8:16], ALU.add, ['bgt'], [PS(bank), 'sm'])
                    act(sm[:, 40:48], sm[:, 32:40], AF.Exp, ['sm'], ['sm'], scale=-1.0)
                    act(sm[:, 48:56], sm[:, 40:48], AF.Ln, ['sm'], ['sm'], bias=eps_rms[:, 1:2], scale=1.0)
                    ts('dve', lf[:, i, :], sm[:, 48:56], -1.0, 0.0, ALU.mult, ALU.add, ['sm'], ['lf'])

            def do_mk_tok(n, wslot):
                for i in range(n):
                    bank = nbank()
                    proj_tok(i, wslot, 0, 512, bank)
                    cp('act' if i % 2 == 0 else 'dve', mk_tok[:, i, :], ps[bank][:, 0:512], [], [PS(bank), 'mk_tok'])

            def do_mv(n, wslot, half):
                for i in range(n):
                    bank = nbank()
                    proj_tok(i, wslot, 0, 512, bank)
                    cp('dve' if i % 2 == 0 else 'act', mv_aug[:, i, half * 2:half * 2 + 2, 0:256],
                       ps[bank][:, 0:512].rearrange("p (h d) -> p h d", h=2), [], [PS(bank), 'mv_aug'])

            def do_feat(dst, dst_res, wslot):
                for j in range(4):
                    bank = nbank()
                    proj_feat(j, wslot, bank)
                    cp('act' if j % 2 == 0 else 'dve', dst[:, j, :], ps[bank][:, 0:512], [], [PS(bank), dst_res])

            def mlstm_head(i, d, h, k, with_out, add_hbl):
                MASK = Umask if d == 0 else Lmask
                NEGM = NEGU if d == 0 else NEGL
                last = 127 if d == 0 else 0
                if True:
                    j = d * 4 + h
                    bX, bY, bZ = (0, 1, 2) if k == 0 else (3, 4, 5)
                    fcol = lf[:, i, j:j + 1]
                    icol = ig[:, i, j:j + 1]
                    ts('dve', Fm[k][:], MASK, fcol, 0.0, ALU.mult, ALU.add, ['cst', 'lf'], [('Fm', k)])
                    yield
                    mm(ps[bX][:, 0:128], ones_f, Fm[k][:], True, True, ['cst', ('Fm', k)], [PS(bX)])
                    yield
                    if with_out:
                        mm(ps[bX][:, 128:256], ones_f, Fm[k][:], True, False, ['cst', ('Fm', k)], [PS(bX)])
                        yield
                        mm(ps[bX][:, 128:256], ident_f, NEGM, False, True, ['cst'], [PS(bX)])
                        yield
                    mm(ps[bX][:, 256:257], MASK, fcol, True, True, ['cst', 'lf'], [PS(bX)])
                    yield
                    tt('dve', mc[k][:, 0:1], icol, ps[bX][:, 256:257], ALU.subtract, ['ig'], [PS(bX), ('mc', k)])
                    yield
                    act(mc[k][:, 1:2], ps[bX][:, last:last + 1], AF.Exp, [('mc', k)], [PS(bX), ('mc', k)], bias=mc[k][:, 0:1], scale=1.0)
                    yield
                    act(mc[k][:, 2:3], ps[bX][:, last:last + 1], AF.Exp, [('mc', k)], [PS(bX), ('mc', k)])
                    yield
                    if with_out:
                        act(EB[k][:], ps[bX][:, 0:128], AF.Exp, [], [PS(bX), ('EB', k)])
                        yield
                        act(DT[k][:], ps[bX][:, 128:256], AF.Exp, [('mc', k)], [PS(bX), ('DT', k)], bias=mc[k][:, 0:1], scale=1.0)
                        yield
                        mm(ps[bX][:, 384:512], mkT[:, h, i * 128:(i + 1) * 128], mqT[:, h, i * 128:(i + 1) * 128],
                           True, True, ['mkT', 'mqT'], [PS(bX)])
                        yield
                        tt('dve', PTm[k][:], ps[bX][:, 384:512], DT[k][:], ALU.mult, [('DT', k)], [PS(bX), ('PTm', k)])
                        yield
                        tt('dve', qsT[k][:], mqT[:, h, i * 128:(i + 1) * 128], EB[k][:], ALU.mult, ['mqT', ('EB', k)], [('qsT', k)])
                        yield
                        mm(ps[bY][:, 0:257], PTm[k][:], mv_aug[:, i, h, 0:257], True, False, [('PTm', k), 'mv_aug'], [PS(bY)])
                        yield
                        mm(ps[bY][:, 0:257], qsT[k][:], Cb[d][:, h, 0:257], False, True, [('qsT', k), ('Cb', d, h)], [PS(bY)])
                        yield
                        act(mc[k][:, 3:4], ps[bY][:, 256:257], AF.Abs, [('mc', k)], [PS(bY), ('mc', k)])
                        yield
                        ts('dve', mc[k][:, 4:5], mc[k][:, 3:4], 1.0, 0.0, ALU.max, ALU.add, [('mc', k)], [('mc', k)])
                        yield
                        recip(mc[k][:, 5:6], mc[k][:, 4:5], [('mc', k)], [('mc', k)])
                        yield
                        if add_hbl:
                            stt(hout[:, h * 256:(h + 1) * 256], ps[bY][:, 0:256], mc[k][:, 5:6], hbl[:, h * 256:(h + 1) * 256],
                                ALU.mult, ALU.add, [('mc', k), 'hbl'], [PS(bY), ('hout', h)])
                            yield
                        else:
                            ts('dve', hout[:, h * 256:(h + 1) * 256], ps[bY][:, 0:256], mc[k][:, 5:6], 0.0, ALU.mult, ALU.add,
                               [('mc', k)], [PS(bY), ('hout', h)])
                            yield
                    ts('dve', kw[k][:], mk_tok[:, i, h * 128:(h + 1) * 128], mc[k][:, 1:2], 0.0, ALU.mult, ALU.add,
                       ['mk_tok', ('mc', k)], [('kw', k)])
                    yield
                    mm(ps[bZ][:, 0:257], kw[k][:], mv_aug[:, i, h, 0:257], True, True, [('kw', k), 'mv_aug'], [PS(bZ)])
                    yield
                    stt(Cf[d][:, h, :], Cf[d][:, h, :], mc[k][:, 2:3], ps[bZ][:, 0:257], ALU.mult, ALU.add,
                        [('mc', k)], [PS(bZ), ('Cf', d, h)])
                    yield
                    cp('act', Cb[d][:, h, 0:257], Cf[d][:, h, :], [('Cf', d, h)], [('Cb', d, h)])
                    yield


            def run_rr(gens):
                gens = list(gens)
                while gens:
                    for g in list(gens):
                        try:
                            next(g)
                        except StopIteration:
                            gens.remove(g)

            def mlstm_tile(i, d, with_out, add_hbl):
                for pair in ((0, 1), (2, 3)):
                    run_rr([mlstm_head(i, d, h, k, with_out, add_hbl) for k, h in enumerate(pair)])

            for grp in range(7, 3, -1):
                tiles = [grp * 4 + i for i in range(4)]
                load_group(tiles)
                cast_w(6)
                ws = load_w(1024, 512)
                do_kv(tiles, ws)
                do_gates(4)
                ws = load_w(2048, 512)
                do_mk_tok(4, ws)
                ws = load_w(2560, 512)
                do_mv(4, ws, 0)
                ws = load_w(3072, 512)
                do_mv(4, ws, 1)
                for i in range(3, -1, -1):
                    mlstm_tile(i, 1, False, False)

            for grp in range(3, -1, -1):
                tiles = [grp * 4 + i for i in range(4)]
                load_group(tiles)
                cast_w(6)
                ws = load_w(1024, 512)
                do_kv(tiles, ws)
                do_gates(4)
                ws = load_w(1536, 512)
                do_feat(mqT, 'mqT', ws)
                ws = load_w(2048, 512)
                do_feat(mkT, 'mkT', ws)
                do_mk_tok(4, ws)
                ws = load_w(2560, 512)
                do_mv(4, ws, 0)
                ws = load_w(3072, 512)
                do_mv(4, ws, 1)
                for i in range(3, -1, -1):
                    mlstm_tile(i, 1, True, False)
                    tile = tiles[i]
                    P.dma('sp', hb_d[tile * 128:(tile + 1) * 128, :], hout[:], reads=[('hout', 0), ('hout', 1), ('hout', 2), ('hout', 3)], writes=[('hb', tile)],
                          semkey='hout_st')

            for grp in range(4):
                tiles = [grp * 4 + i for i in range(4)]
                load_group(tiles)
                cast_w(6)
                for blk in range(2):
                    ws = load_w(blk * 512, 512)
                    for i, tile in enumerate(tiles):
                        bank = nbank()
                        proj_tok(i, ws, 0, 512, bank)
                        rs = load_rope(tile)
                        norm_rope(ps[bank][:, 0:512], bank, 4, gqk_t[:, 0:128], rs, qkb[:, 0:512])
                        tb = 6 + (rr_state['tb'] % 2)
                        rr_state['tb'] += 1
                        psb = ps[tb][:].bitcast(BF16)
                        for hh in range(4):
                            tr(psb[:, hh * 128:(hh + 1) * 128], qkb[:, hh * 128:(hh + 1) * 128], ['qkb'], [PS(tb)])
                        cp('act', QT[:, blk * 4:(blk + 1) * 4, i * 128:(i + 1) * 128],
                           psb[:, 0:512].rearrange("p (h t) -> p h t", h=4), [], [PS(tb), 'QT'])
                do_gates(4)
                ws = load_w(1536, 512)
                do_feat(mqT, 'mqT', ws)
                ws = load_w(2048, 512)
                do_feat(mkT, 'mkT', ws)
                do_mk_tok(4, ws)
                ws = load_w(2560, 512)
                do_mv(4, ws, 0)
                ws = load_w(3072, 512)
                do_mv(4, ws, 1)
                for blk in range(2):
                    ws = load_w(3584 + blk * 512, 512)
                    for i in range(4):
                        bank = nbank()
                        proj_tok(i, ws, 0, 512, bank)
                        act(mo_sig[:, i, blk * 512:(blk + 1) * 512], ps[bank][:, 0:512], AF.Sigmoid, [], [PS(bank), 'mo_sig'])
                its = [(hq, kt) for hq in range(8) for kt in range(32)]

                def issue_S(n):
                    hq, kt = its[n]
                    g = hq // 4
                    bS = 4 + (n % 2)
                    pt = n % 3
                    mm(ps[bS][:, 0:512], KT[:, g, kt * 128:(kt + 1) * 128], QT[:, hq, :], True, True, ['KT', 'QT'], [PS(bS)])
                    act(PTb[pt][:], ps[bS][:, 0:512], AF.Exp, [], [PS(bS), ('PTb', pt)], scale=128.0 ** -0.5)

                def issue_PV(n):
                    hq, kt = its[n]
                    g = hq // 4
                    pt = n % 3
                    for qs in range(4):
                        mm(ps[qs][:, 0:129], PTb[pt][:, qs * 128:(qs + 1) * 128], VA[:, kt, g, 0:129], kt == 0, kt == 31,
                           [('PTb', pt), 'VA'], [PS(qs)])
                    if kt == 31:
                        for qs in range(4):
                            recip(sm[:, 56 + qs:57 + qs], ps[qs][:, 128:129], [], [PS(qs), 'sm'])
                            ts('dve', att_g[:, qs, hq * 128:(hq + 1) * 128], ps[qs][:, 0:128], sm[:, 56 + qs:57 + qs], 0.0,
                               ALU.mult, ALU.add, ['sm'], [PS(qs), 'att_g'])

                issue_S(0)
                for n in range(len(its)):
                    if n + 1 < len(its):
                        issue_S(n + 1)
                    issue_PV(n)
                for i, tile in enumerate(tiles):
                    P.dma('sp', hbl[:], hb_d[tile * 128:(tile + 1) * 128, :], reads=[('hb', tile)], writes=['hbl'])
                    mlstm_tile(i, 0, True, True)
                    tt('dve', t0[:], hout[:], hout[:], ALU.mult, [('hout', 0), ('hout', 1), ('hout', 2), ('hout', 3)], ['t0'])
                    red(sm[:, 0:4], t0[:].rearrange("p (h d) -> p h d", h=4), ['t0'], ['sm'])
                    act(sm[:, 8:12], sm[:, 0:4], AF.Sqrt, ['sm'], ['sm'], bias=eps_rms[:, 0:1], scale=1.0 / 256.0)
                    recip(sm[:, 16:20], sm[:, 8:12], ['sm'], ['sm'])
                    tt('dve', t0[:].rearrange("p (h d) -> p h d", h=4), hout[:].rearrange("p (h d) -> p h d", h=4),
                       sm[:, 16:20].unsqueeze(2).to_broadcast([128, 4, 256]), ALU.mult, [('hout', 0), ('hout', 1), ('hout', 2), ('hout', 3)] + ['sm'], ['t0'])
                    tt('pool', t1[:], t0[:], mln_t[:], ALU.mult, ['t0', 'mln_t'], ['t1'])
                    ms = i % 2
                    tt('pool', mlb[ms][:], t1[:], mo_sig[:, i, :], ALU.mult, ['t1', 'mo_sig'], [('mlb', ms)])
                    P.dma('sp', cat_d[tile * 128:(tile + 1) * 128, 1024:2048], mlb[ms][:], reads=[('mlb', ms)],
                          writes=[('cat_ml', tile)], semkey=('mlb_st', ms))
                    P.dma('sp', cat_d[tile * 128:(tile + 1) * 128, 0:1024], att_g[:, i, :], reads=['att_g'],
                          writes=[('cat_att', tile)], semkey='att_st')
            P.barrier()

        if stage == 'mix':
            with contextlib.ExitStack() as Dg:
                cb = sbuf(Dg, "dbg_cb", [128, 2048], BF16)
                for tile in range(16):
                    P.dma('sp', cb[:], cat_d[tile * 128:(tile + 1) * 128, :], reads=[('cat_ml', tile), ('cat_att', tile)], writes=['dbg_cb'])
                    final.append(P.dma('sp', out_d[tile * 128:(tile + 1) * 128, :], cb[:], reads=['dbg_cb'], writes=[('out', tile)],
                                       semkey='dbg_out'))
            P.emit(final_tokens=final[-1:])
            return nc

        with contextlib.ExitStack() as E:
            xbE = [sbuf(E, "xbE%d" % i, [128, 2048], BF16) for i in range(2)]
            aT = sbuf(E, "aT", [128, 16, 512], BF16)
            wbD = [sbuf(E, "wbD%d" % i, [128, 16, 512], BF16) for i in range(2)]
            r_g = sbuf(E, "r_g", [128, 4, 2048], F32)
            ln_g = sbuf(E, "ln_g", [128, 2048], F32)
            ln_b = sbuf(E, "ln_b", [128, 2048], F32)
            st6 = sbuf(E, "st6", [128, 4, 6], F32)
            lmv = sbuf(E, "lmv", [128, 8], F32)
            eps_ln = sbuf(E, "eps_ln", [128, 1], F32)
            memT = sbuf(E, "memT", [128, 16, 256], BF16)
            kmT = sbuf(E, "kmT", [128, 16, 256], BF16)
            vm = sbuf(E, "vm", [128, 2, 2048], BF16)
            q1T = sbuf(E, "q1T", [128, 16, 512], BF16)
            o_g = sbuf(E, "o_g", [128, 4, 2048], BF16)
            PTx = [sbuf(E, "PTx%d" % i, [128, 512], BF16) for i in range(4)]
            rrx = sbuf(E, "rrx", [128, 8], F32)
            memset('pool', eps_ln[:], LN_EPS, ['eps_ln'])
            sE = {'w': 0, 'x': 0, 'bank': 0, 'pt': 0}

            def load_wE(W, col0, bf=False):
                slot = sE['w'] % 2
                sE['w'] += 1
                Wv = W.rearrange("(c p) n -> p c n", p=128)
                for q in range(4):
                    P.dma('sp' if bf else 'pool', wbD[slot][:, q * 4:(q + 1) * 4, :], Wv[:, q * 4:(q + 1) * 4, col0:col0 + 512],
                          writes=[('wbD', slot)], semkey=('wbD', slot, bf))
                return slot

            def nbE():
                b = sE['bank'] % 6
                sE['bank'] += 1
                return b

            def load_T(src_rows, dst, dst_res, col0, cast, extra_reads=()):
                slot = sE['x'] % 2
                sE['x'] += 1
                P.dma('pool' if cast else 'sp', xbE[slot][:], src_rows, reads=list(extra_reads), writes=[('xbE', slot)])
                transpose16(xbE[slot], ('xbE', slot), dst, dst_res, col0)

            def load_ln(k):
                P.dma('sp', ln_g[:], lnp[2 * k:2 * k + 1, :].partition_broadcast(128), writes=['ln_g'])
                P.dma('sp', ln_b[:], lnp[2 * k + 1:2 * k + 2, :].partition_broadcast(128), writes=['ln_b'])

            def dense_ln(W, tiles, out_d_, out_key, bf=True):
                for cbk in range(4):
                    ws = load_wE(W, cbk * 512, bf)
                    for i in range(4):
                        bank = nbE()
                        for c in range(16):
                            mm(ps[bank][:, 0:512], aT[:, c, i * 128:(i + 1) * 128], wbD[ws][:, c, :], c == 0, c == 15,
                               ['aT', ('wbD', ws)], [PS(bank)])
                        stt(r_g[:, i, cbk * 512:(cbk + 1) * 512], r_g[:, i, cbk * 512:(cbk + 1) * 512], ALPHA,
                            ps[bank][:, 0:512], ALU.mult, ALU.add, [], [PS(bank), ('r_g', i)])
                for i, tile in enumerate(tiles):
                    for q in range(4):
                        op_bnstats(st6[:, q, :], r_g[:, i, q * 512:(q + 1) * 512], [('r_g', i)], ['st6'])
                    op_bnaggr(lmv[:, 0:2], st6[:].rearrange("p a b -> p (a b)"), ['st6'], ['lmv'])
                    act(lmv[:, 2:3], lmv[:, 1:2], AF.Sqrt, ['lmv', 'eps_ln'], ['lmv'], bias=eps_ln[:, 0:1], scale=1.0)
                    recip(lmv[:, 3:4], lmv[:, 2:3], ['lmv'], ['lmv'])
                    stt(lmv[:, 4:5], lmv[:, 0:1], -1.0, lmv[:, 3:4], ALU.mult, ALU.mult, ['lmv'], ['lmv'])
                    act(r_g[:, i, :], r_g[:, i, :], AF.Identity, ['lmv'], [('r_g', i)], bias=lmv[:, 4:5], scale=lmv[:, 3:4])
                    tt('dve', r_g[:, i, :], r_g[:, i, :], ln_g[:], ALU.mult, ['ln_g'], [('r_g', i)])
                    tt('dve', r_g[:, i, :], r_g[:, i, :], ln_b[:], ALU.add, ['ln_b'], [('r_g', i)])
                    tk = P.dma('sp', out_d_[tile * 128:(tile + 1) * 128, :], r_g[:, i, :], reads=[('r_g', i)],
                               writes=[(out_key, tile)], semkey=('r_g_st', i))
                    if out_key == 'out':
                        final.append(tk)

            load_ln(0)
            for grp in range(4):
                tiles = [grp * 4 + i for i in range(4)]
                for i, tile in enumerate(tiles):
                    P.dma('sp', r_g[:, i, :], xl[tile * 128:(tile + 1) * 128, :], writes=[('r_g', i)])
                    load_T(cat_d[tile * 128:(tile + 1) * 128, :], aT, 'aT', i * 128, False,
                           extra_reads=[('cat_ml', tile), ('cat_att', tile)])
                dense_ln(w_out_b, tiles, x1_d, 'x1')

            load_ln(1)
            for mt in range(2):
                load_T(memb[mt * 128:(mt + 1) * 128, :], memT, 'memT', mt * 128, True)
            for cbk in range(4):
                ws = load_wE(xa_wk, cbk * 512)
                for j in range(4):
                    bank = nbE()
                    for c in range(16):
                        mm(ps[bank][:, 0:256], wbD[ws][:, c, j * 128:(j + 1) * 128], memT[:, c, :], c == 0, c == 15,
                           ['memT', ('wbD', ws)], [PS(bank)])
                    cp('act' if j % 2 == 0 else 'dve', kmT[:, cbk * 4 + j, :], ps[bank][:, 0:256], [], [PS(bank), 'kmT'])
            for cbk in range(4):
                ws = load_wE(xa_wv, cbk * 512)
                for mt in range(2):
                    bank = nbE()
                    for c in range(16):
                        mm(ps[bank][:, 0:512], memT[:, c, mt * 128:(mt + 1) * 128], wbD[ws][:, c, :], c == 0, c == 15,
                           ['memT', ('wbD', ws)], [PS(bank)])
                    cp('act' if mt % 2 == 0 else 'dve', vm[:, mt, cbk * 512:(cbk + 1) * 512], ps[bank][:, 0:512], [], [PS(bank), 'vm'])
            for grp in range(4):
                tiles = [grp * 4 + i for i in range(4)]
                for i, tile in enumerate(tiles):
                    P.dma('sp', r_g[:, i, :], x1_d[tile * 128:(tile + 1) * 128, :], reads=[('x1', tile)], writes=[('r_g', i)])
                    load_T(x1_d[tile * 128:(tile + 1) * 128, :], aT, 'aT', i * 128, True, extra_reads=[('x1', tile)])
                for cbk in range(4):
                    ws = load_wE(xa_wq_b, cbk * 512, True)
                    for j in range(4):
                        bank = nbE()
                        for c in range(16):
                            mm(ps[bank][:, 0:512], wbD[ws][:, c, j * 128:(j + 1) * 128], aT[:, c, :], c == 0, c == 15,
                               ['aT', ('wbD', ws)], [PS(bank)])
                        cp('act' if j % 2 == 0 else 'dve', q1T[:, cbk * 4 + j, :], ps[bank][:, 0:512], [], [PS(bank), 'q1T'])
                for h in range(4):
                    pts = []
                    for mt in range(2):
                        bank = nbE()
                        for dc in range(4):
                            mm(ps[bank][:, 0:512], kmT[:, h * 4 + dc, mt * 128:(mt + 1) * 128], q1T[:, h * 4 + dc, :],
                               dc == 0, dc == 3, ['kmT', 'q1T'], [PS(bank)])
                        pt = sE['pt'] % 4
                        sE['pt'] += 1
                        act(PTx[pt][:], ps[bank][:, 0:512], AF.Exp, [], [PS(bank), ('PTx', pt)], scale=512.0 ** -0.5)
                        pts.append(pt)
                    for qs in range(4):
                        bank = nbE()
                        for mt in range(2):
                            mm(ps[bank][:, 0:512], PTx[pts[mt]][:, qs * 128:(qs + 1) * 128], vm[:, mt, h * 512:(h + 1) * 512],
                               mt == 0, mt == 1, [('PTx', pts[mt]), 'vm'], [PS(bank)])
                        b2 = 6 + (qs % 2)
                        for mt in range(2):
                            mm(ps[b2][:, 0:1], PTx[pts[mt]][:, qs * 128:(qs + 1) * 128], ones_b[:, 0:1],
                               mt == 0, mt == 1, [('PTx', pts[mt]), 'ones_b'], [PS(b2)])
                        recip(rrx[:, qs:qs + 1], ps[b2][:, 0:1], [], [PS(b2), 'rrx'])
                        ts('dve', o_g[:, qs, h * 512:(h + 1) * 512], ps[bank][:, 0:512], rrx[:, qs:qs + 1], 0.0, ALU.mult, ALU.add,
                           ['rrx'], [PS(bank), ('o_g', qs)])
                for i in range(4):
                    transpose16(o_g[:, i, :], ('o_g', i), aT, 'aT', i * 128)
                dense_ln(xa_wo_b, tiles, x2_d, 'x2')
            P.barrier()

        if stage == 'de':
            with contextlib.ExitStack() as Dg:
                cb2 = sbuf(Dg, "dbg_cb2", [128, 2048], F32)
                for tile in range(16):
                    P.dma('sp', cb2[:], x2_d[tile * 128:(tile + 1) * 128, :], reads=[('x2', tile)], writes=['dbg_cb2'])
                    final.append(P.dma('sp', out_d[tile * 128:(tile + 1) * 128, :], cb2[:], reads=['dbg_cb2'], writes=[('out', tile)],
                                       semkey='dbg_out'))
            P.emit(final_tokens=final[-1:])
            return nc

        Wd4 = wd_d.rearrange("s p (j t) -> s p j t", j=128)
        with contextlib.ExitStack() as F1:
            xbF = [sbuf(F1, "xbF%d" % i, [128, 2048], BF16) for i in range(2)]
            x2Ta = sbuf(F1, "x2Ta", [128, 16, 256], BF16)
            wbF = [sbuf(F1, "wbF%d" % i, [128, 16, 128], BF16) for i in range(2)]
            qpT = sbuf(F1, "qpT", [128, 16, 256], F32)
            skS = sbuf(F1, "skS", [128, 16 * 128], F32)
            s_sb = sbuf(F1, "s_sb", [128, 16, 128], F32)
            s2 = sbuf(F1, "s2", [128, 256], F32)
            vals = sbuf(F1, "vals", [128, 16, 16], F32)
            idx = sbuf(F1, "idx", [128, 16, 16], U32)
            idxf = sbuf(F1, "idxf", [128, 16, 16], F32)
            cand = sbuf(F1, "cand", [128, 8, 256], F32)
            cv = sbuf(F1, "cv", [128, 8, 16], F32)
            cpos = sbuf(F1, "cpos", [128, 8, 16], U32)
            rk = sbuf(F1, "rk", [128, 2, 128], U32)
            rkf = sbuf(F1, "rkf", [128, 2, 128], F32)
            oh = sbuf(F1, "oh", [128, 8, 16, 16], F32)
            sel = sbuf(F1, "sel", [128, 3, 128], F32)
            selT = sbuf(F1, "selT", [128, 3, 128], F32)
            gz = sbuf(F1, "gz", [128, 16], F32)
            OA = [sbuf(F1, "OA%d" % i, [128, 16, 128], BF16) for i in range(2)]
            OB = [sbuf(F1, "OB%d" % i, [128, 16, 128], BF16) for i in range(2)]
            Wt = [sbuf(F1, "Wt%d" % i, [128, 128, 128], BF16) for i in range(2)]
            P.dma('sp', skS[:], skT, writes=['skS'])
            pwq_v = pwq_b.rearrange("(c p) n -> p c n", p=128)
            xcnt = 0
            for grp in range(8):
                tiles = [grp * 2, grp * 2 + 1]
                for i, tile in enumerate(tiles):
                    xs = xcnt % 2
                    xcnt += 1
                    P.dma('pool', xbF[xs][:], x2_d[tile * 128:(tile + 1) * 128, :], writes=[('xbF', xs)])
                    transpose16(xbF[xs], ('xbF', xs), x2Ta, 'x2Ta', i * 128)
                for hc in range(16):
                    ws = hc % 2
                    for q in range(2):
                        P.dma('sp', wbF[ws][:, q * 8:(q + 1) * 8, :], pwq_v[:, q * 8:(q + 1) * 8, hc * 128:(hc + 1) * 128],
                              writes=[('wbF', ws)])
                    bank = hc % 4
                    for c in range(16):
                        mm(ps[bank][:, 0:256], wbF[ws][:, c, :], x2Ta[:, c, :], c == 0, c == 15, ['x2Ta', ('wbF', ws)], [PS(bank)])
                    cp('act' if hc % 2 == 0 else 'dve', qpT[:, hc, :], ps[bank][:, 0:256], [], [PS(bank), 'qpT'])
                for t in range(2):
                    tile = tiles[t]
                    wsl = tile % 2
                    for q4 in range(4):
                        bank = q4
                        for r4 in range(4):
                            hc = q4 * 4 + r4
                            mm(ps[bank][:, r4 * 128:(r4 + 1) * 128], qpT[:, hc, t * 128:(t + 1) * 128], skS[:, hc * 128:(hc + 1) * 128],
                               True, True, ['qpT', 'skS'], [PS(bank)])
                        cp('act', s_sb[:, q4 * 4:(q4 + 1) * 4, :],
                           ps[bank][:, 0:512].rearrange("p (a k) -> p a k", a=4), [], [PS(bank), 's_sb'])
                    for hc in range(16):
                        op_max(vals[:, hc, 0:8], s_sb[:, hc, :], ['s_sb'], ['vals'])
                        op_maxidx(idx[:, hc, 0:8], vals[:, hc, 0:8], s_sb[:, hc, :], ['s_sb', 'vals'], ['idx'])
                        op_mrep(s2[:, 0:128], vals[:, hc, 0:8], s_sb[:, hc, :], ['s_sb', 'vals'], ['s2'])
                        op_max(vals[:, hc, 8:16], s2[:, 0:128], ['s2'], ['vals'])
                        op_maxidx(idx[:, hc, 8:16], vals[:, hc, 8:16], s2[:, 0:128], ['s2', 'vals'], ['idx'])
                    cp('dve', idxf[:], idx[:], ['idx'], ['idxf'])
                    vv = vals[:].rearrange("p (h c) a -> p h c a", c=2)
                    tt('dve', cand[:].rearrange("p h (a b) -> p h a b", a=16), vv[:, :, 0, :].unsqueeze(3).to_broadcast([128, 8, 16, 16]),
                       vv[:, :, 1, :].unsqueeze(2).to_broadcast([128, 8, 16, 16]), ALU.add, ['vals'], ['cand'])
                    for h in range(8):
                        op_max(cv[:, h, 0:8], cand[:, h, :], ['cand'], ['cv'])
                        op_maxidx(cpos[:, h, 0:8], cv[:, h, 0:8], cand[:, h, :], ['cand', 'cv'], ['cpos'])
                        op_mrep(s2[:], cv[:, h, 0:8], cand[:, h, :], ['cand', 'cv'], ['s2'])
                        op_max(cv[:, h, 8:16], s2[:], ['s2'], ['cv'])
                        op_maxidx(cpos[:, h, 8:16], cv[:, h, 8:16], s2[:], ['s2', 'cv'], ['cpos'])
                    g3 = sel[:, 2, :].rearrange("p (h c) -> p h c", h=8)
                    tt('dve', g3, cv[:], cv[:, :, 0:1].to_broadcast([128, 8, 16]), ALU.subtract, ['cv'], ['sel'])
                    act(g3, g3, AF.Exp, [], ['sel'])
                    red(gz[:, 0:8], g3, ['sel'], ['gz'])
                    recip(gz[:, 8:16], gz[:, 0:8], ['gz'], ['gz'])
                    tt('dve', g3, g3, gz[:, 8:16].unsqueeze(2).to_broadcast([128, 8, 16]), ALU.mult, ['gz'], ['sel'])
                    cpf = cpos[:].rearrange("p h c -> p (h c)")
                    op_tss(rk[:, 0, :], cpf, 4, ALU.logical_shift_right, ['cpos'], ['rk'])
                    op_tss(rk[:, 1, :], cpf, 15, ALU.bitwise_and, ['cpos'], ['rk'])
                    cp('dve', rkf[:], rk[:], ['rk'], ['rkf'])
                    idv = idxf[:].rearrange("p (h c) a -> p h c a", c=2)
                    for half in range(2):
                        rv = rkf[:, half, :].rearrange("p (h c) -> p h c", h=8)
                        tt('dve', oh[:], rv.unsqueeze(3).to_broadcast([128, 8, 16, 16]),
                           iota_f[:, 0:16].unsqueeze(1).unsqueeze(1).to_broadcast([128, 8, 16, 16]), ALU.is_equal, ['rkf', 'cst'], ['oh'])
                        tt('dve', oh[:], oh[:], idv[:, :, half, :].unsqueeze(2).to_broadcast([128, 8, 16, 16]), ALU.mult, ['idxf'], ['oh'])
                        red(sel[:, half, :].rearrange("p (h c) -> p h c", h=8), oh[:], ['oh'], ['sel'])
                    for q3 in range(3):
                        tr(ps[4][:, q3 * 128:(q3 + 1) * 128], sel[:, q3, :], ['sel'], [PS(4)], fp32=True)
                    cp('act', selT[:], ps[4][:, 0:384].rearrange("p (a t) -> p a t", a=3), [], [PS(4), 'selT'])
                    for tc in range(8):
                        k = tc % 2
                        tok0 = tc * 16
                        io = iota_f.unsqueeze(1).to_broadcast([128, 16, 128])
                        tt('dve', OA[k][:], io, selT[:, 0, tok0:tok0 + 16].unsqueeze(2).to_broadcast([128, 16, 128]), ALU.is_equal,
                           ['cst', 'selT'], [('OA', k)])
                        tt('dve', OA[k][:], OA[k][:], selT[:, 2, tok0:tok0 + 16].unsqueeze(2).to_broadcast([128, 16, 128]), ALU.mult,
                           ['selT'], [('OA', k)])
                        tt('dve', OB[k][:], io, selT[:, 1, tok0:tok0 + 16].unsqueeze(2).to_broadcast([128, 16, 128]), ALU.is_equal,
                           ['cst', 'selT'], [('OB', k)])
                        for q4 in range(4):
                            bank = 5 + ((tc * 4 + q4) % 3)
                            for r4 in range(4):
                                tl = q4 * 4 + r4
                                mm(ps[bank][:, r4 * 128:(r4 + 1) * 128], OA[k][:, tl, :], OB[k][:, tl, :], True, True,
                                   [('OA', k), ('OB', k)], [PS(bank)])
                            a0 = tok0 + q4 * 4
                            cp('act', Wt[wsl][:, :, a0:a0 + 4],
                               ps[bank][:, 0:512].rearrange("p (t j) -> p j t", t=4), [], [PS(bank), ('Wt', wsl)])
                    P.dma('sp', wd_d[tile], Wt[wsl][:].rearrange("p j t -> p (j t)"), reads=[('Wt', wsl)], writes=[('wd', tile)],
                          semkey=('Wt_st', wsl))
            P.barrier(new_epoch=False)

        for pp in range(2):
            with contextlib.ExitStack() as P2:
                pn = "p%d_" % pp
                x2T = sbuf(P2, pn + "x2T", [128, 16, 1024], BF16)
                acc_sb = sbuf(P2, pn + "acc", [128, 8, 2048], F32)
                with contextlib.ExitStack() as F2:
                    JB = 4
                    NS = 6
                    ub = [sbuf(F2, pn + "ub%d" % i, [128, 16, 128], BF16) for i in range(NS)]
                    vb = [sbuf(F2, pn + "vb%d" % i, [128, 2048], BF16) for i in range(NS)]
                    Wj4 = [sbuf(F2, pn + "Wj4%d" % i, [128, 8, 4, 128], BF16) for i in range(2)]
                    ga = [sbuf(F2, pn + "ga%d" % i, [128, 512], F32) for i in range(2)]
                    aTj = [sbuf(F2, pn + "aTj%d" % i, [128, 1024], BF16) for i in range(2 * JB)]
                    for s8 in range(8):
                        tile = pp * 8 + s8
                        P.dma('pool', vb[s8 % 2][:], x2_d[tile * 128:(tile + 1) * 128, :], writes=[('vb', s8 % 2)])
                        transpose16(vb[s8 % 2], ('vb', s8 % 2), x2T, 'x2T', s8 * 128)
                    accset = 0
                    gcnt = 0
                    for jb in range(128 // JB):
                        wq4 = jb % 2
                        j0 = jb * JB
                        for sh in range(2):
                            P.dma('sp', Wj4[wq4][:, sh * 4:(sh + 1) * 4, :, :],
                                  Wd4[pp * 8 + sh * 4:pp * 8 + sh * 4 + 4, :, j0:j0 + 4, :].rearrange("s p j t -> p s j t"),
                                  writes=[('Wj4', wq4)])
                        for jj in range(JB):
                            j = jb * JB + jj
                            sl = j % NS
                            sa = (jb % 2) * JB + jj
                            P.dma('pool', ub[sl][:], uh[j].rearrange("p (c i) -> p c i", c=16), writes=[('ub', sl)])
                            P.dma('pool', vb[sl][:], vh[j], writes=[('vb', sl)])
                            for half in range(2):
                                bank = (j % 2) * 2 + half
                                for c in range(16):
                                    mm(ps[bank][:, 0:512], ub[sl][:, c, :], x2T[:, c, half * 512:(half + 1) * 512], c == 0, c == 15,
                                       [('ub', sl), 'x2T'], [PS(bank)])
                                gs = gcnt % 2
                                gcnt += 1
                                act(ga[gs][:], ps[bank][:, 0:512], AF.Gelu, [], [PS(bank), ('ga', gs)])
                                tt('dve', aTj[sa][:, half * 512:(half + 1) * 512].rearrange("p (s t) -> p s t", s=4),
                                   ga[gs][:].rearrange("p (s t) -> p s t", s=4), Wj4[wq4][:, half * 4:(half + 1) * 4, jj, :], ALU.mult,
                                   [('ga', gs), ('Wj4', wq4)], [('aTj', sa)])
                        for s8 in range(8):
                            for cpair in range(2):
                                b0 = 4 + (accset % 2) * 2
                                accset += 1
                                for jj in range(JB):
                                    sl = (jb * JB + jj) % NS
                                    sa = (jb % 2) * JB + jj
                                    for cbk in range(2):
                                        col0 = (cpair * 2 + cbk) * 512
                                        mm(ps[b0 + cbk][:, 0:512], aTj[sa][:, s8 * 128:(s8 + 1) * 128], vb[sl][:, col0:col0 + 512],
                                           jj == 0, jj == JB - 1, [('aTj', sa), ('vb', sl)], [PS(b0 + cbk)])
                                for cbk in range(2):
                                    col0 = (cpair * 2 + cbk) * 512
                                    if jb == 0:
                                        cp('dve', acc_sb[:, s8, col0:col0 + 512], ps[b0 + cbk][:, 0:512], [], [PS(b0 + cbk), ('acc', s8)])
                                    else:
                                        tt('dve', acc_sb[:, s8, col0:col0 + 512], acc_sb[:, s8, col0:col0 + 512], ps[b0 + cbk][:, 0:512], ALU.add,
                                           [], [PS(b0 + cbk), ('acc', s8)])
                    P.barrier(new_epoch=False)
                with contextlib.ExitStack() as F3:
                    rF = [sbuf(F3, pn + "rF%d" % i, [128, 2048], F32) for i in range(2)]
                    lg = sbuf(F3, pn + "lg", [128, 2048], F32)
                    lb = sbuf(F3, pn + "lb", [128, 2048], F32)
                    st6f = sbuf(F3, pn + "st6f", [128, 4, 6], F32)
                    lmvf = sbuf(F3, pn + "lmvf", [128, 8], F32)
                    epsf = sbuf(F3, pn + "epsf", [128, 1], F32)
                    memset('pool', epsf[:], LN_EPS, ['epsf'])
                    P.dma('sp', lg[:], lnp[4:5, :].partition_broadcast(128), writes=['lg'])
                    P.dma('sp', lb[:], lnp[5:6, :].partition_broadcast(128), writes=['lb'])
                    for s8 in range(8):
                        tile = pp * 8 + s8
                        r_ = rF[s8 % 2]
                        rk_ = ('rF', s8 % 2)
                        P.dma('sp', r_[:], x2_d[tile * 128:(tile + 1) * 128, :], writes=[rk_])
                        stt(r_[:], r_[:], ALPHA, acc_sb[:, s8, :], ALU.mult, ALU.add, [('acc', s8)], [rk_])
                        for q in range(4):
                            op_bnstats(st6f[:, q, :], r_[:, q * 512:(q + 1) * 512], [rk_], ['st6f'])
                        op_bnaggr(lmvf[:, 0:2], st6f[:].rearrange("p a b -> p (a b)"), ['st6f'], ['lmvf'])
                        act(lmvf[:, 2:3], lmvf[:, 1:2], AF.Sqrt, ['lmvf', 'epsf'], ['lmvf'], bias=epsf[:, 0:1], scale=1.0)
                        recip(lmvf[:, 3:4], lmvf[:, 2:3], ['lmvf'], ['lmvf'])
                        stt(lmvf[:, 4:5], lmvf[:, 0:1], -1.0, lmvf[:, 3:4], ALU.mult, ALU.mult, ['lmvf'], ['lmvf'])
                        act(r_[:], r_[:], AF.Identity, ['lmvf'], [rk_], bias=lmvf[:, 4:5], scale=lmvf[:, 3:4])
                        tt('dve', r_[:], r_[:], lg[:], ALU.mult, ['lg'], [rk_])
                        tt('dve', r_[:], r_[:], lb[:], ALU.add, ['lb'], [rk_])
                        final.append(P.dma('sp', out_d[tile * 128:(tile + 1) * 128, :], r_[:], reads=[rk_],
                                           writes=[('out', tile)], semkey=('rF_st', s8 % 2)))
                    P.barrier(new_epoch=False)
        P.emit(final_tokens=final)
        return nc
    return nc


def _consts():
    c = np.zeros((128, 1024), np.float32)
    idx = np.arange(128)
    c[:, 0:128] = np.eye(128, dtype=np.float32)
    c[:, 128:256] = (idx[:, None] <= idx[None, :]).astype(np.float32)
    c[:, 256:384] = (idx[:, None] >= idx[None, :]).astype(np.float32)
    c[:, 384:512] = np.where(idx[:, None] <= idx[None, :], 0.0, NEG)
    c[:, 512:640] = np.where(idx[:, None] >= idx[None, :], 0.0, NEG)
    c[:, 640:768] = 1.0
    c[:, 768:896] = idx[None, :].astype(np.float32)
    return c


def _rope_table(pos):
    pos = np.asarray(pos)
    row = (pos // 64).astype(np.float32)
    col = (pos % 64).astype(np.float32)
    n_freq = 32
    inv_freq = (np.float32(10000.0) ** (-np.arange(n_freq, dtype=np.float32) / np.float32(n_freq))).astype(np.float32)
    ang_r = (row[:, None] * inv_freq).astype(np.float32)
    ang_c = (col[:, None] * inv_freq).astype(np.float32)
    cr, sr, cc, sc = np.cos(ang_r), np.sin(ang_r), np.cos(ang_c), np.sin(ang_c)
    tab = np.concatenate([cr, cr, cc, cc, -sr, sr, -sc, sc], axis=1).astype(np.float32)
    return np.ascontiguousarray(tab)


def prep_inputs(inp, cores=range(8)):
    f = lambda a: np.ascontiguousarray(np.asarray(a, dtype=np.float32))
    x = f(inp['x'])
    mem = f(inp['mem'])
    w_in = f(inp['w_in'])[0]
    w_in_sw = w_in.copy()
    w_in_sw[:, 4608:4612] = w_in[:, 4612:4616]
    w_in_sw[:, 4612:4616] = w_in[:, 4608:4612]
    w_in_sw[:, 4616:4620] = w_in[:, 4620:4624]
    w_in_sw[:, 4620:4624] = w_in[:, 4616:4620]
    bi = f(inp['b_igate'])[0]
    bf = f(inp['b_fgate'])[0]
    bg0 = np.concatenate([bi.reshape(8), bf.reshape(8)])[None, :]
    bg1 = np.concatenate([bi[::-1].reshape(8), bf[::-1].reshape(8)])[None, :]
    gqk = np.concatenate([f(inp['att_q_norm'])[0], f(inp['att_k_norm'])[0]])[None, :]
    mln = f(inp['ml_norm'])
    lnp = np.stack([f(inp['ln1_g'])[0], f(inp['ln1_b'])[0], f(inp['ln2_g'])[0], f(inp['ln2_b'])[0],
                    f(inp['ln3_g'])[0], f(inp['ln3_b'])[0]])
    sk = f(inp['peer_subkeys'])[0]
    skT = np.ascontiguousarray(sk.transpose(3, 0, 1, 2).reshape(128, 16 * 128))
    u = f(inp['peer_u'])[0]
    v = f(inp['peer_v'])[0]
    uh = np.ascontiguousarray(u.reshape(128, 128, 16, 128).transpose(1, 3, 2, 0)).reshape(128, 128, 2048)
    vh = np.ascontiguousarray(v.reshape(128, 128, 2048).transpose(1, 0, 2))
    cst = _consts()
    common = dict(cst=cst, gqk=f(gqk), mln=mln, w_out=f(inp['w_out'])[0], xa_wq=f(inp['xa_wq'])[0],
                  xa_wk=f(inp['xa_wk'])[0], xa_wv=f(inp['xa_wv'])[0], xa_wo=f(inp['xa_wo'])[0], lnp=f(lnp),
                  pwq=f(inp['peer_wq'])[0], skT=skT, uh=uh, vh=vh)
    rope0 = _rope_table(np.arange(4096))
    rope1 = _rope_table(4095 - np.arange(4096))
    maps = []
    for c in cores:
        b, half = c // 2, c % 2
        m = dict(common)
        if half == 0:
            m['xl'] = np.ascontiguousarray(x[b])
            m['rope'] = rope0
            m['w_in'] = w_in
            m['bg'] = f(bg0)
        else:
            m['xl'] = np.ascontiguousarray(x[b][::-1])
            m['rope'] = rope1
            m['w_in'] = w_in_sw
            m['bg'] = f(bg1)
        m['memb'] = np.ascontiguousarray(mem[b])
        maps.append(m)
    return maps


def assemble(results, cores=range(8)):
    out = np.zeros((4, 4096, 2048), np.float32)
    for r, c in zip(results, cores):
        b, half = c // 2, c % 2
        o = np.asarray(r["out"], dtype=np.float32)
        if half == 0:
            out[b, 0:2048] = o
        else:
            out[b, 2048:4096] = o[::-1]
    return out


_NC_CACHE = {}


def kernel(**inputs):
    if 'nc' not in _NC_CACHE:
        _NC_CACHE['nc'] = build_program('all')
    nc = _NC_CACHE['nc']
    maps = prep_inputs(inputs)
    res = run_bass_kernel_spmd(nc, maps, core_ids=list(range(8)))
    return assemble(res.results)
```

```python
import contextlib
import math
import numpy as np
import concourse.bass as bass
import concourse.mybir as mybir
from concourse.bass_utils import run_bass_kernel_spmd

F32 = mybir.dt.float32
BF16 = mybir.dt.bfloat16
U32 = mybir.dt.uint32
AF = mybir.ActivationFunctionType
ALU = mybir.AluOpType
AX = mybir.AxisListType

ALPHA = 2.0 ** 0.25
LN_EPS = 1e-5
RMS_EPS = 1e-6
NEG = -1.0e4


class Prog:
    ENGS = ('pe', 'act', 'dve', 'pool', 'sp')

    def __init__(self, nc):
        self.nc = nc
        self.ops = {e: [] for e in self.ENGS}
        self.res = {}
        self.dma_sem_count = {}
        self.epoch = 0

    def _deps(self, eng, reads, writes, is_dma):
        deps = []
        for r in reads:
            st = self.res.get(r)
            if st:
                deps.extend(st['w'])
        for w in writes:
            st = self.res.get(w)
            if st:
                for t in st['w']:
                    if not (is_dma and t[0] == 'dma'):
                        deps.append(t)
                deps.extend(st['r'])
        out = []
        for d in deps:
            if d[0] == 'eng' and d[1] == eng and eng == 'pe':
                continue
            if d not in out:
                out.append(d)
        return out

    def _commit(self, tok, reads, writes):
        for r in reads:
            st = self.res.setdefault(r, {'w': [], 'r': []})
            st['r'].append(tok)
        for w in writes:
            st = self.res.setdefault(w, {'w': [], 'r': []})
            if tok[0] == 'dma' and st['w'] and all(t[0] == 'dma' for t in st['w']) and not st['r']:
                st['w'] = [t for t in st['w'] if t[1] != tok[1]] + [tok]
            else:
                st['w'] = [tok]
            st['r'] = []

    def op(self, eng, fn, reads=(), writes=()):
        deps = self._deps(eng, reads, writes, False)
        tok = ('eng', eng, len(self.ops[eng]))
        self.ops[eng].append({'fn': fn, 'deps': deps, 'needed': False, 'dma': None, 'ep': self.epoch})
        self._commit(tok, reads, writes)
        return tok

    def dma(self, eng, out, in_, reads=(), writes=(), semkey=None):
        deps = self._deps(eng, reads, writes, True)
        sk = semkey if semkey is not None else (writes[0], eng)
        self.dma_sem_count[sk] = self.dma_sem_count.get(sk, 0) + 16
        tok = ('dma', sk, self.dma_sem_count[sk])
        self.ops[eng].append({'fn': (lambda e: e.dma_start(out=out, in_=in_)), 'deps': deps,
                              'needed': True, 'dma': sk, 'ep': self.epoch})
        self._commit(tok, reads, writes)
        return tok

    def all_tokens(self):
        toks = []
        for e in self.ENGS:
            for i in range(len(self.ops[e]) - 1, -1, -1):
                o = self.ops[e][i]
                if o['fn'] is not None and o['dma'] is None:
                    toks.append(('eng', e, i))
                    break
        for sk, v in self.dma_sem_count.items():
            toks.append(('dma', sk, v))
        return toks

    def barrier(self, new_epoch=True):
        toks = self.all_tokens()
        for e in self.ENGS:
            self.ops[e].append({'fn': None, 'deps': [t for t in toks if not (t[0] == 'eng' and t[1] == e)],
                                'needed': False, 'dma': None, 'ep': self.epoch})
        self.res = {}
        if new_epoch:
            self.epoch += 1

    def emit(self, final_tokens=()):
        nc = self.nc
        for e in self.ENGS:
            for o in self.ops[e]:
                for d in o['deps']:
                    if d[0] == 'eng':
                        self.ops[d[1]][d[2]]['needed'] = True
        self.maxval = {}
        for e in self.ENGS:
            c = {}
            for o in self.ops[e]:
                if o['dma'] is None and o['needed']:
                    c[o['ep']] = c.get(o['ep'], 0) + 1
                    o['val'] = c[o['ep']]
            self.maxval[e] = dict(c)
        with contextlib.ExitStack() as st:
            esem = {(e, ep): st.enter_context(nc.semaphore('es_%s_%d' % (e, ep)))
                    for e in self.ENGS if e != 'sp' for ep in range(self.epoch + 1)}
            dsem = {}
            for i, k in enumerate(self.dma_sem_count):
                dsem[k] = st.enter_context(nc.semaphore('ds_%d' % i))
            block = st.enter_context(nc.Block())

            def tokval(d):
                if d[0] == 'eng':
                    o = self.ops[d[1]][d[2]]
                    return ('e', d[1], o['ep']), esem[(d[1], o['ep'])], o['val']
                return ('d', d[1]), dsem[d[1]], d[2]

            def body(ename, extra_final=()):
                def f(eng):
                    seen = {}

                    def waits(deps):
                        best = {}
                        for d in deps:
                            k, s, v = tokval(d)
                            if v > best.get(k, (None, 0))[1]:
                                best[k] = (s, v)
                        for k, (s, v) in best.items():
                            if seen.get(k, 0) >= v:
                                continue
                            seen[k] = v
                            eng.wait_ge(s, v)
                    for o in self.ops[ename]:
                        waits(o['deps'])
                        if o['fn'] is None:
                            continue
                        ins = o['fn'](eng)
                        if o['dma'] is not None:
                            ins.then_inc(dsem[o['dma']], 16)
                        elif o['needed']:
                            ins.then_inc(esem[(ename, o['ep'])], 1)
                    waits(extra_final)
                return f
            block.tensor(body('pe'))
            block.scalar(body('act'))
            block.vector(body('dve'))
            block.gpsimd(body('pool'))
            block.sync(body('sp', tuple(final_tokens)))


def build_program(stage='all'):
    nc = bass.Bass("TRN2", target_bir_lowering=False)

    def din(n, s, d=F32):
        return nc.dram_tensor(n, s, d, kind="ExternalInput").ap()

    def dscr(n, s, d):
        return nc.dram_tensor(n, s, d, kind="Internal").ap()

    xl = din("xl", [4096, 2048])
    rope = din("rope", [4096, 256])
    w_in = din("w_in", [2048, 4624])
    bg = din("bg", [1, 16])
    gqk = din("gqk", [1, 256])
    mln_d = din("mln", [1, 1024])
    memb = din("memb", [256, 2048])
    cst_d = din("cst", [128, 1024])
    w_out = din("w_out", [2048, 2048])
    xa_wq = din("xa_wq", [2048, 2048])
    xa_wk = din("xa_wk", [2048, 2048])
    xa_wv = din("xa_wv", [2048, 2048])
    xa_wo = din("xa_wo", [2048, 2048])
    lnp = din("lnp", [6, 2048])
    pwq = din("pwq", [2048, 2048])
    skT = din("skT", [128, 16 * 128])
    uh = din("uh", [128, 128, 2048])
    vh = din("vh", [128, 128, 2048])

    hb_d = dscr("hb_s", [2048, 1024], F32)
    cat_d = dscr("cat_s", [2048, 2048], BF16)
    x1_d = dscr("x1_s", [2048, 2048], F32)
    x2_d = dscr("x2_s", [2048, 2048], F32)
    wd_d = dscr("wd_s", [16, 128, 128 * 128], BF16)
    w_in_b = dscr("w_in_b", [2048, 4624], BF16)
    w_out_b = dscr("w_out_b", [2048, 2048], BF16)
    xa_wq_b = dscr("xa_wq_b", [2048, 2048], BF16)
    xa_wo_b = dscr("xa_wo_b", [2048, 2048], BF16)
    pwq_b = dscr("pwq_b", [2048, 2048], BF16)

    if stage == 'mix':
        out_d = nc.dram_tensor("out", [2048, 2048], BF16, kind="ExternalOutput").ap()
    else:
        out_d = nc.dram_tensor("out", [2048, 2048], F32, kind="ExternalOutput").ap()

    P = Prog(nc)
    final = []

    with contextlib.ExitStack() as G:
        def sbuf(st, n, s, d):
            return st.enter_context(nc.sbuf_tensor('sb_' + n, s, d))

        ps = [G.enter_context(nc.psum_tensor("ps%d" % i, [128, 512], F32)) for i in range(8)]

        def PS(i):
            return ('ps', i)

        cst = sbuf(G, "cst", [128, 1024], F32)
        ident_b = sbuf(G, "ident_b", [128, 128], BF16)
        ones_b = sbuf(G, "ones_b", [128, 128], BF16)
        ident_f = cst[:, 0:128]
        Umask = cst[:, 128:256]
        Lmask = cst[:, 256:384]
        NEGU = cst[:, 384:512]
        NEGL = cst[:, 512:640]
        ones_f = cst[:, 640:768]
        iota_f = cst[:, 768:896]

        P.dma('sp', cst[:], cst_d, writes=['cst'])
        P.op('dve', lambda e: e.tensor_copy(out=ident_b[:], in_=ident_f), reads=['cst'], writes=['ident_b'])
        P.op('dve', lambda e: e.tensor_copy(out=ones_b[:], in_=ones_f), reads=['cst'], writes=['ones_b'])

        def mm(out, lhsT, rhs, start, stop, reads, writes):
            return P.op('pe', lambda e: e.matmul(out, lhsT, rhs, start=start, stop=stop), reads=reads, writes=writes)

        def tr(out, in_, reads, writes, fp32=False):
            idn = ident_f if fp32 else ident_b[:]
            return P.op('pe', lambda e: e.transpose(out=out, in_=in_, identity=idn),
                        reads=list(reads) + (['cst'] if fp32 else ['ident_b']), writes=writes)

        def act(out, in_, func, reads, writes, bias=None, scale=None):
            kw = {}
            if bias is not None:
                kw['bias'] = bias
            if scale is not None:
                kw['scale'] = scale
            return P.op('act', lambda e: e.activation(out=out, in_=in_, func=func, **kw), reads=reads, writes=writes)

        def tt(eng, out, in0, in1, op, reads, writes):
            return P.op(eng, lambda e: e.tensor_tensor(out=out, in0=in0, in1=in1, op=op), reads=reads, writes=writes)

        def ts(eng, out, in0, s1, s2, op0, op1, reads, writes):
            return P.op(eng, lambda e: e.tensor_scalar(out=out, in0=in0, scalar1=s1, scalar2=s2, op0=op0, op1=op1),
                        reads=reads, writes=writes)

        def stt(out, in0, scalar, in1, op0, op1, reads, writes):
            return P.op('dve', lambda e: e.scalar_tensor_tensor(out=out, in0=in0, scalar=scalar, in1=in1, op0=op0, op1=op1),
                        reads=reads, writes=writes)

        def cp(eng, out, in_, reads, writes):
            if eng == 'act':
                return P.op('act', lambda e: e.activation(out=out, in_=in_, func=AF.Copy), reads=reads, writes=writes)
            return P.op(eng, lambda e: e.tensor_copy(out=out, in_=in_), reads=reads, writes=writes)

        def red(out, in_, reads, writes, op=ALU.add):
            return P.op('dve', lambda e: e.tensor_reduce(out=out, in_=in_, axis=AX.X, op=op), reads=reads, writes=writes)

        def recip(out, in_, reads, writes):
            return P.op('dve', lambda e: e.reciprocal(out=out, in_=in_), reads=reads, writes=writes)

        def memset(eng, ap, val, writes):
            return P.op(eng, lambda e: e.memset(ap, val), writes=writes)

        def op_max(out, in_, reads, writes):
            return P.op('dve', lambda e: e.max(out=out, in_=in_), reads=reads, writes=writes)

        def op_maxidx(out, mx, vals_, reads, writes):
            return P.op('dve', lambda e: e.max_index(out=out, in_max=mx, in_values=vals_), reads=reads, writes=writes)

        def op_mrep(out, rep, vals_, reads, writes):
            return P.op('dve', lambda e: e.match_replace(out=out, in_to_replace=rep, in_values=vals_, imm_value=-1.0e30),
                        reads=reads, writes=writes)

        def op_tss(out, in_, scalar, op, reads, writes):
            return P.op('dve', lambda e: e.tensor_single_scalar(out=out, in_=in_, scalar=scalar, op=op), reads=reads, writes=writes)

        def op_bnstats(out, in_, reads, writes):
            return P.op('dve', lambda e: e.bn_stats(out=out, in_=in_), reads=reads, writes=writes)

        def op_bnaggr(out, in_, reads, writes):
            return P.op('dve', lambda e: e.bn_aggr(out=out, in_=in_), reads=reads, writes=writes)

        rr_state = {'tb': 0, 'ev': 0, 'uv': 0, 'wc': 0}
        for rb in range(16):
            P.dma('pool', w_in_b[rb * 128:(rb + 1) * 128, :], w_in[rb * 128:(rb + 1) * 128, :], writes=['w_in_b'], semkey='wcast_in')
        wcast_list = [(dst, src, rb) for (dst, src) in ((w_out_b, w_out), (xa_wq_b, xa_wq), (xa_wo_b, xa_wo), (pwq_b, pwq)) for rb in range(16)]

        def cast_w(n):
            for _ in range(n):
                k = rr_state['wc']
                if k >= len(wcast_list):
                    return
                rr_state['wc'] += 1
                dst, src, rb = wcast_list[k]
                P.dma('pool', dst[rb * 128:(rb + 1) * 128, :], src[rb * 128:(rb + 1) * 128, :], writes=[('wcast', k)], semkey='wcast_rest')

        def transpose16(src, src_res, dstT, dst_res, col0):
            for half in range(2):
                bank = 6 + (rr_state['tb'] % 2)
                rr_state['tb'] += 1
                psb = ps[bank][:].bitcast(BF16)
                for k in range(8):
                    c = half * 8 + k
                    tr(psb[:, k * 128:(k + 1) * 128], src[:, c * 128:(c + 1) * 128], [src_res], [PS(bank)])
                eng = 'act' if rr_state['ev'] % 2 == 0 else 'dve'
                rr_state['ev'] += 1
                cp(eng, dstT[:, half * 8:(half + 1) * 8, col0:col0 + 128],
                   psb.rearrange("p (k t) -> p k t", k=8), [], [PS(bank), dst_res])

        with contextlib.ExitStack() as M:
            KT = sbuf(M, "KT", [128, 2, 4096], BF16)
            VA = sbuf(M, "VA", [128, 32, 2, 130], BF16)
            xb = [sbuf(M, "xb%d" % i, [128, 2048], BF16) for i in range(2)]
            xT = sbuf(M, "xT", [128, 16, 512], BF16)
            wb = [sbuf(M, "wb%d" % i, [128, 16, 512], BF16) for i in range(2)]
            wg = sbuf(M, "wg", [128, 16, 16], BF16)
            ropeT = [sbuf(M, "ropeT%d" % i, [128, 256], F32) for i in range(2)]
            gqk_t = sbuf(M, "gqk_t", [128, 256], F32)
            bgt = sbuf(M, "bgt", [128, 16], F32)
            mln_t = sbuf(M, "mln_t", [128, 1024], F32)
            ig = sbuf(M, "ig", [128, 4, 8], F32)
            lf = sbuf(M, "lf", [128, 4, 8], F32)
            t0 = sbuf(M, "t0", [128, 1024], F32)
            t1 = sbuf(M, "t1", [128, 1024], F32)
            qkb = sbuf(M, "qkb", [128, 512], BF16)
            sm = sbuf(M, "sm", [128, 64], F32)
            QT = sbuf(M, "QT", [128, 8, 512], BF16)
            mqT = sbuf(M, "mqT", [128, 4, 512], BF16)
            mkT = sbuf(M, "mkT", [128, 4, 512], BF16)
            mk_tok = sbuf(M, "mk_tok", [128, 4, 512], BF16)
            mv_aug = sbuf(M, "mv_aug", [128, 4, 4, 258], BF16)
            mo_sig = sbuf(M, "mo_sig", [128, 4, 1024], BF16)
            att_g = sbuf(M, "att_g", [128, 4, 1024], BF16)
            mlb = [sbuf(M, "mlb%d" % i, [128, 1024], BF16) for i in range(2)]
            PTb = [sbuf(M, "PTb%d" % i, [128, 512], BF16) for i in range(3)]
            Fm = [sbuf(M, "Fm%d" % i, [128, 128], F32) for i in range(2)]
            DT = [sbuf(M, "DT%d" % i, [128, 128], F32) for i in range(2)]
            EB = [sbuf(M, "EB%d" % i, [128, 128], F32) for i in range(2)]
            PTm = [sbuf(M, "PTm%d" % i, [128, 128], BF16) for i in range(2)]
            qsT = [sbuf(M, "qsT%d" % i, [128, 128], BF16) for i in range(2)]
            kw = [sbuf(M, "kw%d" % i, [128, 128], BF16) for i in range(2)]
            mc = [sbuf(M, "mc%d" % i, [128, 8], F32) for i in range(2)]
            Cf = [sbuf(M, "Cf%d" % i, [128, 4, 257], F32) for i in range(2)]
            Cb = [sbuf(M, "Cb%d" % i, [128, 4, 258], BF16) for i in range(2)]
            hout = sbuf(M, "hout", [128, 1024], F32)
            hbl = sbuf(M, "hbl", [128, 1024], F32)

            P.dma('sp', gqk_t[:], gqk.partition_broadcast(128), writes=['gqk_t'])
            P.dma('sp', bgt[:], bg.partition_broadcast(128), writes=['bgt'])
            P.dma('sp', mln_t[:], mln_d.partition_broadcast(128), writes=['mln_t'])
            w_in_v = w_in_b.rearrange("(c p) n -> p c n", p=128)
            P.dma('sp', wg[:], w_in_v[:, :, 4608:4624], reads=['w_in_b'], writes=['wg'])
            memset('pool', VA[:], 1.0, ['VA'])
            memset('pool', mv_aug[:], 1.0, ['mv_aug'])
            for d in range(2):
                memset('pool', Cf[d][:], 0.0, [('Cf', d, h) for h in range(4)])
                memset('pool', Cb[d][:], 0.0, [('Cb', d, h) for h in range(4)])

            st = {'xslot': 0, 'wslot': 0, 'rslot': 0, 'bank': 0, 'mi': 0}

            def load_w(col0, ncols):
                slot = st['wslot'] % 2
                st['wslot'] += 1
                for q in range(4):
                    P.dma('sp', wb[slot][:, q * 4:(q + 1) * 4, 0:ncols], w_in_v[:, q * 4:(q + 1) * 4, col0:col0 + ncols],
                          reads=['w_in_b'], writes=[('wb', slot)])
                return slot

            def nbank():
                b = st['bank'] % 6
                st['bank'] += 1
                return b

            def load_group(tiles):
                for i, tile in enumerate(tiles):
                    slot = st['xslot'] % 2
                    st['xslot'] += 1
                    P.dma('pool', xb[slot][:], xl[tile * 128:(tile + 1) * 128, :], writes=[('xb', slot)])
                    transpose16(xb[slot], ('xb', slot), xT, 'xT', i * 128)

            def proj_tok(i, wslot, c0, ncols, bank):
                for c in range(16):
                    mm(ps[bank][:, 0:ncols], xT[:, c, i * 128:(i + 1) * 128], wb[wslot][:, c, c0:c0 + ncols],
                       c == 0, c == 15, ['xT', ('wb', wslot)], [PS(bank)])

            def proj_feat(j, wslot, bank):
                for c in range(16):
                    mm(ps[bank][:, 0:512], wb[wslot][:, c, j * 128:(j + 1) * 128], xT[:, c, :],
                       c == 0, c == 15, ['xT', ('wb', wslot)], [PS(bank)])

            def norm_rope(psap, bank, H, gain, rslot, outb):
                W = H * 128
                cp('act', t0[:, 0:W], psap, [], [PS(bank), 't0'])
                tt('dve', t1[:, 0:W], t0[:, 0:W], t0[:, 0:W], ALU.mult, ['t0'], ['t1'])
                red(sm[:, 0:H], t1[:, 0:W].rearrange("p (h d) -> p h d", h=H), ['t1'], ['sm'])
                act(sm[:, 8:8 + H], sm[:, 0:H], AF.Sqrt, ['sm'], ['sm'], bias=eps_rms[:, 0:1], scale=1.0 / 128.0)
                recip(sm[:, 16:16 + H], sm[:, 8:8 + H], ['sm'], ['sm'])
                t0v = t0[:, 0:W].rearrange("p (h d) -> p h d", h=H)
                t1v = t1[:, 0:W].rearrange("p (h d) -> p h d", h=H)
                tt('dve', t1v, t0v, sm[:, 16:16 + H].unsqueeze(2).to_broadcast([128, H, 128]), ALU.mult,
                   ['t0', 'sm'], ['t1'])
                tt('dve', t1v, t1v, gain.unsqueeze(1).to_broadcast([128, H, 128]), ALU.mult, ['t1', 'gqk_t'], ['t1'])
                rt = ropeT[rslot]
                tt('dve', t0v, t1v, rt[:, 0:128].unsqueeze(1).to_broadcast([128, H, 128]), ALU.mult,
                   ['t1', ('rope', rslot)], ['t0'])
                t1z = t1[:, 0:W].rearrange("p (h a z d) -> p h a z d", h=H, a=2, z=2)
                sz = rt[:, 128:256].rearrange("p (a z d) -> p a z d", a=2, z=2)
                q2 = qk2[:, 0:W].rearrange("p (h a z d) -> p h a z d", h=H, a=2, z=2)
                for z in range(2):
                    tt('dve', q2[:, :, :, z, :], t1z[:, :, :, 1 - z, :],
                       sz[:, :, z, :].unsqueeze(1).to_broadcast([128, H, 2, 32]), ALU.mult,
                       ['t1', ('rope', rslot)], ['qk2'])
                tt('dve', outb, t0[:, 0:W], qk2[:, 0:W], ALU.add, ['t0', 'qk2'], ['qkb'])

            qk2 = sbuf(M, "qk2", [128, 512], F32)
            eps_rms = sbuf(M, "eps_rms", [128, 2], F32)
            memset('pool', eps_rms[:, 0:1], RMS_EPS, ['eps_rms'])
            memset('pool', eps_rms[:, 1:2], 1.0, ['eps_rms'])

            def load_rope(tile):
                slot = st['rslot'] % 2
                st['rslot'] += 1
                P.dma('sp', ropeT[slot][:], rope[tile * 128:(tile + 1) * 128, :], writes=[('rope', slot)])
                return slot

            def do_kv(tiles, wslot):
                for i, tile in enumerate(tiles):
                    bank = nbank()
                    proj_tok(i, wslot, 0, 512, bank)
                    rs = load_rope(tile)
                    cp('act', VA[:, tile, :, 0:128], ps[bank][:, 256:512].rearrange("p (g d) -> p g d", g=2),
                       [], [PS(bank), 'VA'])
                    norm_rope(ps[bank][:, 0:256], bank, 2, gqk_t[:, 128:256], rs, qkb[:, 0:256])
                    tb = 6 + (rr_state['tb'] % 2)
                    rr_state['tb'] += 1
                    psb = ps[tb][:].bitcast(BF16)
                    for g in range(2):
                        tr(psb[:, g * 128:(g + 1) * 128], qkb[:, g * 128:(g + 1) * 128], ['qkb'], [PS(tb)])
                    cp('act', KT[:, :, tile * 128:(tile + 1) * 128], psb[:, 0:256].rearrange("p (g t) -> p g t", g=2),
                       [], [PS(tb), 'KT'])

            def do_gates(n):
                for i in range(n):
                    bank = nbank()
                    for c in range(16):
                        mm(ps[bank][:, 0:16], xT[:, c, i * 128:(i + 1) * 128], wg[:, c, :], c == 0, c == 15,
                           ['xT', 'wg'], [PS(bank)])
                    stt(ig[:, i, :], ps[bank][:, 0:8], -0.5 * math.log(128.0), bgt[:, 0:8], ALU.add, ALU.add,
                        ['bgt'], [PS(bank), 'ig'])
                    tt('dve', sm[:, 32:40], ps[bank][:, 8:16], bgt[:, 8:16], ALU.add, ['bgt'], [PS(bank), 'sm'])
                    act(sm[:, 40:48], sm[:, 32:40], AF.Exp, ['sm'], ['sm'], scale=-1.0)
                    act(sm[:, 48:56], sm[:, 40:48], AF.Ln, ['sm'], ['sm'], bias=eps_rms[:, 1:2], scale=1.0)
                    ts('dve', lf[:, i, :], sm[:, 48:56], -1.0, 0.0, ALU.mult, ALU.add, ['sm'], ['lf'])

            def do_mk_tok(n, wslot):
                for i in range(n):
                    bank = nbank()
                    proj_tok(i, wslot, 0, 512, bank)
                    cp('act' if i % 2 == 0 else 'dve', mk_tok[:, i, :], ps[bank][:, 0:512], [], [PS(bank), 'mk_tok'])

            def do_mv(n, wslot, half):
                for i in range(n):
                    bank = nbank()
                    proj_tok(i, wslot, 0, 512, bank)
                    cp('dve' if i % 2 == 0 else 'act', mv_aug[:, i, half * 2:half * 2 + 2, 0:256],
                       ps[bank][:, 0:512].rearrange("p (h d) -> p h d", h=2), [], [PS(bank), 'mv_aug'])

            def do_feat(dst, dst_res, wslot):
                for j in range(4):
                    bank = nbank()
                    proj_feat(j, wslot, bank)
                    cp('act' if j % 2 == 0 else 'dve', dst[:, j, :], ps[bank][:, 0:512], [], [PS(bank), dst_res])

            def mlstm_head(i, d, h, k, with_out, add_hbl):
                MASK = Umask if d == 0 else Lmask
                NEGM = NEGU if d == 0 else NEGL
                last = 127 if d == 0 else 0
                if True:
                    j = d * 4 + h
                    bX, bY, bZ = (0, 1, 2) if k == 0 else (3, 4, 5)
                    fcol = lf[:, i, j:j + 1]
                    icol = ig[:, i, j:j + 1]
                    ts('dve', Fm[k][:], MASK, fcol, 0.0, ALU.mult, ALU.add, ['cst', 'lf'], [('Fm', k)])
                    yield
                    mm(ps[bX][:, 0:128], ones_f, Fm[k][:], True, True, ['cst', ('Fm', k)], [PS(bX)])
                    yield
                    if with_out:
                        mm(ps[bX][:, 128:256], ones_f, Fm[k][:], True, False, ['cst', ('Fm', k)], [PS(bX)])
                        yield
                        mm(ps[bX][:, 128:256], ident_f, NEGM, False, True, ['cst'], [PS(bX)])
                        yield
                    mm(ps[bX][:, 256:257], MASK, fcol, True, True, ['cst', 'lf'], [PS(bX)])
                    yield
                    tt('dve', mc[k][:, 0:1], icol, ps[bX][:, 256:257], ALU.subtract, ['ig'], [PS(bX), ('mc', k)])
                    yield
                    act(mc[k][:, 1:2], ps[bX][:, last:last + 1], AF.Exp, [('mc', k)], [PS(bX), ('mc', k)], bias=mc[k][:, 0:1], scale=1.0)
                    yield
                    act(mc[k][:, 2:3], ps[bX][:, last:last + 1], AF.Exp, [('mc', k)], [PS(bX), ('mc', k)])
                    yield
                    if with_out:
                        act(EB[k][:], ps[bX][:, 0:128], AF.Exp, [], [PS(bX), ('EB', k)])
                        yield
                        act(DT[k][:], ps[bX][:, 128:256], AF.Exp, [('mc', k)], [PS(bX), ('DT', k)], bias=mc[k][:, 0:1], scale=1.0)
                        yield
                        mm(ps[bX][:, 384:512], mkT[:, h, i * 128:(i + 1) * 128], mqT[:, h, i * 128:(i + 1) * 128],
                           True, True, ['mkT', 'mqT'], [PS(bX)])
                        yield
                        tt('dve', PTm[k][:], ps[bX][:, 384:512], DT[k][:], ALU.mult, [('DT', k)], [PS(bX), ('PTm', k)])
                        yield
                        tt('dve', qsT[k][:], mqT[:, h, i * 128:(i + 1) * 128], EB[k][:], ALU.mult, ['mqT', ('EB', k)], [('qsT', k)])
                        yield
                        mm(ps[bY][:, 0:257], PTm[k][:], mv_aug[:, i, h, 0:257], True, False, [('PTm', k), 'mv_aug'], [PS(bY)])
                        yield
                        mm(ps[bY][:, 0:257], qsT[k][:], Cb[d][:, h, 0:257], False, True, [('qsT', k), ('Cb', d, h)], [PS(bY)])
                        yield
                        act(mc[k][:, 3:4], ps[bY][:, 256:257], AF.Abs, [('mc', k)], [PS(bY), ('mc', k)])
                        yield
                        ts('dve', mc[k][:, 4:5], mc[k][:, 3:4], 1.0, 0.0, ALU.max, ALU.add, [('mc', k)], [('mc', k)])
                        yield
                        recip(mc[k][:, 5:6], mc[k][:, 4:5], [('mc', k)], [('mc', k)])
                        yield
                        if add_hbl:
                            stt(hout[:, h * 256:(h + 1) * 256], ps[bY][:, 0:256], mc[k][:, 5:6], hbl[:, h * 256:(h + 1) * 256],
                                ALU.mult, ALU.add, [('mc', k), 'hbl'], [PS(bY), ('hout', h)])
                            yield
                        else:
                            ts('dve', hout[:, h * 256:(h + 1) * 256], ps[bY][:, 0:256], mc[k][:, 5:6], 0.0, ALU.mult, ALU.add,
                               [('mc', k)], [PS(bY), ('hout', h)])
                            yield
                    ts('dve', kw[k][:], mk_tok[:, i, h * 128:(h + 1) * 128], mc[k][:, 1:2], 0.0, ALU.mult, ALU.add,
                       ['mk_tok', ('mc', k)], [('kw', k)])
                    yield
                    mm(ps[bZ][:, 0:257], kw[k][:], mv_aug[:, i, h, 0:257], True, True, [('kw', k), 'mv_aug'], [PS(bZ)])
                    yield
                    stt(Cf[d][:, h, :], Cf[d][:, h, :], mc[k][:, 2:3], ps[bZ][:, 0:257], ALU.mult, ALU.add,
                        [('mc', k)], [PS(bZ), ('Cf', d, h)])
                    yield
                    cp('act', Cb[d][:, h, 0:257], Cf[d][:, h, :], [('Cf', d, h)], [('Cb', d, h)])
                    yield


            def run_rr(gens):
                gens = list(gens)
                while gens:
                    for g in list(gens):
                        try:
                            next(g)
                        except StopIteration:
                            gens.remove(g)

            def mlstm_tile(i, d, with_out, add_hbl):
                for pair in ((0, 1), (2, 3)):
                    run_rr([mlstm_head(i, d, h, k, with_out, add_hbl) for k, h in enumerate(pair)])

            for grp in range(7, 3, -1):
                tiles = [grp * 4 + i for i in range(4)]
                load_group(tiles)
                cast_w(6)
                ws = load_w(1024, 512)
                do_kv(tiles, ws)
                do_gates(4)
                ws = load_w(2048, 512)
                do_mk_tok(4, ws)
                ws = load_w(2560, 512)
                do_mv(4, ws, 0)
                ws = load_w(3072, 512)
                do_mv(4, ws, 1)
                for i in range(3, -1, -1):
                    mlstm_tile(i, 1, False, False)

            for grp in range(3, -1, -1):
                tiles = [grp * 4 + i for i in range(4)]
                load_group(tiles)
                cast_w(6)
                ws = load_w(1024, 512)
                do_kv(tiles, ws)
                do_gates(4)
                ws = load_w(1536, 512)
                do_feat(mqT, 'mqT', ws)
                ws = load_w(2048, 512)
                do_feat(mkT, 'mkT', ws)
                do_mk_tok(4, ws)
                ws = load_w(2560, 512)
                do_mv(4, ws, 0)
                ws = load_w(3072, 512)
                do_mv(4, ws, 1)
                for i in range(3, -1, -1):
                    mlstm_tile(i, 1, True, False)
                    tile = tiles[i]
                    P.dma('sp', hb_d[tile * 128:(tile + 1) * 128, :], hout[:], reads=[('hout', 0), ('hout', 1), ('hout', 2), ('hout', 3)], writes=[('hb', tile)],
                          semkey='hout_st')

            for grp in range(4):
                tiles = [grp * 4 + i for i in range(4)]
                load_group(tiles)
                cast_w(6)
                for blk in range(2):
                    ws = load_w(blk * 512, 512)
                    for i, tile in enumerate(tiles):
                        bank = nbank()
                        proj_tok(i, ws, 0, 512, bank)
                        rs = load_rope(tile)
                        norm_rope(ps[bank][:, 0:512], bank, 4, gqk_t[:, 0:128], rs, qkb[:, 0:512])
                        tb = 6 + (rr_state['tb'] % 2)
                        rr_state['tb'] += 1
                        psb = ps[tb][:].bitcast(BF16)
                        for hh in range(4):
                            tr(psb[:, hh * 128:(hh + 1) * 128], qkb[:, hh * 128:(hh + 1) * 128], ['qkb'], [PS(tb)])
                        cp('act', QT[:, blk * 4:(blk + 1) * 4, i * 128:(i + 1) * 128],
                           psb[:, 0:512].rearrange("p (h t) -> p h t", h=4), [], [PS(tb), 'QT'])
                do_gates(4)
                ws = load_w(1536, 512)
                do_feat(mqT, 'mqT', ws)
                ws = load_w(2048, 512)
                do_feat(mkT, 'mkT', ws)
                do_mk_tok(4, ws)
                ws = load_w(2560, 512)
                do_mv(4, ws, 0)
                ws = load_w(3072, 512)
                do_mv(4, ws, 1)
                for blk in range(2):
                    ws = load_w(3584 + blk * 512, 512)
                    for i in range(4):
                        bank = nbank()
                        proj_tok(i, ws, 0, 512, bank)
                        act(mo_sig[:, i, blk * 512:(blk + 1) * 512], ps[bank][:, 0:512], AF.Sigmoid, [], [PS(bank), 'mo_sig'])
                its = [(hq, kt) for hq in range(8) for kt in range(32)]

                def issue_S(n):
                    hq, kt = its[n]
                    g = hq // 4
                    bS = 4 + (n % 2)
                    pt = n % 3
                    mm(ps[bS][:, 0:512], KT[:, g, kt * 128:(kt + 1) * 128], QT[:, hq, :], True, True, ['KT', 'QT'], [PS(bS)])
                    act(PTb[pt][:], ps[bS][:, 0:512], AF.Exp, [], [PS(bS), ('PTb', pt)], scale=128.0 ** -0.5)

                def issue_PV(n):
                    hq, kt = its[n]
                    g = hq // 4
                    pt = n % 3
                    for qs in range(4):
                        mm(ps[qs][:, 0:129], PTb[pt][:, qs * 128:(qs + 1) * 128], VA[:, kt, g, 0:129], kt == 0, kt == 31,
                           [('PTb', pt), 'VA'], [PS(qs)])
                    if kt == 31:
                        for qs in range(4):
                            recip(sm[:, 56 + qs:57 + qs], ps[qs][:, 128:129], [], [PS(qs), 'sm'])
                            ts('dve', att_g[:, qs, hq * 128:(hq + 1) * 128], ps[qs][:, 0:128], sm[:, 56 + qs:57 + qs], 0.0,
                               ALU.mult, ALU.add, ['sm'], [PS(qs), 'att_g'])

                issue_S(0)
                for n in range(len(its)):
                    if n + 1 < len(its):
                        issue_S(n + 1)
                    issue_PV(n)
                for i, tile in enumerate(tiles):
                    P.dma('sp', hbl[:], hb_d[tile * 128:(tile + 1) * 128, :], reads=[('hb', tile)], writes=['hbl'])
                    mlstm_tile(i, 0, True, True)
                    tt('dve', t0[:], hout[:], hout[:], ALU.mult, [('hout', 0), ('hout', 1), ('hout', 2), ('hout', 3)], ['t0'])
                    red(sm[:, 0:4], t0[:].rearrange("p (h d) -> p h d", h=4), ['t0'], ['sm'])
                    act(sm[:, 8:12], sm[:, 0:4], AF.Sqrt, ['sm'], ['sm'], bias=eps_rms[:, 0:1], scale=1.0 / 256.0)
                    recip(sm[:, 16:20], sm[:, 8:12], ['sm'], ['sm'])
                    tt('dve', t0[:].rearrange("p (h d) -> p h d", h=4), hout[:].rearrange("p (h d) -> p h d", h=4),
                       sm[:, 16:20].unsqueeze(2).to_broadcast([128, 4, 256]), ALU.mult, [('hout', 0), ('hout', 1), ('hout', 2), ('hout', 3)] + ['sm'], ['t0'])
                    tt('pool', t1[:], t0[:], mln_t[:], ALU.mult, ['t0', 'mln_t'], ['t1'])
                    ms = i % 2
                    tt('pool', mlb[ms][:], t1[:], mo_sig[:, i, :], ALU.mult, ['t1', 'mo_sig'], [('mlb', ms)])
                    P.dma('sp', cat_d[tile * 128:(tile + 1) * 128, 1024:2048], mlb[ms][:], reads=[('mlb', ms)],
                          writes=[('cat_ml', tile)], semkey=('mlb_st', ms))
                    P.dma('sp', cat_d[tile * 128:(tile + 1) * 128, 0:1024], att_g[:, i, :], reads=['att_g'],
                          writes=[('cat_att', tile)], semkey='att_st')
            P.barrier()

        if stage == 'mix':
            with contextlib.ExitStack() as Dg:
                cb = sbuf(Dg, "dbg_cb", [128, 2048], BF16)
                for tile in range(16):
                    P.dma('sp', cb[:], cat_d[tile * 128:(tile + 1) * 128, :], reads=[('cat_ml', tile), ('cat_att', tile)], writes=['dbg_cb'])
                    final.append(P.dma('sp', out_d[tile * 128:(tile + 1) * 128, :], cb[:], reads=['dbg_cb'], writes=[('out', tile)],
                                       semkey='dbg_out'))
            P.emit(final_tokens=final[-1:])
            return nc

        with contextlib.ExitStack() as E:
            xbE = [sbuf(E, "xbE%d" % i, [128, 2048], BF16) for i in range(2)]
            aT = sbuf(E, "aT", [128, 16, 512], BF16)
            wbD = [sbuf(E, "wbD%d" % i, [128, 16, 512], BF16) for i in range(2)]
            r_g = sbuf(E, "r_g", [128, 4, 2048], F32)
            ln_g = sbuf(E, "ln_g", [128, 2048], F32)
            ln_b = sbuf(E, "ln_b", [128, 2048], F32)
            st6 = sbuf(E, "st6", [128, 4, 6], F32)
            lmv = sbuf(E, "lmv", [128, 8], F32)
            eps_ln = sbuf(E, "eps_ln", [128, 1], F32)
            memT = sbuf(E, "memT", [128, 16, 256], BF16)
            kmT = sbuf(E, "kmT", [128, 16, 256], BF16)
            vm = sbuf(E, "vm", [128, 2, 2048], BF16)
            q1T = sbuf(E, "q1T", [128, 16, 512], BF16)
            o_g = sbuf(E, "o_g", [128, 4, 2048], BF16)
            PTx = [sbuf(E, "PTx%d" % i, [128, 512], BF16) for i in range(4)]
            rrx = sbuf(E, "rrx", [128, 8], F32)
            memset('pool', eps_ln[:], LN_EPS, ['eps_ln'])
            sE = {'w': 0, 'x': 0, 'bank': 0, 'pt': 0}

            def load_wE(W, col0, bf=False):
                slot = sE['w'] % 2
                sE['w'] += 1
                Wv = W.rearrange("(c p) n -> p c n", p=128)
                for q in range(4):
                    P.dma('sp' if bf else 'pool', wbD[slot][:, q * 4:(q + 1) * 4, :], Wv[:, q * 4:(q + 1) * 4, col0:col0 + 512],
                          writes=[('wbD', slot)], semkey=('wbD', slot, bf))
                return slot

            def nbE():
                b = sE['bank'] % 6
                sE['bank'] += 1
                return b

            def load_T(src_rows, dst, dst_res, col0, cast, extra_reads=()):
                slot = sE['x'] % 2
                sE['x'] += 1
                P.dma('pool' if cast else 'sp', xbE[slot][:], src_rows, reads=list(extra_reads), writes=[('xbE', slot)])
                transpose16(xbE[slot], ('xbE', slot), dst, dst_res, col0)

            def load_ln(k):
                P.dma('sp', ln_g[:], lnp[2 * k:2 * k + 1, :].partition_broadcast(128), writes=['ln_g'])
                P.dma('sp', ln_b[:], lnp[2 * k + 1:2 * k + 2, :].partition_broadcast(128), writes=['ln_b'])

            def dense_ln(W, tiles, out_d_, out_key, bf=True):
                for cbk in range(4):
                    ws = load_wE(W, cbk * 512, bf)
                    for i in range(4):
                        bank = nbE()
                        for c in range(16):
                            mm(ps[bank][:, 0:512], aT[:, c, i * 128:(i + 1) * 128], wbD[ws][:, c, :], c == 0, c == 15,
                               ['aT', ('wbD', ws)], [PS(bank)])
                        stt(r_g[:, i, cbk * 512:(cbk + 1) * 512], r_g[:, i, cbk * 512:(cbk + 1) * 512], ALPHA,
                            ps[bank][:, 0:512], ALU.mult, ALU.add, [], [PS(bank), ('r_g', i)])
                for i, tile in enumerate(tiles):
                    for q in range(4):
                        op_bnstats(st6[:, q, :], r_g[:, i, q * 512:(q + 1) * 512], [('r_g', i)], ['st6'])
                    op_bnaggr(lmv[:, 0:2], st6[:].rearrange("p a b -> p (a b)"), ['st6'], ['lmv'])
                    act(lmv[:, 2:3], lmv[:, 1:2], AF.Sqrt, ['lmv', 'eps_ln'], ['lmv'], bias=eps_ln[:, 0:1], scale=1.0)
                    recip(lmv[:, 3:4], lmv[:, 2:3], ['lmv'], ['lmv'])
                    stt(lmv[:, 4:5], lmv[:, 0:1], -1.0, lmv[:, 3:4], ALU.mult, ALU.mult, ['lmv'], ['lmv'])
                    act(r_g[:, i, :], r_g[:, i, :], AF.Identity, ['lmv'], [('r_g', i)], bias=lmv[:, 4:5], scale=lmv[:, 3:4])
                    tt('dve', r_g[:, i, :], r_g[:, i, :], ln_g[:], ALU.mult, ['ln_g'], [('r_g', i)])
                    tt('dve', r_g[:, i, :], r_g[:, i, :], ln_b[:], ALU.add, ['ln_b'], [('r_g', i)])
                    tk = P.dma('sp', out_d_[tile * 128:(tile + 1) * 128, :], r_g[:, i, :], reads=[('r_g', i)],
                               writes=[(out_key, tile)], semkey=('r_g_st', i))
                    if out_key == 'out':
                        final.append(tk)

            load_ln(0)
            for grp in range(4):
                tiles = [grp * 4 + i for i in range(4)]
                for i, tile in enumerate(tiles):
                    P.dma('sp', r_g[:, i, :], xl[tile * 128:(tile + 1) * 128, :], writes=[('r_g', i)])
                    load_T(cat_d[tile * 128:(tile + 1) * 128, :], aT, 'aT', i * 128, False,
                           extra_reads=[('cat_ml', tile), ('cat_att', tile)])
                dense_ln(w_out_b, tiles, x1_d, 'x1')

            load_ln(1)
            for mt in range(2):
                load_T(memb[mt * 128:(mt + 1) * 128, :], memT, 'memT', mt * 128, True)
            for cbk in range(4):
                ws = load_wE(xa_wk, cbk * 512)
                for j in range(4):
                    bank = nbE()
                    for c in range(16):
                        mm(ps[bank][:, 0:256], wbD[ws][:, c, j * 128:(j + 1) * 128], memT[:, c, :], c == 0, c == 15,
                           ['memT', ('wbD', ws)], [PS(bank)])
                    cp('act' if j % 2 == 0 else 'dve', kmT[:, cbk * 4 + j, :], ps[bank][:, 0:256], [], [PS(bank), 'kmT'])
            for cbk in range(4):
                ws = load_wE(xa_wv, cbk * 512)
                for mt in range(2):
                    bank = nbE()
                    for c in range(16):
                        mm(ps[bank][:, 0:512], memT[:, c, mt * 128:(mt + 1) * 128], wbD[ws][:, c, :], c == 0, c == 15,
                           ['memT', ('wbD', ws)], [PS(bank)])
                    cp('act' if mt % 2 == 0 else 'dve', vm[:, mt, cbk * 512:(cbk + 1) * 512], ps[bank][:, 0:512], [], [PS(bank), 'vm'])
            for grp in range(4):
                tiles = [grp * 4 + i for i in range(4)]
                for i, tile in enumerate(tiles):
                    P.dma('sp', r_g[:, i, :], x1_d[tile * 128:(tile + 1) * 128, :], reads=[('x1', tile)], writes=[('r_g', i)])
                    load_T(x1_d[tile * 128:(tile + 1) * 128, :], aT, 'aT', i * 128, True, extra_reads=[('x1', tile)])
                for cbk in range(4):
                    ws = load_wE(xa_wq_b, cbk * 512, True)
                    for j in range(4):
                        bank = nbE()
                        for c in range(16):
                            mm(ps[bank][:, 0:512], wbD[ws][:, c, j * 128:(j + 1) * 128], aT[:, c, :], c == 0, c == 15,
                               ['aT', ('wbD', ws)], [PS(bank)])
                        cp('act' if j % 2 == 0 else 'dve', q1T[:, cbk * 4 + j, :], ps[bank][:, 0:512], [], [PS(bank), 'q1T'])
                for h in range(4):
                    pts = []
                    for mt in range(2):
                        bank = nbE()
                        for dc in range(4):
                            mm(ps[bank][:, 0:512], kmT[:, h * 4 + dc, mt * 128:(mt + 1) * 128], q1T[:, h * 4 + dc, :],
                               dc == 0, dc == 3, ['kmT', 'q1T'], [PS(bank)])
                        pt = sE['pt'] % 4
                        sE['pt'] += 1
                        act(PTx[pt][:], ps[bank][:, 0:512], AF.Exp, [], [PS(bank), ('PTx', pt)], scale=512.0 ** -0.5)
                        pts.append(pt)
                    for qs in range(4):
                        bank = nbE()
                        for mt in range(2):
                            mm(ps[bank][:, 0:512], PTx[pts[mt]][:, qs * 128:(qs + 1) * 128], vm[:, mt, h * 512:(h + 1) * 512],
                               mt == 0, mt == 1, [('PTx', pts[mt]), 'vm'], [PS(bank)])
                        b2 = 6 + (qs % 2)
                        for mt in range(2):
                            mm(ps[b2][:, 0:1], PTx[pts[mt]][:, qs * 128:(qs + 1) * 128], ones_b[:, 0:1],
                               mt == 0, mt == 1, [('PTx', pts[mt]), 'ones_b'], [PS(b2)])
                        recip(rrx[:, qs:qs + 1], ps[b2][:, 0:1], [], [PS(b2), 'rrx'])
                        ts('dve', o_g[:, qs, h * 512:(h + 1) * 512], ps[bank][:, 0:512], rrx[:, qs:qs + 1], 0.0, ALU.mult, ALU.add,
                           ['rrx'], [PS(bank), ('o_g', qs)])
                for i in range(4):
                    transpose16(o_g[:, i, :], ('o_g', i), aT, 'aT', i * 128)
                dense_ln(xa_wo_b, tiles, x2_d, 'x2')
            P.barrier()

        if stage == 'de':
            with contextlib.ExitStack() as Dg:
                cb2 = sbuf(Dg, "dbg_cb2", [128, 2048], F32)
                for tile in range(16):
                    P.dma('sp', cb2[:], x2_d[tile * 128:(tile + 1) * 128, :], reads=[('x2', tile)], writes=['dbg_cb2'])
                    final.append(P.dma('sp', out_d[tile * 128:(tile + 1) * 128, :], cb2[:], reads=['dbg_cb2'], writes=[('out', tile)],
                                       semkey='dbg_out'))
            P.emit(final_tokens=final[-1:])
            return nc

        Wd4 = wd_d.rearrange("s p (j t) -> s p j t", j=128)
        with contextlib.ExitStack() as F1:
            xbF = [sbuf(F1, "xbF%d" % i, [128, 2048], BF16) for i in range(2)]
            x2Ta = sbuf(F1, "x2Ta", [128, 16, 512], BF16)
            wbF = [sbuf(F1, "wbF%d" % i, [128, 16, 128], BF16) for i in range(2)]
            qpT = sbuf(F1, "qpT", [128, 16, 512], F32)
            skS = sbuf(F1, "skS", [128, 16 * 128], F32)
            s_sbs = [sbuf(F1, "s_sb%d" % i, [128, 16, 128], F32) for i in range(1)]
            s2 = sbuf(F1, "s2", [128, 256], F32)
            vals = sbuf(F1, "vals", [128, 16, 16], F32)
            idx = sbuf(F1, "idx", [128, 16, 16], U32)
            idxf = sbuf(F1, "idxf", [128, 16, 16], F32)
            cand = sbuf(F1, "cand", [128, 8, 256], F32)
            cv = sbuf(F1, "cv", [128, 8, 16], F32)
            cpos = sbuf(F1, "cpos", [128, 8, 16], U32)
            rk = sbuf(F1, "rk", [128, 2, 128], U32)
            rkf = sbuf(F1, "rkf", [128, 2, 128], F32)
            oh = sbuf(F1, "oh", [128, 8, 16, 16], F32)
            sel = sbuf(F1, "sel", [128, 3, 128], F32)
            selT = sbuf(F1, "selT", [128, 3, 128], F32)
            gz = sbuf(F1, "gz", [128, 16], F32)
            OA = [sbuf(F1, "OA%d" % i, [128, 16, 128], BF16) for i in range(2)]
            OB = [sbuf(F1, "OB%d" % i, [128, 16, 128], BF16) for i in range(2)]
            Wt = [sbuf(F1, "Wt%d" % i, [128, 128, 128], BF16) for i in range(2)]
            P.dma('sp', skS[:], skT, writes=['skS'])
            pwq_v = pwq_b.rearrange("(c p) n -> p c n", p=128)
            wcnt = {'n': 0}

            def prep_group(grp):
                gs = grp % 2
                for i in range(2):
                    tile = grp * 2 + i
                    sl = tile % 2
                    P.dma('pool', xbF[sl][:], x2_d[tile * 128:(tile + 1) * 128, :], writes=[('xbF', sl)])
                    transpose16(xbF[sl], ('xbF', sl), x2Ta, ('x2Ta', gs), gs * 256 + i * 128)
                for hc in range(16):
                    ws = wcnt['n'] % 2
                    wcnt['n'] += 1
                    for q in range(2):
                        P.dma('sp', wbF[ws][:, q * 8:(q + 1) * 8, :], pwq_v[:, q * 8:(q + 1) * 8, hc * 128:(hc + 1) * 128],
                              writes=[('wbF', ws)])
                    bank = hc % 4
                    for c in range(16):
                        mm(ps[bank][:, 0:256], wbF[ws][:, c, :], x2Ta[:, c, gs * 256:(gs + 1) * 256], c == 0, c == 15,
                           [('x2Ta', gs), ('wbF', ws)], [PS(bank)])
                    cp('act', qpT[:, hc, gs * 256:(gs + 1) * 256], ps[bank][:, 0:256], [], [PS(bank), ('qpT', gs)])

            def scores(tile):
                gs = (tile // 2) % 2
                c0 = gs * 256 + (tile % 2) * 128
                for q4 in range(4):
                    bank = q4
                    for r4 in range(4):
                        hc = q4 * 4 + r4
                        mm(ps[bank][:, r4 * 128:(r4 + 1) * 128], qpT[:, hc, c0:c0 + 128], skS[:, hc * 128:(hc + 1) * 128],
                           True, True, [('qpT', gs), 'skS'], [PS(bank)])
                    cp('act', s_sbs[0][:, q4 * 4:(q4 + 1) * 4, :],
                       ps[bank][:, 0:512].rearrange("p (a k) -> p a k", a=4), [], [PS(bank), ('s_sb', 0)])

            def select(tile):
                s_sb = s_sbs[0]
                sk = ('s_sb', 0)
                for hc in range(16):
                    op_max(vals[:, hc, 0:8], s_sb[:, hc, :], [sk], ['vals'])
                    op_maxidx(idx[:, hc, 0:8], vals[:, hc, 0:8], s_sb[:, hc, :], [sk, 'vals'], ['idx'])
                    op_mrep(s2[:, 0:128], vals[:, hc, 0:8], s_sb[:, hc, :], [sk, 'vals'], ['s2'])
                    op_max(vals[:, hc, 8:16], s2[:, 0:128], ['s2'], ['vals'])
                    op_maxidx(idx[:, hc, 8:16], vals[:, hc, 8:16], s2[:, 0:128], ['s2', 'vals'], ['idx'])
                cp('dve', idxf[:], idx[:], ['idx'], ['idxf'])
                vv = vals[:].rearrange("p (h c) a -> p h c a", c=2)
                tt('dve', cand[:].rearrange("p h (a b) -> p h a b", a=16), vv[:, :, 0, :].unsqueeze(3).to_broadcast([128, 8, 16, 16]),
                   vv[:, :, 1, :].unsqueeze(2).to_broadcast([128, 8, 16, 16]), ALU.add, ['vals'], ['cand'])
                for h in range(8):
                    op_max(cv[:, h, 0:8], cand[:, h, :], ['cand'], ['cv'])
                    op_maxidx(cpos[:, h, 0:8], cv[:, h, 0:8], cand[:, h, :], ['cand', 'cv'], ['cpos'])
                    op_mrep(s2[:], cv[:, h, 0:8], cand[:, h, :], ['cand', 'cv'], ['s2'])
                    op_max(cv[:, h, 8:16], s2[:], ['s2'], ['cv'])
                    op_maxidx(cpos[:, h, 8:16], cv[:, h, 8:16], s2[:], ['s2', 'cv'], ['cpos'])
                g3 = sel[:, 2, :].rearrange("p (h c) -> p h c", h=8)
                tt('dve', g3, cv[:], cv[:, :, 0:1].to_broadcast([128, 8, 16]), ALU.subtract, ['cv'], ['sel'])
                act(g3, g3, AF.Exp, [], ['sel'])
                red(gz[:, 0:8], g3, ['sel'], ['gz'])
                recip(gz[:, 8:16], gz[:, 0:8], ['gz'], ['gz'])
                tt('dve', g3, g3, gz[:, 8:16].unsqueeze(2).to_broadcast([128, 8, 16]), ALU.mult, ['gz'], ['sel'])
                cpf = cpos[:].rearrange("p h c -> p (h c)")
                op_tss(rk[:, 0, :], cpf, 4, ALU.logical_shift_right, ['cpos'], ['rk'])
                op_tss(rk[:, 1, :], cpf, 15, ALU.bitwise_and, ['cpos'], ['rk'])
                cp('dve', rkf[:], rk[:], ['rk'], ['rkf'])
                idv = idxf[:].rearrange("p (h c) a -> p h c a", c=2)
                for half in range(2):
                    rv = rkf[:, half, :].rearrange("p (h c) -> p h c", h=8)
                    tt('dve', oh[:], rv.unsqueeze(3).to_broadcast([128, 8, 16, 16]),
                       iota_f[:, 0:16].unsqueeze(1).unsqueeze(1).to_broadcast([128, 8, 16, 16]), ALU.is_equal, ['rkf', 'cst'], ['oh'])
                    tt('dve', oh[:], oh[:], idv[:, :, half, :].unsqueeze(2).to_broadcast([128, 8, 16, 16]), ALU.mult, ['idxf'], ['oh'])
                    red(sel[:, half, :].rearrange("p (h c) -> p h c", h=8), oh[:], ['oh'], ['sel'])
                for q3 in range(3):
                    tr(ps[4][:, q3 * 128:(q3 + 1) * 128], sel[:, q3, :], ['sel'], [PS(4)], fp32=True)
                cp('act', selT[:], ps[4][:, 0:384].rearrange("p (a t) -> p a t", a=3), [], [PS(4), 'selT'])

            def expand(tile):
                wsl = tile % 2
                for tc in range(8):
                    k = tc % 2
                    tok0 = tc * 16
                    io = iota_f.unsqueeze(1).to_broadcast([128, 16, 128])
                    tt('dve', OA[k][:], io, selT[:, 0, tok0:tok0 + 16].unsqueeze(2).to_broadcast([128, 16, 128]), ALU.is_equal,
                       ['cst', 'selT'], [('OA', k)])
                    tt('dve', OA[k][:], OA[k][:], selT[:, 2, tok0:tok0 + 16].unsqueeze(2).to_broadcast([128, 16, 128]), ALU.mult,
                       ['selT'], [('OA', k)])
                    tt('dve', OB[k][:], io, selT[:, 1, tok0:tok0 + 16].unsqueeze(2).to_broadcast([128, 16, 128]), ALU.is_equal,
                       ['cst', 'selT'], [('OB', k)])
                    for q4 in range(4):
                        bank = 5 + ((tc * 4 + q4) % 3)
                        for r4 in range(4):
                            tl = q4 * 4 + r4
                            mm(ps[bank][:, r4 * 128:(r4 + 1) * 128], OA[k][:, tl, :], OB[k][:, tl, :], True, True,
                               [('OA', k), ('OB', k)], [PS(bank)])
                        a0 = tok0 + q4 * 4
                        cp('act', Wt[wsl][:, :, a0:a0 + 4],
                           ps[bank][:, 0:512].rearrange("p (t j) -> p j t", t=4), [], [PS(bank), ('Wt', wsl)])
                P.dma('sp', wd_d[tile], Wt[wsl][:].rearrange("p j t -> p (j t)"), reads=[('Wt', wsl)], writes=[('wd', tile)],
                      semkey=('Wt_st', wsl))

            prep_group(0)
            for grp in range(8):
                scores(grp * 2)
                select(grp * 2)
                expand(grp * 2)
                scores(grp * 2 + 1)
                if grp + 1 < 8:
                    prep_group(grp + 1)
                select(grp * 2 + 1)
                expand(grp * 2 + 1)
            P.barrier(new_epoch=False)

        for pp in range(2):
            with contextlib.ExitStack() as P2:
                pn = "p%d_" % pp
                x2T = sbuf(P2, pn + "x2T", [128, 16, 1024], BF16)
                acc_sb = sbuf(P2, pn + "acc", [128, 8, 2048], F32)
                with contextlib.ExitStack() as F2:
                    JB = 4
                    NS = 6
                    ub = [sbuf(F2, pn + "ub%d" % i, [128, 16, 128], BF16) for i in range(NS)]
                    vb = [sbuf(F2, pn + "vb%d" % i, [128, 2048], BF16) for i in range(NS)]
                    Wj4 = [sbuf(F2, pn + "Wj4%d" % i, [128, 8, 4, 128], BF16) for i in range(2)]
                    ga = [sbuf(F2, pn + "ga%d" % i, [128, 512], F32) for i in range(2)]
                    aTj = [sbuf(F2, pn + "aTj%d" % i, [128, 1024], BF16) for i in range(2 * JB)]
                    for s8 in range(8):
                        tile = pp * 8 + s8
                        P.dma('pool', vb[s8 % 2][:], x2_d[tile * 128:(tile + 1) * 128, :], writes=[('vb', s8 % 2)])
                        transpose16(vb[s8 % 2], ('vb', s8 % 2), x2T, 'x2T', s8 * 128)
                    accset = 0
                    gcnt = 0
                    for jb in range(128 // JB):
                        wq4 = jb % 2
                        j0 = jb * JB
                        for sh in range(2):
                            P.dma('sp', Wj4[wq4][:, sh * 4:(sh + 1) * 4, :, :],
                                  Wd4[pp * 8 + sh * 4:pp * 8 + sh * 4 + 4, :, j0:j0 + 4, :].rearrange("s p j t -> p s j t"),
                                  writes=[('Wj4', wq4)])
                        for jj in range(JB):
                            j = jb * JB + jj
                            sl = j % NS
                            sa = (jb % 2) * JB + jj
                            P.dma('pool', ub[sl][:], uh[j].rearrange("p (c i) -> p c i", c=16), writes=[('ub', sl)])
                            P.dma('pool', vb[sl][:], vh[j], writes=[('vb', sl)])
                            for half in range(2):
                                bank = (j % 2) * 2 + half
                                for c in range(16):
                                    mm(ps[bank][:, 0:512], ub[sl][:, c, :], x2T[:, c, half * 512:(half + 1) * 512], c == 0, c == 15,
                                       [('ub', sl), 'x2T'], [PS(bank)])
                                gs = gcnt % 2
                                gcnt += 1
                                act(ga[gs][:], ps[bank][:, 0:512], AF.Gelu, [], [PS(bank), ('ga', gs)])
                                tt('dve', aTj[sa][:, half * 512:(half + 1) * 512].rearrange("p (s t) -> p s t", s=4),
                                   ga[gs][:].rearrange("p (s t) -> p s t", s=4), Wj4[wq4][:, half * 4:(half + 1) * 4, jj, :], ALU.mult,
                                   [('ga', gs), ('Wj4', wq4)], [('aTj', sa)])
                        for s8 in range(8):
                            for cpair in range(2):
                                b0 = 4 + (accset % 2) * 2
                                accset += 1
                                for jj in range(JB):
                                    sl = (jb * JB + jj) % NS
                                    sa = (jb % 2) * JB + jj
                                    for cbk in range(2):
                                        col0 = (cpair * 2 + cbk) * 512
                                        mm(ps[b0 + cbk][:, 0:512], aTj[sa][:, s8 * 128:(s8 + 1) * 128], vb[sl][:, col0:col0 + 512],
                                           jj == 0, jj == JB - 1, [('aTj', sa), ('vb', sl)], [PS(b0 + cbk)])
                                for cbk in range(2):
                                    col0 = (cpair * 2 + cbk) * 512
                                    if jb == 0:
                                        cp('dve', acc_sb[:, s8, col0:col0 + 512], ps[b0 + cbk][:, 0:512], [], [PS(b0 + cbk), ('acc', s8)])
                                    else:
                                        tt('dve', acc_sb[:, s8, col0:col0 + 512], acc_sb[:, s8, col0:col0 + 512], ps[b0 + cbk][:, 0:512], ALU.add,
                                           [], [PS(b0 + cbk), ('acc', s8)])
                    P.barrier(new_epoch=False)
                with contextlib.ExitStack() as F3:
                    rF = [sbuf(F3, pn + "rF%d" % i, [128, 2048], F32) for i in range(2)]
                    lg = sbuf(F3, pn + "lg", [128, 2048], F32)
                    lb = sbuf(F3, pn + "lb", [128, 2048], F32)
                    st6f = sbuf(F3, pn + "st6f", [128, 4, 6], F32)
                    lmvf = sbuf(F3, pn + "lmvf", [128, 8], F32)
                    epsf = sbuf(F3, pn + "epsf", [128, 1], F32)
                    memset('pool', epsf[:], LN_EPS, ['epsf'])
                    P.dma('sp', lg[:], lnp[4:5, :].partition_broadcast(128), writes=['lg'])
                    P.dma('sp', lb[:], lnp[5:6, :].partition_broadcast(128), writes=['lb'])
                    for s8 in range(8):
                        tile = pp * 8 + s8
                        r_ = rF[s8 % 2]
                        rk_ = ('rF', s8 % 2)
                        P.dma('sp', r_[:], x2_d[tile * 128:(tile + 1) * 128, :], writes=[rk_])
                        stt(r_[:], r_[:], ALPHA, acc_sb[:, s8, :], ALU.mult, ALU.add, [('acc', s8)], [rk_])
                        for q in range(4):
                            op_bnstats(st6f[:, q, :], r_[:, q * 512:(q + 1) * 512], [rk_], ['st6f'])
                        op_bnaggr(lmvf[:, 0:2], st6f[:].rearrange("p a b -> p (a b)"), ['st6f'], ['lmvf'])
                        act(lmvf[:, 2:3], lmvf[:, 1:2], AF.Sqrt, ['lmvf', 'epsf'], ['lmvf'], bias=epsf[:, 0:1], scale=1.0)
                        recip(lmvf[:, 3:4], lmvf[:, 2:3], ['lmvf'], ['lmvf'])
                        stt(lmvf[:, 4:5], lmvf[:, 0:1], -1.0, lmvf[:, 3:4], ALU.mult, ALU.mult, ['lmvf'], ['lmvf'])
                        act(r_[:], r_[:], AF.Identity, ['lmvf'], [rk_], bias=lmvf[:, 4:5], scale=lmvf[:, 3:4])
                        tt('dve', r_[:], r_[:], lg[:], ALU.mult, ['lg'], [rk_])
                        tt('dve', r_[:], r_[:], lb[:], ALU.add, ['lb'], [rk_])
                        final.append(P.dma('sp', out_d[tile * 128:(tile + 1) * 128, :], r_[:], reads=[rk_],
                                           writes=[('out', tile)], semkey=('rF_st', s8 % 2)))
                    P.barrier(new_epoch=False)
        P.emit(final_tokens=final)
        return nc
    return nc


def _consts():
    c = np.zeros((128, 1024), np.float32)
    idx = np.arange(128)
    c[:, 0:128] = np.eye(128, dtype=np.float32)
    c[:, 128:256] = (idx[:, None] <= idx[None, :]).astype(np.float32)
    c[:, 256:384] = (idx[:, None] >= idx[None, :]).astype(np.float32)
    c[:, 384:512] = np.where(idx[:, None] <= idx[None, :], 0.0, NEG)
    c[:, 512:640] = np.where(idx[:, None] >= idx[None, :], 0.0, NEG)
    c[:, 640:768] = 1.0
    c[:, 768:896] = idx[None, :].astype(np.float32)
    return c


def _rope_table(pos):
    pos = np.asarray(pos)
    row = (pos // 64).astype(np.float32)
    col = (pos % 64).astype(np.float32)
    n_freq = 32
    inv_freq = (np.float32(10000.0) ** (-np.arange(n_freq, dtype=np.float32) / np.float32(n_freq))).astype(np.float32)
    ang_r = (row[:, None] * inv_freq).astype(np.float32)
    ang_c = (col[:, None] * inv_freq).astype(np.float32)
    cr, sr, cc, sc = np.cos(ang_r), np.sin(ang_r), np.cos(ang_c), np.sin(ang_c)
    tab = np.concatenate([cr, cr, cc, cc, -sr, sr, -sc, sc], axis=1).astype(np.float32)
    return np.ascontiguousarray(tab)


def prep_inputs(inp, cores=range(8)):
    f = lambda a: np.ascontiguousarray(np.asarray(a, dtype=np.float32))
    x = f(inp['x'])
    mem = f(inp['mem'])
    w_in = f(inp['w_in'])[0]
    w_in_sw = w_in.copy()
    w_in_sw[:, 4608:4612] = w_in[:, 4612:4616]
    w_in_sw[:, 4612:4616] = w_in[:, 4608:4612]
    w_in_sw[:, 4616:4620] = w_in[:, 4620:4624]
    w_in_sw[:, 4620:4624] = w_in[:, 4616:4620]
    bi = f(inp['b_igate'])[0]
    bf = f(inp['b_fgate'])[0]
    bg0 = np.concatenate([bi.reshape(8), bf.reshape(8)])[None, :]
    bg1 = np.concatenate([bi[::-1].reshape(8), bf[::-1].reshape(8)])[None, :]
    gqk = np.concatenate([f(inp['att_q_norm'])[0], f(inp['att_k_norm'])[0]])[None, :]
    mln = f(inp['ml_norm'])
    lnp = np.stack([f(inp['ln1_g'])[0], f(inp['ln1_b'])[0], f(inp['ln2_g'])[0], f(inp['ln2_b'])[0],
                    f(inp['ln3_g'])[0], f(inp['ln3_b'])[0]])
    sk = f(inp['peer_subkeys'])[0]
    skT = np.ascontiguousarray(sk.transpose(3, 0, 1, 2).reshape(128, 16 * 128))
    u = f(inp['peer_u'])[0]
    v = f(inp['peer_v'])[0]
    uh = np.ascontiguousarray(u.reshape(128, 128, 16, 128).transpose(1, 3, 2, 0)).reshape(128, 128, 2048)
    vh = np.ascontiguousarray(v.reshape(128, 128, 2048).transpose(1, 0, 2))
    cst = _consts()
    common = dict(cst=cst, gqk=f(gqk), mln=mln, w_out=f(inp['w_out'])[0], xa_wq=f(inp['xa_wq'])[0],
                  xa_wk=f(inp['xa_wk'])[0], xa_wv=f(inp['xa_wv'])[0], xa_wo=f(inp['xa_wo'])[0], lnp=f(lnp),
                  pwq=f(inp['peer_wq'])[0], skT=skT, uh=uh, vh=vh)
    rope0 = _rope_table(np.arange(4096))
    rope1 = _rope_table(4095 - np.arange(4096))
    maps = []
    for c in cores:
        b, half = c // 2, c % 2
        m = dict(common)
        if half == 0:
            m['xl'] = np.ascontiguousarray(x[b])
            m['rope'] = rope0
            m['w_in'] = w_in
            m['bg'] = f(bg0)
        else:
            m['xl'] = np.ascontiguousarray(x[b][::-1])
            m['rope'] = rope1
            m['w_in'] = w_in_sw
            m['bg'] = f(bg1)
        m['memb'] = np.ascontiguousarray(mem[b])
        maps.append(m)
    return maps


def assemble(results, cores=range(8)):
    out = np.zeros((4, 4096, 2048), np.float32)
    for r, c in zip(results, cores):
        b, half = c // 2, c % 2
        o = np.asarray(r["out"], dtype=np.float32)
        if half == 0:
            out[b, 0:2048] = o
        else:
            out[b, 2048:4096] = o[::-1]
    return out


_NC_CACHE = {}


def kernel(**inputs):
    if 'nc' not in _NC_CACHE:
        _NC_CACHE['nc'] = build_program('all')
    nc = _NC_CACHE['nc']
    maps = prep_inputs(inputs)
    res = run_bass_kernel_spmd(nc, maps, core_ids=list(range(8)))
    return assemble(res.results)
```
